# Optimizing a Trainium2 kernel written in Bass

```python
import jax, jax.numpy as jnp
from jax import lax
import numpy as np

D_MODEL = 4096
BATCH = 4
SEQ = 2048
DEPTH = 4

N_A_LAYERS = DEPTH // 2
N_B_LAYERS = DEPTH - N_A_LAYERS
D_FF = 6144
ROPE_THETA = 10000.0
EPS = 1e-6
NEG = -1e30
FORCE = 1e6
Q_BLOCK = 128

MLA_HEADS = 32
MLA_Q_LORA = 1024
MLA_KV_LORA = 512
MLA_NOPE = 128
MLA_ROPE = 64
MLA_V = 128
MLA_QK = MLA_NOPE + MLA_ROPE
MLA_IN = MLA_Q_LORA + MLA_KV_LORA + MLA_ROPE

NSA_HEADS = 32
NSA_GROUPS = 4
NSA_HPG = NSA_HEADS // NSA_GROUPS
NSA_DH = 128
N_BRANCH = 3
N_KV_PARTS = 2 * N_BRANCH
CMP_BLOCK = 32
CMP_STRIDE = 16
CMP_HIDDEN = 256
SLC_BLOCK = 64
SLC_TOPK = 16
SLC_Q_BLOCK = 32
WINDOW = 512
NSA_Q_WIDTH = NSA_HEADS * NSA_DH
NSA_IN = NSA_Q_WIDTH + N_BRANCH * NSA_HEADS

kernel_name = "yoco_mla_nsa_macaron_trunk"


def rms_norm(x, g):
    xf = x.astype(jnp.float32)
    y = xf * lax.rsqrt(jnp.mean(xf * xf, axis=-1, keepdims=True) + EPS)
    return (y * g.astype(jnp.float32)).astype(x.dtype)


def rope_cos_sin(positions, dim):
    inv = jnp.power(ROPE_THETA, -jnp.arange(0, dim, 2, dtype=jnp.float32) / dim)
    ang = positions.astype(jnp.float32)[..., None] * inv
    return jnp.cos(ang), jnp.sin(ang)


def apply_rope(x, cos, sin):
    nmid = x.ndim - 3
    shp = cos.shape[:2] + (1,) * nmid + cos.shape[-1:]
    c = cos.reshape(shp)
    s = sin.reshape(shp)
    x1, x2 = jnp.split(x.astype(jnp.float32), 2, axis=-1)
    return jnp.concatenate([x1 * c - x2 * s, x2 * c + x1 * s], axis=-1).astype(x.dtype)


def swiglu(x, wg, wu, wd):
    return (jax.nn.silu(x @ wg) * (x @ wu)) @ wd


def causal_dense_attention(q, k, v):
    B, S, H, Dq = q.shape
    nb = S // Q_BLOCK
    scale = Dq ** -0.5
    qb = q.reshape(B, nb, Q_BLOCK, H, Dq).transpose(1, 0, 2, 3, 4)
    kpos = jnp.arange(S)

    def one(args):
        qi, i = args
        s = jnp.einsum('bqhd,bkhd->bhqk', qi, k).astype(jnp.float32) * scale
        qpos = i * Q_BLOCK + jnp.arange(Q_BLOCK)
        mask = kpos[None, :] <= qpos[:, None]
        p = jax.nn.softmax(jnp.where(mask, s, NEG), axis=-1)
        return jnp.einsum('bhqk,bkhd->bqhd', p.astype(v.dtype), v)

    o = lax.map(one, (qb, jnp.arange(nb)))
    return o.transpose(1, 0, 2, 3, 4).reshape(B, S, H, v.shape[-1])


def mla_mixer(u, cos, sin, w_in, g_cq, g_ckv, w_uq, w_ukv, g_q, g_k, w_o):
    B, S, _ = u.shape
    c = u @ w_in
    cq, ckv, k_rope = jnp.split(c, [MLA_Q_LORA, MLA_Q_LORA + MLA_KV_LORA], axis=-1)
    q = (rms_norm(cq, g_cq) @ w_uq).reshape(B, S, MLA_HEADS, MLA_QK)
    kv = (rms_norm(ckv, g_ckv) @ w_ukv).reshape(B, S, MLA_HEADS, MLA_NOPE + MLA_V)
    k_nope, v = jnp.split(kv, [MLA_NOPE], axis=-1)
    k = jnp.concatenate([k_nope, jnp.broadcast_to(k_rope[:, :, None, :], (B, S, MLA_HEADS, MLA_ROPE))], axis=-1)
    q = rms_norm(q, g_q)
    k = rms_norm(k, g_k)
    q = jnp.concatenate([q[..., :MLA_NOPE], apply_rope(q[..., MLA_NOPE:], cos, sin)], axis=-1)
    k = jnp.concatenate([k[..., :MLA_NOPE], apply_rope(k[..., MLA_NOPE:], cos, sin)], axis=-1)
    o = causal_dense_attention(q, k, v)
    return o.reshape(B, S, MLA_HEADS * MLA_V) @ w_o


def nsa_shared_kv(h, cos, sin, kv_norm, kv_w, cmp_pos_k, cmp_pos_v, cmp_k_w1, cmp_k_b1, cmp_k_w2,
                  cmp_v_w1, cmp_v_b1, cmp_v_w2, g_k_cmp, g_k_slc, g_k_win):
    B, S, _ = h.shape
    y = rms_norm(h, kv_norm)
    kv = (y @ kv_w).reshape(B, S, N_KV_PARTS, NSA_GROUPS, NSA_DH)
    k_cmp_raw, v_cmp_raw, k_slc, v_slc, k_win, v_win = [kv[:, :, i] for i in range(N_KV_PARTS)]
    n_cmp = (S - CMP_BLOCK) // CMP_STRIDE + 1
    idx = np.arange(n_cmp)[:, None] * CMP_STRIDE + np.arange(CMP_BLOCK)[None, :]

    def compress(t, pos, w1, b1, w2):
        blk = t[:, idx] + pos[None, None, :, None, :]
        blk = blk.transpose(0, 1, 3, 2, 4).reshape(B, n_cmp, NSA_GROUPS, CMP_BLOCK * NSA_DH)
        return jax.nn.silu(blk @ w1 + b1) @ w2

    k_cmp = rms_norm(compress(k_cmp_raw, cmp_pos_k, cmp_k_w1, cmp_k_b1, cmp_k_w2), g_k_cmp)
    v_cmp = compress(v_cmp_raw, cmp_pos_v, cmp_v_w1, cmp_v_b1, cmp_v_w2)
    k_slc = apply_rope(rms_norm(k_slc, g_k_slc), cos, sin)
    k_win = apply_rope(rms_norm(k_win, g_k_win), cos, sin)
    return (k_cmp, v_cmp, k_slc, v_slc, k_win, v_win)


def cmp_to_slc_weights(n_cmp, n_slc):
    cs = np.arange(n_cmp)[:, None] * CMP_STRIDE
    ss = np.arange(n_slc)[None, :] * SLC_BLOCK
    ov = np.clip(np.minimum(cs + CMP_BLOCK, ss + SLC_BLOCK) - np.maximum(cs, ss), 0, None)
    return (ov / CMP_STRIDE).astype(np.float32)


def selected_attention(q, k, v, sel):
    B, S, G, HPG, DH = q.shape
    n_slc = S // SLC_BLOCK
    n_sel = sel.shape[-1]
    scale = DH ** -0.5
    kb = k.reshape(B, n_slc, SLC_BLOCK, G, DH).transpose(0, 3, 1, 2, 4)
    vb = v.reshape(B, n_slc, SLC_BLOCK, G, DH).transpose(0, 3, 1, 2, 4)
    nq = S // SLC_Q_BLOCK
    qb = q.reshape(B, nq, SLC_Q_BLOCK, G, HPG, DH).transpose(1, 0, 2, 3, 4, 5)
    selb = sel.reshape(B, G, nq, SLC_Q_BLOCK, n_sel).transpose(2, 0, 1, 3, 4)
    bi = jnp.arange(B)[:, None, None, None]
    gi = jnp.arange(G)[None, :, None, None]

    def one(args):
        qi, si, i = args
        kg = kb[bi, gi, si]
        vg = vb[bi, gi, si]
        s = jnp.einsum('bqghd,bgqnld->bghqnl', qi, kg).astype(jnp.float32) * scale
        qpos = i * SLC_Q_BLOCK + jnp.arange(SLC_Q_BLOCK)
        kpos = si[..., None] * SLC_BLOCK + jnp.arange(SLC_BLOCK)
        mask = (kpos <= qpos[None, None, :, None, None])[:, :, None]
        s = jnp.where(mask, s, NEG).reshape(B, G, HPG, SLC_Q_BLOCK, n_sel * SLC_BLOCK)
        p = jax.nn.softmax(s, axis=-1).reshape(B, G, HPG, SLC_Q_BLOCK, n_sel, SLC_BLOCK)
        return jnp.einsum('bghqnl,bgqnld->bqghd', p.astype(vg.dtype), vg)

    o = lax.map(one, (qb, selb, jnp.arange(nq)))
    return o.transpose(1, 0, 2, 3, 4, 5).reshape(B, S, G, HPG, DH)


def window_attention(q, k, v):
    B, S, G, HPG, DH = q.shape
    nb = S // Q_BLOCK
    span = Q_BLOCK + WINDOW
    scale = DH ** -0.5
    kp = jnp.pad(k, ((0, 0), (WINDOW, 0), (0, 0), (0, 0)))
    vp = jnp.pad(v, ((0, 0), (WINDOW, 0), (0, 0), (0, 0)))
    qb = q.reshape(B, nb, Q_BLOCK, G, HPG, DH).transpose(1, 0, 2, 3, 4, 5)

    def one(args):
        qi, i = args
        start = i * Q_BLOCK
        ki = lax.dynamic_slice_in_dim(kp, start, span, axis=1)
        vi = lax.dynamic_slice_in_dim(vp, start, span, axis=1)
        s = jnp.einsum('bqghd,bkgd->bghqk', qi, ki).astype(jnp.float32) * scale
        qpos = start + jnp.arange(Q_BLOCK)
        kpos = start - WINDOW + jnp.arange(span)
        diff = qpos[:, None] - kpos[None, :]
        mask = (diff >= 0) & (diff < WINDOW) & (kpos[None, :] >= 0)
        p = jax.nn.softmax(jnp.where(mask, s, NEG), axis=-1)
        return jnp.einsum('bghqk,bkgd->bqghd', p.astype(vi.dtype), vi)

    o = lax.map(one, (qb, jnp.arange(nb)))
    return o.transpose(1, 0, 2, 3, 4, 5).reshape(B, S, G, HPG, DH)


def nsa_mixer(u, cos, sin, shared, w_in, b_gate, g_q, w_o):
    k_cmp, v_cmp, k_slc, v_slc, k_win, v_win = shared
    B, S, _ = u.shape
    proj = u @ w_in
    q = proj[..., :NSA_Q_WIDTH].reshape(B, S, NSA_GROUPS, NSA_HPG, NSA_DH)
    gates = jax.nn.sigmoid(proj[..., NSA_Q_WIDTH:] + b_gate).reshape(B, S, NSA_GROUPS, NSA_HPG, N_BRANCH)
    q = rms_norm(q, g_q)
    q_rot = apply_rope(q, cos, sin)
    scale = NSA_DH ** -0.5
    spos = jnp.arange(S)
    n_cmp = k_cmp.shape[1]
    cmp_end = jnp.arange(n_cmp) * CMP_STRIDE + CMP_BLOCK - 1
    cmask = cmp_end[None, :] <= spos[:, None]
    sc = jnp.einsum('bsghd,bngd->bghsn', q, k_cmp).astype(jnp.float32) * scale
    p_cmp = jax.nn.softmax(jnp.where(cmask, sc, NEG), axis=-1) * cmask
    o_cmp = jnp.einsum('bghsn,bngd->bsghd', p_cmp.astype(v_cmp.dtype), v_cmp)
    n_slc = S // SLC_BLOCK
    n_sel = min(SLC_TOPK, n_slc)
    agg = jnp.asarray(cmp_to_slc_weights(n_cmp, n_slc))
    imp = jnp.einsum('bghsn,nj->bgsj', p_cmp, agg)
    jb = jnp.arange(n_slc)
    cur = spos // SLC_BLOCK
    valid = (jb[None, :] * SLC_BLOCK) <= spos[:, None]
    forced = (jb[None, :] == 0) | (jb[None, :] == cur[:, None]) | (jb[None, :] == cur[:, None] - 1)
    imp = jnp.where(forced, FORCE, jnp.where(valid, imp, -1.0))
    _, sel = lax.top_k(imp, n_sel)
    o_slc = selected_attention(q_rot, k_slc, v_slc, sel)
    o_win = window_attention(q_rot, k_win, v_win)
    o = gates[..., 0:1] * o_cmp + gates[..., 1:2] * o_slc + gates[..., 2:3] * o_win
    return o.reshape(B, S, NSA_Q_WIDTH) @ w_o


def setup_inputs(seed: int = 0) -> dict:
    key = jax.random.key(seed)
    ks = iter(jax.random.split(key, 64))

    def w(shape, fan_in):
        return jax.random.normal(next(ks), shape, jnp.float32) * (fan_in ** -0.5)

    def gain(shape):
        return 1.0 + 0.02 * jax.random.normal(next(ks), shape, jnp.float32)

    def small(shape, s=0.02):
        return s * jax.random.normal(next(ks), shape, jnp.float32)

    D, F = D_MODEL, D_FF
    return {
        'x': jax.random.normal(next(ks), (BATCH, SEQ, D), jnp.float32),
        'positions': jnp.broadcast_to(jnp.arange(SEQ, dtype=jnp.int32), (BATCH, SEQ)),
        'ffn1_norm': gain((DEPTH, D)),
        'ffn1_w_gate': w((DEPTH, D, F), D),
        'ffn1_w_up': w((DEPTH, D, F), D),
        'ffn1_w_down': w((DEPTH, F, D), F),
        'mix_norm': gain((DEPTH, D)),
        'ffn2_norm': gain((DEPTH, D)),
        'ffn2_w_gate': w((DEPTH, D, F), D),
        'ffn2_w_up': w((DEPTH, D, F), D),
        'ffn2_w_down': w((DEPTH, F, D), F),
        'mla_w_in': w((N_A_LAYERS, D, MLA_IN), D),
        'mla_g_cq': gain((N_A_LAYERS, MLA_Q_LORA)),
        'mla_g_ckv': gain((N_A_LAYERS, MLA_KV_LORA)),
        'mla_w_uq': w((N_A_LAYERS, MLA_Q_LORA, MLA_HEADS * MLA_QK), MLA_Q_LORA),
        'mla_w_ukv': w((N_A_LAYERS, MLA_KV_LORA, MLA_HEADS * (MLA_NOPE + MLA_V)), MLA_KV_LORA),
        'mla_g_q': gain((N_A_LAYERS, MLA_QK)),
        'mla_g_k': gain((N_A_LAYERS, MLA_QK)),
        'mla_w_o': w((N_A_LAYERS, MLA_HEADS * MLA_V, D), MLA_HEADS * MLA_V),
        'kv_norm': gain((D,)),
        'kv_w': w((D, N_KV_PARTS * NSA_GROUPS * NSA_DH), D),
        'cmp_pos_k': small((CMP_BLOCK, NSA_DH), 0.1),
        'cmp_pos_v': small((CMP_BLOCK, NSA_DH), 0.1),
        'cmp_k_w1': w((CMP_BLOCK * NSA_DH, CMP_HIDDEN), CMP_BLOCK * NSA_DH),
        'cmp_k_b1': small((CMP_HIDDEN,)),
        'cmp_k_w2': w((CMP_HIDDEN, NSA_DH), CMP_HIDDEN),
        'cmp_v_w1': w((CMP_BLOCK * NSA_DH, CMP_HIDDEN), CMP_BLOCK * NSA_DH),
        'cmp_v_b1': small((CMP_HIDDEN,)),
        'cmp_v_w2': w((CMP_HIDDEN, NSA_DH), CMP_HIDDEN),
        'g_k_cmp': gain((NSA_DH,)),
        'g_k_slc': gain((NSA_DH,)),
        'g_k_win': gain((NSA_DH,)),
        'nsa_w_in': w((N_B_LAYERS, D, NSA_IN), D),
        'nsa_b_gate': small((N_B_LAYERS, N_BRANCH * NSA_HEADS)),
        'nsa_g_q': gain((N_B_LAYERS, NSA_DH)),
        'nsa_w_o': w((N_B_LAYERS, NSA_Q_WIDTH, D), NSA_Q_WIDTH),
    }


def reference(x, positions, ffn1_norm, ffn1_w_gate, ffn1_w_up, ffn1_w_down, mix_norm, ffn2_norm,
              ffn2_w_gate, ffn2_w_up, ffn2_w_down, mla_w_in, mla_g_cq, mla_g_ckv, mla_w_uq, mla_w_ukv,
              mla_g_q, mla_g_k, mla_w_o, kv_norm, kv_w, cmp_pos_k, cmp_pos_v, cmp_k_w1, cmp_k_b1,
              cmp_k_w2, cmp_v_w1, cmp_v_b1, cmp_v_w2, g_k_cmp, g_k_slc, g_k_win, nsa_w_in, nsa_b_gate,
              nsa_g_q, nsa_w_o):
    cos_a, sin_a = rope_cos_sin(positions, MLA_ROPE)
    cos_b, sin_b = rope_cos_sin(positions, NSA_DH)
    h = x
    shared = None
    for layer in range(DEPTH):
        h = h + 0.5 * swiglu(rms_norm(h, ffn1_norm[layer]), ffn1_w_gate[layer], ffn1_w_up[layer], ffn1_w_down[layer])
        u = rms_norm(h, mix_norm[layer])
        if layer < N_A_LAYERS:
            a = layer
            h = h + mla_mixer(u, cos_a, sin_a, mla_w_in[a], mla_g_cq[a], mla_g_ckv[a], mla_w_uq[a],
                              mla_w_ukv[a], mla_g_q[a], mla_g_k[a], mla_w_o[a])
        else:
            b = layer - N_A_LAYERS
            h = h + nsa_mixer(u, cos_b, sin_b, shared, nsa_w_in[b], nsa_b_gate[b], nsa_g_q[b], nsa_w_o[b])
        h = h + 0.5 * swiglu(rms_norm(h, ffn2_norm[layer]), ffn2_w_gate[layer], ffn2_w_up[layer], ffn2_w_down[layer])
        if layer == N_A_LAYERS - 1:
            shared = nsa_shared_kv(h, cos_b, sin_b, kv_norm, kv_w, cmp_pos_k, cmp_pos_v, cmp_k_w1, cmp_k_b1,
                                   cmp_k_w2, cmp_v_w1, cmp_v_b1, cmp_v_w2, g_k_cmp, g_k_slc, g_k_win)
    return h
```

```python
import numpy as np
from contextlib import ExitStack
import concourse.bass as bass
import concourse.mybir as mybir
from concourse.bass_utils import run_bass_kernel_spmd

F32 = mybir.dt.float32
BF16 = mybir.dt.bfloat16
I32 = mybir.dt.int32
ALU = mybir.AluOpType
AF = mybir.ActivationFunctionType

D = 4096; S = 2048; FF = 6144; DEPTH = 4; NB = 4
KC = D // 128; FC = FF // 128
EPS = 1e-6
TT = 512; NTT = S // TT
H = 32
MLA_QL = 1024; MLA_KVL = 512; MLA_IN = 1600
NSA_IN = 4192
N_CMP = 127
PI = float(np.pi)


class Res:
    __slots__ = ("w", "r")

    def __init__(self):
        self.w = None
        self.r = {}


class CSem:
    def __init__(self, h):
        self.h = h
        self.count = 0


class KB:
    def __init__(self, nc, n_dma_sems=48):
        self.nc = nc
        self.eng = {"pe": nc.tensor, "act": nc.scalar, "dve": nc.vector, "pool": nc.gpsimd, "sp": nc.sync}
        self.esem = {e: CSem(nc.alloc_semaphore(f"s_{e}")) for e in ("pe", "act", "dve", "pool")}
        self.seen = {e: {} for e in self.eng}
        self.dsems = [CSem(nc.alloc_semaphore(f"s_d{i}")) for i in range(n_dma_sems)]
        self.free_dsems = list(self.dsems)
        self.n_ins = 0

    def get_dsem(self):
        return self.free_dsems.pop()

    def put_dsem(self, s):
        self.free_dsems.append(s)

    def _waits(self, e, reads, writes):
        need = {}
        for r in reads:
            if r is not None and r.w is not None:
                s, v = r.w
                if need.get(s, 0) < v:
                    need[s] = v
        for w in writes:
            if w is None:
                continue
            if w.w is not None:
                s, v = w.w
                if need.get(s, 0) < v:
                    need[s] = v
            for s, v in w.r.items():
                if need.get(s, 0) < v:
                    need[s] = v
        seen = self.seen[e]
        own = self.esem.get(e)
        for s, v in need.items():
            if e == "pe" and s is own:
                continue
            if seen.get(s, 0) < v:
                self.eng[e].wait_ge(s.h, v)
                seen[s] = v
                self.n_ins += 1

    def op(self, e, fn, reads=(), writes=(), inc=True):
        self._waits(e, reads, writes)
        ins = fn(self.eng[e])
        self.n_ins += 1
        s = self.esem[e]
        if inc:
            s.count += 1
            ins.then_inc(s.h, 1)
            v = s.count
        else:
            v = s.count + 1
        for r in reads:
            if r is not None and r.r.get(s, 0) < v:
                r.r[s] = v
        for w in writes:
            if w is not None:
                w.w = (s, v)
                w.r = {}
        return ins

    def dma(self, q, out, in_, sem, reads=(), writes=()):
        self._waits(q, reads, writes)
        ins = self.eng[q].dma_start(out=out, in_=in_)
        self.n_ins += 1
        sem.count += 16
        ins.then_inc(sem.h, 16)
        v = sem.count
        for r in reads:
            if r is not None and r.r.get(sem, 0) < v:
                r.r[sem] = v
        for w in writes:
            if w is not None:
                w.w = (sem, v)
                w.r = {}
        return ins

    def barrier(self):
        sems = list(self.esem.values()) + self.dsems
        for e in self.eng:
            seen = self.seen[e]
            for s in sems:
                if s.count > seen.get(s, 0):
                    self.eng[e].wait_ge(s.h, s.count)
                    seen[s] = s.count
                    self.n_ins += 1


class Ring:
    def __init__(self, kb, tiles, dma=False):
        self.kb = kb
        self.tiles = tiles
        self.res = [Res() for _ in tiles]
        self.sems = [kb.get_dsem() for _ in tiles] if dma else [None] * len(tiles)
        self.i = 0

    def next(self):
        i = self.i
        self.i = (i + 1) % len(self.tiles)
        return self.tiles[i], self.res[i], self.sems[i]

    def release(self):
        for s in self.sems:
            if s is not None:
                self.kb.put_dsem(s)


class Ctx:
    pass


_UID = [0]


def sb(es, nc, name, shape, dt):
    _UID[0] += 1
    return es.enter_context(nc.sbuf_tensor(f"{name}_{_UID[0]}", shape, dt))


def load_consts(cx, es, names):
    kb, nc = cx.kb, cx.nc
    sem = kb.get_dsem()
    for nm in names:
        ap, shape, dt = cx.cdram[nm]
        t = sb(es, nc, "k_" + nm, shape, dt)
        kb.dma("sp", t[tuple(slice(None) for _ in shape)], ap, sem)
        setattr(cx, nm, t)
    kb.barrier()
    kb.put_dsem(sem)


def rms_rstd(cx, es, chunks, dim, N, rstd, rstd_r, sqring):
    kb = cx.kb
    ps, ps_r, _ = cx.psA.next()
    n = len(chunks)
    for i, (ap, r, P) in enumerate(chunks):
        s, s_r, _ = sqring.next()
        kb.op("act", lambda e: e.activation(out=s[0:P, 0:N], in_=ap, func=AF.Square), reads=[r], writes=[s_r])
        kb.op("pe", lambda e: e.matmul(ps[:, 0:N], lhsT=cx.ones[0:P, :], rhs=s[0:P, 0:N], start=(i == 0), stop=(i == n - 1)),
              reads=[s_r], writes=[ps_r])
    kb.op("act", lambda e: e.activation(out=rstd[:, 0:N], in_=ps[:, 0:N], func=AF.Sqrt, scale=1.0 / dim, bias=cx.eps_col[:, 0:1]),
          reads=[ps_r], writes=[rstd_r])
    kb.op("dve", lambda e: e.reciprocal(out=rstd[:, 0:N], in_=rstd[:, 0:N]), reads=[rstd_r], writes=[rstd_r])


def norm_tile(cx, es, hin_v, gcol0, t0, N, yT, yT_r, ycol0, xring, sqring, rstd, rstd_r):
    kb = cx.kb
    xt = []
    for q in range(4):
        x, x_r, x_s = xring.next()
        kb.dma("sp", x[:, :, 0:N], hin_v[:, q * 8:(q + 1) * 8, t0:t0 + N], x_s, writes=[x_r])
        xt.append((x, x_r))
    chunks = [(xt[c // 8][0][:, c % 8, 0:N], xt[c // 8][1], 128) for c in range(KC)]
    rms_rstd(cx, es, chunks, D, N, rstd, rstd_r, sqring)
    for c in range(KC):
        x, x_r = xt[c // 8]
        kb.op("dve", lambda e: e.scalar_tensor_tensor(out=yT[:, c, ycol0:ycol0 + N], in0=x[:, c % 8, 0:N],
              scalar=cx.vecs[:, gcol0 + c:gcol0 + c + 1], in1=rstd[:, 0:N], op0=ALU.mult, op1=ALU.mult),
              reads=[x_r, rstd_r], writes=[yT_r])


def proj_fm(cx, wv, cols, x, x_r, n_kc, tiles, evac, wring):
    kb = cx.kb
    for ci, (c0, M) in enumerate(cols):
        w, w_r, w_s = wring.next()
        kb.dma("pool", w[:, 0:n_kc, 0:M], wv[:, 0:n_kc, c0:c0 + M], w_s, writes=[w_r])
        for ti, (col0, N) in enumerate(tiles):
            ps, ps_r, _ = cx.psA.next()
            for kc in range(n_kc):
                kb.op("pe", lambda e: e.matmul(ps[0:M, 0:N], lhsT=w[:, kc, 0:M], rhs=x[:, kc, col0:col0 + N],
                      start=(kc == 0), stop=(kc == n_kc - 1)), reads=[w_r, x_r], writes=[ps_r], inc=(kc == n_kc - 1))
            evac(ci, ti, ps, ps_r, M, N)


def out_proj_stage(cx, oT_d, w_o, hin, hout):
    kb, nc = cx.kb, cx.nc
    o_v = oT_d.rearrange("(c p) n -> p c n", p=128)
    w_v = w_o.rearrange("(c p) f -> p c f", p=128)
    hin_v = hin.rearrange("(c p) n -> p c n", p=128)
    hout_v = hout.rearrange("(c p) n -> p c n", p=128)
    with ExitStack() as es:
        oT = sb(es, nc, "op_oT", [128, KC, 1024], BF16); oT_r = Res()
        osem = kb.get_dsem()
        wring = Ring(kb, [sb(es, nc, f"op_w{i}", [128, KC, 128], BF16) for i in range(3)], dma=True)
        hxring = Ring(kb, [sb(es, nc, f"op_hx{i}", [128, TT], F32) for i in range(3)], dma=True)
        obring = Ring(kb, [sb(es, nc, f"op_ob{i}", [128, TT], F32) for i in range(3)], dma=True)
        for half in range(2):
            h0 = half * 1024
            for q in range(4):
                kb.dma("sp", oT[:, q * 8:(q + 1) * 8, :], o_v[:, q * 8:(q + 1) * 8, h0:h0 + 1024], osem, writes=[oT_r])

            def evac(ci, ti, ps, ps_r, M, N):
                t0 = h0 + ti * TT
                hx, hx_r, hx_s = hxring.next()
                kb.dma("sp", hx[:, :], hin_v[:, ci, t0:t0 + TT], hx_s, writes=[hx_r])
                o, o_r, o_s = obring.next()
                kb.op("dve", lambda e: e.tensor_tensor(out=o[:, :], in0=ps[:, :], in1=hx[:, :], op=ALU.add),
                      reads=[ps_r, hx_r], writes=[o_r])
                kb.dma("sp", hout_v[:, ci, t0:t0 + TT], o[:, :], o_s, reads=[o_r])

            proj_fm(cx, w_v, [(c * 128, 128) for c in range(KC)], oT, oT_r, KC, [(0, TT), (TT, TT)], evac, wring)
        kb.barrier()
        wring.release(); hxring.release(); obring.release(); kb.put_dsem(osem)


def ffn_half(cx, hin, hout, gcol0, wg, wu, wd, tok0):
    kb, nc = cx.kb, cx.nc
    NT = 1024
    hin_v = hin.rearrange("(c p) n -> p c n", p=128)
    hout_v = hout.rearrange("(c p) n -> p c n", p=128)
    wg_v = wg.rearrange("(c p) f -> p c f", p=128)
    wu_v = wu.rearrange("(c p) f -> p c f", p=128)
    wd_v = wd.rearrange("(c p) f -> p c f", p=128)
    with ExitStack() as es:
        yT = sb(es, nc, "ff_yT", [128, KC, NT], BF16); yT_r = Res()
        with ExitStack() as es0:
            xring = Ring(kb, [sb(es0, nc, f"ff_x{i}", [128, 8, TT], F32) for i in range(4)], dma=True)
            sqring = Ring(kb, [sb(es0, nc, f"ff_sq{i}", [128, TT], BF16) for i in range(3)])
            rstd = sb(es0, nc, "ff_rstd", [128, TT], F32); rstd_r = Res()
            for t in range(NT // TT):
                norm_tile(cx, es0, hin_v, gcol0, tok0 + t * TT, TT, yT, yT_r, t * TT, xring, sqring, rstd, rstd_r)
            kb.barrier()
            xring.release()
        with ExitStack() as es1:
            actT = sb(es1, nc, "ff_actT", [128, FC, NT], BF16); actT_r = Res()
            with ExitStack() as es1a:
                wring = Ring(kb, [sb(es1a, nc, f"ff_w{i}", [128, 16, 128], BF16) for i in range(8)], dma=True)
                sgring = Ring(kb, [sb(es1a, nc, f"ff_sg{i}", [128, TT], F32) for i in range(2)])
                for fc in range(FC):
                    fsl = slice(fc * 128, (fc + 1) * 128)
                    units = {}
                    for nm, wv in (("g", wg_v), ("u", wu_v)):
                        for kh in range(2):
                            w, w_r, w_s = wring.next()
                            kb.dma("pool", w[:, :, :], wv[:, kh * 16:(kh + 1) * 16, fsl], w_s, writes=[w_r])
                            units[(nm, kh)] = (w, w_r)
                    for t in range(NT // TT):
                        tsl = slice(t * TT, (t + 1) * TT)
                        pg, pg_r, _ = cx.psA.next()
                        pu, pu_r, _ = cx.psA.next()
                        for nm, p, p_r in (("g", pg, pg_r), ("u", pu, pu_r)):
                            for c in range(KC):
                                w, w_r = units[(nm, c // 16)]
                                kb.op("pe", lambda e: e.matmul(p[:, :], lhsT=w[:, c % 16, :], rhs=yT[:, c, tsl],
                                      start=(c == 0), stop=(c == KC - 1)), reads=[w_r, yT_r], writes=[p_r], inc=(c == KC - 1))
                        s, s_r, _ = sgring.next()
                        kb.op("act", lambda e: e.activation(out=s[:, :], in_=pg[:, :], func=AF.Silu), reads=[pg_r], writes=[s_r])
                        kb.op("dve", lambda e: e.tensor_tensor(out=actT[:, fc, tsl], in0=s[:, :], in1=pu[:, :], op=ALU.mult),
                              reads=[s_r, pu_r], writes=[actT_r])
                kb.barrier()
                wring.release()
            with ExitStack() as es2:
                wdring = Ring(kb, [sb(es2, nc, f"ff_wd{i}", [128, 16, 128], BF16) for i in range(6)], dma=True)
                hxring = Ring(kb, [sb(es2, nc, f"ff_hx{i}", [128, TT], F32) for i in range(3)], dma=True)
                obring = Ring(kb, [sb(es2, nc, f"ff_ob{i}", [128, TT], F32) for i in range(3)], dma=True)
                for dc in range(KC):
                    dsl = slice(dc * 128, (dc + 1) * 128)
                    units = []
                    for kh in range(3):
                        w, w_r, w_s = wdring.next()
                        kb.dma("pool", w[:, :, :], wd_v[:, kh * 16:(kh + 1) * 16, dsl], w_s, writes=[w_r])
                        units.append((w, w_r))
                    for t in range(NT // TT):
                        tsl = slice(t * TT, (t + 1) * TT)
                        gsl = slice(tok0 + t * TT, tok0 + (t + 1) * TT)
                        hx, hx_r, hx_s = hxring.next()
                        kb.dma("sp", hx[:, :], hin_v[:, dc, gsl], hx_s, writes=[hx_r])
                        po, po_r, _ = cx.psA.next()
                        for c in range(FC):
                            w, w_r = units[c // 16]
                            kb.op("pe", lambda e: e.matmul(po[:, :], lhsT=w[:, c % 16, :], rhs=actT[:, c, tsl],
                                  start=(c == 0), stop=(c == FC - 1)), reads=[w_r, actT_r], writes=[po_r], inc=(c == FC - 1))
                        o, o_r, o_s = obring.next()
                        kb.op("dve", lambda e: e.scalar_tensor_tensor(out=o[:, :], in0=po[:, :], scalar=0.5, in1=hx[:, :],
                              op0=ALU.mult, op1=ALU.add), reads=[po_r, hx_r], writes=[o_r])
                        kb.dma("sp", hout_v[:, dc, gsl], o[:, :], o_s, reads=[o_r])
                kb.barrier()
                wdring.release(); hxring.release(); obring.release()


def ffn_stage(cx, hin, hout, gcol0, wg, wu, wd):
    for half in range(2):
        ffn_half(cx, hin, hout, gcol0, wg, wu, wd, half * 1024)


def rope_apply(cx, xb, xb_r, P, N, tcol0, perm, cosT, sinT, out_ap, out_r, tmpring):
    kb = cx.kb
    ps, ps_r, _ = cx.psA.next()
    kb.op("pe", lambda e: e.matmul(ps[0:P, 0:N], lhsT=perm[0:P, 0:P], rhs=xb, start=True, stop=True), reads=[xb_r], writes=[ps_r])
    t1, t1_r, _ = tmpring.next()
    kb.op("dve", lambda e: e.tensor_tensor(out=t1[0:P, 0:N], in0=ps[0:P, 0:N], in1=sinT[0:P, tcol0:tcol0 + N], op=ALU.mult),
          reads=[ps_r], writes=[t1_r])
    t2, t2_r, _ = tmpring.next()
    kb.op("pool", lambda e: e.tensor_tensor(out=t2[0:P, 0:N], in0=xb, in1=cosT[0:P, tcol0:tcol0 + N], op=ALU.mult),
          reads=[xb_r], writes=[t2_r])
    kb.op("dve", lambda e: e.tensor_tensor(out=out_ap, in0=t1[0:P, 0:N], in1=t2[0:P, 0:N], op=ALU.add),
          reads=[t1_r, t2_r], writes=[out_r])


def build_rope_tables(cx, pos_d, invcol, sgncol, P, cos_d, sin_d):
    kb, nc = cx.kb, cx.nc
    r = Res()
    with ExitStack() as es2:
        cosT = sb(es2, nc, "rt_cos", [128, S], F32)
        sinT = sb(es2, nc, "rt_sin", [128, S], F32)
        pi_ = sb(es2, nc, "rt_pi", [128, S], I32)
        ang = sb(es2, nc, "rt_ang", [128, S], F32)
        tmp = sb(es2, nc, "rt_tmp", [128, S], F32)
        sem = kb.get_dsem()
        src = bass.AP(pos_d.tensor, 0, [[0, 128], [1, S]])
        kb.dma("sp", pi_[:, :], src, sem, writes=[r])
        kb.op("dve", lambda e: e.memset(cosT[:, :], 0.0), writes=[r])
        kb.op("dve", lambda e: e.memset(sinT[:, :], 0.0), writes=[r])
        kb.op("dve", lambda e: e.tensor_copy(out=ang[0:P, :], in_=pi_[0:P, :]), reads=[r], writes=[r])
        kb.op("dve", lambda e: e.tensor_scalar(out=ang[0:P, :], in0=ang[0:P, :], scalar1=invcol, scalar2=None, op0=ALU.mult), reads=[r], writes=[r])
        ki = sb(es2, nc, "rt_ki", [128, S], I32)
        msk = sb(es2, nc, "rt_m", [128, S], F32)

        def sin_of(out_t, shift):
            kb.op("dve", lambda e: e.tensor_scalar(out=tmp[0:P, :], in0=ang[0:P, :], scalar1=shift, scalar2=None, op0=ALU.add), reads=[r], writes=[r])
            kb.op("dve", lambda e: e.tensor_scalar(out=msk[0:P, :], in0=tmp[0:P, :], scalar1=1.0 / (2 * PI), scalar2=None, op0=ALU.mult), reads=[r], writes=[r])
            kb.op("dve", lambda e: e.tensor_copy(out=ki[0:P, :], in_=msk[0:P, :]), reads=[r], writes=[r])
            kb.op("dve", lambda e: e.tensor_copy(out=msk[0:P, :], in_=ki[0:P, :]), reads=[r], writes=[r])
            kb.op("dve", lambda e: e.scalar_tensor_tensor(out=tmp[0:P, :], in0=msk[0:P, :], scalar=-2 * PI, in1=tmp[0:P, :], op0=ALU.mult, op1=ALU.add), reads=[r], writes=[r])
            kb.op("dve", lambda e: e.tensor_scalar(out=msk[0:P, :], in0=tmp[0:P, :], scalar1=PI, scalar2=None, op0=ALU.is_gt), reads=[r], writes=[r])
            kb.op("dve", lambda e: e.scalar_tensor_tensor(out=tmp[0:P, :], in0=msk[0:P, :], scalar=-2 * PI, in1=tmp[0:P, :], op0=ALU.mult, op1=ALU.add), reads=[r], writes=[r])
            kb.op("dve", lambda e: e.tensor_scalar(out=msk[0:P, :], in0=tmp[0:P, :], scalar1=-PI, scalar2=None, op0=ALU.is_lt), reads=[r], writes=[r])
            kb.op("dve", lambda e: e.scalar_tensor_tensor(out=tmp[0:P, :], in0=msk[0:P, :], scalar=2 * PI, in1=tmp[0:P, :], op0=ALU.mult, op1=ALU.add), reads=[r], writes=[r])
            kb.op("dve", lambda e: e.tensor_scalar(out=tmp[0:P, :], in0=tmp[0:P, :], scalar1=-PI, scalar2=PI, op0=ALU.max, op1=ALU.min), reads=[r], writes=[r])
            kb.op("act", lambda e: e.activation(out=out_t[0:P, :], in_=tmp[0:P, :], func=AF.Sin), reads=[r], writes=[r])

        sin_of(sinT, 0.0)
        kb.op("dve", lambda e: e.tensor_scalar(out=sinT[0:P, :], in0=sinT[0:P, :], scalar1=sgncol, scalar2=None, op0=ALU.mult), reads=[r], writes=[r])
        sin_of(cosT, PI / 2)
        kb.dma("sp", cos_d, cosT[:, :], sem, reads=[r])
        kb.dma("sp", sin_d, sinT[:, :], sem, reads=[r])
        kb.barrier()
        kb.put_dsem(sem)


def attn_tile(cx, qparts, kparts, scale, v_ap, v_r, ones_ap, mask_ap, mask_r, o_ps, o_r, d_ps, d_r, first, last, pring, N=TT, KP=128):
    kb = cx.kb
    ps, ps_r, _ = cx.psA.next()
    n = len(qparts)
    for i in range(n):
        q_ap, q_r = qparts[i]
        k_ap, k_r = kparts[i]
        kb.op("pe", lambda e: e.matmul(ps[0:KP, 0:N], lhsT=k_ap, rhs=q_ap, start=(i == 0), stop=(i == n - 1)),
              reads=[q_r, k_r], writes=[ps_r], inc=(i == n - 1))
    p, p_r, _ = pring.next()
    kb.op("act", lambda e: e.activation(out=p[0:KP, 0:N], in_=ps[0:KP, 0:N], func=AF.Exp, scale=scale), reads=[ps_r], writes=[p_r])
    if mask_ap is not None:
        kb.op("dve", lambda e: e.tensor_tensor(out=p[0:KP, 0:N], in0=p[0:KP, 0:N], in1=mask_ap, op=ALU.mult), reads=[p_r, mask_r], writes=[p_r])
    kb.op("pe", lambda e: e.matmul(o_ps[:, 0:N], lhsT=v_ap, rhs=p[0:KP, 0:N], start=first, stop=last), reads=[p_r, v_r], writes=[o_r])
    kb.op("pe", lambda e: e.matmul(d_ps[:, 0:N], lhsT=ones_ap, rhs=p[0:KP, 0:N], start=first, stop=last), reads=[p_r], writes=[d_r])


def mla_stage(cx, L, a, hin, hout, W, scr):
    kb, nc = cx.kb, cx.nc
    V = cx.vcol
    hin_v = hin.rearrange("(c p) n -> p c n", p=128)
    w_in_v = W["mla_w_in"][a].rearrange("(c p) f -> p c f", p=128)
    w_uq_v = W["mla_w_uq"][a].rearrange("(c p) f -> p c f", p=128)
    w_ukv_v = W["mla_w_ukv"][a].rearrange("(c p) f -> p c f", p=128)
    cqn_d, ckvn_d, kr_d, oT_d = scr["cqn"], scr["ckvn"], scr["kr"], scr["oT"]
    cqn_dv = cqn_d.rearrange("(c p) n -> p c n", p=128)
    ckvn_dv = ckvn_d.rearrange("(c p) n -> p c n", p=128)
    oT_dv = oT_d.rearrange("(c p) n -> p c n", p=128)
    with ExitStack() as es:
        uT = sb(es, nc, "m1_uT", [128, KC, TT], BF16); uT_r = Res()
        xring = Ring(kb, [sb(es, nc, f"m1_x{i}", [128, 8, TT], F32) for i in range(4)], dma=True)
        sqring = Ring(kb, [sb(es, nc, f"m1_sq{i}", [128, TT], BF16) for i in range(3)])
        rstd = sb(es, nc, "m1_rstd", [128, TT], F32); rstd_r = Res()
        wring = Ring(kb, [sb(es, nc, f"m1_w{i}", [128, KC, 128], BF16) for i in range(3)], dma=True)
        cbuf = sb(es, nc, "m1_c", [128, 13, TT], F32); c_rs = [Res() for _ in range(13)]
        cn = sb(es, nc, "m1_cn", [128, 12, TT], BF16); cn_r = Res()
        cnsem = kb.get_dsem()
        cols = [(c * 128, 128) for c in range(12)] + [(1536, 64)]
        for t in range(NTT):
            t0 = t * TT
            norm_tile(cx, es, hin_v, V["mix_norm"][L], t0, TT, uT, uT_r, 0, xring, sqring, rstd, rstd_r)

            def evac(ci, ti, ps, ps_r, M, N):
                kb.op("act", lambda e: e.activation(out=cbuf[0:M, ci, :], in_=ps[0:M, :], func=AF.Copy), reads=[ps_r], writes=[c_rs[ci]])

            proj_fm(cx, w_in_v, cols, uT, uT_r, KC, [(0, TT)], evac, wring)
            rms_rstd(cx, es, [(cbuf[:, c, :], c_rs[c], 128) for c in range(8)], MLA_QL, TT, rstd, rstd_r, sqring)
            for c in range(8):
                kb.op("dve", lambda e: e.scalar_tensor_tensor(out=cn[:, c, :], in0=cbuf[:, c, :], scalar=cx.vecs[:, V["mla_g_cq"][a] + c:V["mla_g_cq"][a] + c + 1],
                      in1=rstd[:, :], op0=ALU.mult, op1=ALU.mult), reads=[c_rs[c], rstd_r], writes=[cn_r])
            rms_rstd(cx, es, [(cbuf[:, 8 + c, :], c_rs[8 + c], 128) for c in range(4)], MLA_KVL, TT, rstd, rstd_r, sqring)
            for c in range(4):
                kb.op("dve", lambda e: e.scalar_tensor_tensor(out=cn[:, 8 + c, :], in0=cbuf[:, 8 + c, :], scalar=cx.vecs[:, V["mla_g_ckv"][a] + c:V["mla_g_ckv"][a] + c + 1],
                      in1=rstd[:, :], op0=ALU.mult, op1=ALU.mult), reads=[c_rs[8 + c], rstd_r], writes=[cn_r])
            kb.dma("sp", cqn_dv[:, :, t0:t0 + TT], cn[:, 0:8, :], cnsem, reads=[cn_r])
            kb.dma("sp", ckvn_dv[:, :, t0:t0 + TT], cn[:, 8:12, :], cnsem, reads=[cn_r])
            kb.dma("sp", kr_d[:, t0:t0 + TT], cbuf[0:64, 12, :], cnsem, reads=[c_rs[12]])
        kb.barrier()
        xring.release(); wring.release(); kb.put_dsem(cnsem)
    with ExitStack() as es:
        load_consts(cx, es, ["cosA", "sinA", "maskC"])
        cqn = sb(es, nc, "m2_cqn", [128, 8, S], BF16)
        ckvn = sb(es, nc, "m2_ckvn", [128, 4, S], BF16)
        kr = sb(es, nc, "m2_kr", [64, S], F32)
        krg = sb(es, nc, "m2_krg", [64, S], BF16)
        ssq_kr = sb(es, nc, "m2_ssqkr", [128, S], F32)
        ld_r = Res()
        sem = kb.get_dsem()
        kb.dma("sp", cqn[:, :, :], cqn_dv, sem, writes=[ld_r])
        kb.dma("sp", ckvn[:, :, :], ckvn_dv, sem, writes=[ld_r])
        kb.dma("sp", kr[:, :], kr_d, sem, writes=[ld_r])
        sqring = Ring(kb, [sb(es, nc, f"m2_sq{i}", [128, TT], BF16) for i in range(3)])
        tmpring = Ring(kb, [sb(es, nc, f"m2_tmp{i}", [128, TT], F32) for i in range(4)])
        tbring = Ring(kb, [sb(es, nc, f"m2_tb{i}", [128, TT], BF16) for i in range(3)])
        gk_r = cx.vecs[0:64, V["mla_g_k_r"][a]:V["mla_g_k_r"][a] + 1]
        gk_n = cx.vecs[:, V["mla_g_k_n"][a]:V["mla_g_k_n"][a] + 1]
        gq_r = cx.vecs[0:64, V["mla_g_q_r"][a]:V["mla_g_q_r"][a] + 1]
        gq_n = cx.vecs[:, V["mla_g_q_n"][a]:V["mla_g_q_n"][a] + 1]
        for t in range(NTT):
            tsl = slice(t * TT, (t + 1) * TT)
            s, s_r, _ = sqring.next()
            kb.op("act", lambda e: e.activation(out=s[0:64, :], in_=kr[:, tsl], func=AF.Square), reads=[ld_r], writes=[s_r])
            ps, ps_r, _ = cx.psA.next()
            kb.op("pe", lambda e: e.matmul(ps[:, :], lhsT=cx.ones[0:64, :], rhs=s[0:64, :], start=True, stop=True), reads=[s_r], writes=[ps_r])
            kb.op("act", lambda e: e.activation(out=ssq_kr[:, tsl], in_=ps[:, :], func=AF.Copy), reads=[ps_r], writes=[ld_r])
            tb, tb_r, _ = tbring.next()
            kb.op("dve", lambda e: e.tensor_scalar(out=tb[0:64, :], in0=kr[:, tsl], scalar1=gk_r, scalar2=None, op0=ALU.mult), reads=[ld_r], writes=[tb_r])
            rope_apply(cx, tb[0:64, :], tb_r, 64, TT, t * TT, cx.perm64, cx.cosA, cx.sinA, krg[:, tsl], ld_r, tmpring)
        kb.barrier()
        kT = sb(es, nc, "m2_kT", [128, S], BF16); kT_r = Res()
        krT = sb(es, nc, "m2_krT", [64, S], BF16); krT_r = Res()
        vtm = sb(es, nc, "m2_v", [128, 16, 128], BF16); v_r = Res()
        qT = sb(es, nc, "m2_qT", [128, S], BF16); qT_r = Res()
        qrT = sb(es, nc, "m2_qrT", [64, S], BF16); qrT_r = Res()
        oT = sb(es, nc, "m2_oT", [128, S], BF16); oT_r = Res()
        rstd = sb(es, nc, "m2_rstd", [128, TT], F32); rstd_r = Res()
        rden = sb(es, nc, "m2_rden", [128, TT], F32); rden_r = Res()
        raw = sb(es, nc, "m2_raw", [128, 2, TT], F32); raw_rs = [Res(), Res()]
        wk = Ring(kb, [sb(es, nc, f"m2_wk{i}", [128, 4, 256], BF16) for i in range(2)], dma=True)
        wq = Ring(kb, [sb(es, nc, f"m2_wq{i}", [128, 8, 192], BF16) for i in range(2)], dma=True)
        pring = Ring(kb, [sb(es, nc, f"m2_p{i}", [128, TT], BF16) for i in range(3)])
        osem = kb.get_dsem()
        scale = 192.0 ** -0.5
        for h in range(H):
            wkv, wkv_r, wkv_s = wk.next()
            kb.dma("pool", wkv[:, :, :], w_ukv_v[:, :, h * 256:(h + 1) * 256], wkv_s, writes=[wkv_r])
            wqh, wqh_r, wqh_s = wq.next()
            kb.dma("pool", wqh[:, :, :], w_uq_v[:, :, h * 192:(h + 1) * 192], wqh_s, writes=[wqh_r])
            for t in range(NTT):
                tsl = slice(t * TT, (t + 1) * TT)
                ps, ps_r, _ = cx.psA.next()
                for kc in range(4):
                    kb.op("pe", lambda e: e.matmul(ps[:, :], lhsT=wkv[:, kc, 0:128], rhs=ckvn[:, kc, tsl], start=(kc == 0), stop=(kc == 3)),
                          reads=[wkv_r], writes=[ps_r], inc=(kc == 3))
                kb.op("act", lambda e: e.activation(out=raw[:, 0, :], in_=ps[:, :], func=AF.Copy), reads=[ps_r], writes=[raw_rs[0]])
                s, s_r, _ = sqring.next()
                kb.op("act", lambda e: e.activation(out=s[:, :], in_=raw[:, 0, :], func=AF.Square), reads=[raw_rs[0]], writes=[s_r])
                ps2, ps2_r, _ = cx.psA.next()
                kb.op("pe", lambda e: e.matmul(ps2[:, :], lhsT=cx.ones[:, :], rhs=s[:, :], start=True, stop=True), reads=[s_r], writes=[ps2_r])
                kb.op("dve", lambda e: e.tensor_tensor(out=rstd[:, :], in0=ps2[:, :], in1=ssq_kr[:, tsl], op=ALU.add), reads=[ps2_r], writes=[rstd_r])
                kb.op("act", lambda e: e.activation(out=rstd[:, :], in_=rstd[:, :], func=AF.Sqrt, scale=1.0 / 192, bias=cx.eps_col[:, 0:1]), reads=[rstd_r], writes=[rstd_r])
                kb.op("dve", lambda e: e.reciprocal(out=rstd[:, :], in_=rstd[:, :]), reads=[rstd_r], writes=[rstd_r])
                kb.op("dve", lambda e: e.scalar_tensor_tensor(out=kT[:, tsl], in0=raw[:, 0, :], scalar=gk_n, in1=rstd[:, :], op0=ALU.mult, op1=ALU.mult),
                      reads=[raw_rs[0], rstd_r], writes=[kT_r])
                kb.op("pool", lambda e: e.tensor_tensor(out=krT[:, tsl], in0=krg[:, tsl], in1=rstd[0:64, :], op=ALU.mult), reads=[rstd_r], writes=[krT_r])
                for tt in range(4):
                    k0 = t * TT + tt * 128
                    psv, psv_r, _ = cx.psA.next()
                    for kc in range(4):
                        kb.op("pe", lambda e: e.matmul(psv[:, 0:128], lhsT=ckvn[:, kc, k0:k0 + 128], rhs=wkv[:, kc, 128:256], start=(kc == 0), stop=(kc == 3)),
                              reads=[wkv_r], writes=[psv_r], inc=(kc == 3))
                    kb.op("act", lambda e: e.activation(out=vtm[:, t * 4 + tt, :], in_=psv[:, 0:128], func=AF.Copy), reads=[psv_r], writes=[v_r])
                psq, psq_r, _ = cx.psA.next()
                for kc in range(8):
                    kb.op("pe", lambda e: e.matmul(psq[:, :], lhsT=wqh[:, kc, 0:128], rhs=cqn[:, kc, tsl], start=(kc == 0), stop=(kc == 7)),
                          reads=[wqh_r], writes=[psq_r], inc=(kc == 7))
                psr_, psr_r, _ = cx.psA.next()
                for kc in range(8):
                    kb.op("pe", lambda e: e.matmul(psr_[0:64, :], lhsT=wqh[:, kc, 128:192], rhs=cqn[:, kc, tsl], start=(kc == 0), stop=(kc == 7)),
                          reads=[wqh_r], writes=[psr_r], inc=(kc == 7))
                kb.op("act", lambda e: e.activation(out=raw[:, 0, :], in_=psq[:, :], func=AF.Copy), reads=[psq_r], writes=[raw_rs[0]])
                kb.op("act", lambda e: e.activation(out=raw[0:64, 1, :], in_=psr_[0:64, :], func=AF.Copy), reads=[psr_r], writes=[raw_rs[1]])
                rms_rstd(cx, es, [(raw[:, 0, :], raw_rs[0], 128), (raw[0:64, 1, :], raw_rs[1], 64)], 192, TT, rstd, rstd_r, sqring)
                kb.op("dve", lambda e: e.scalar_tensor_tensor(out=qT[:, tsl], in0=raw[:, 0, :], scalar=gq_n, in1=rstd[:, :], op0=ALU.mult, op1=ALU.mult),
                      reads=[raw_rs[0], rstd_r], writes=[qT_r])
                tb, tb_r, _ = tbring.next()
                kb.op("dve", lambda e: e.scalar_tensor_tensor(out=tb[0:64, :], in0=raw[0:64, 1, :], scalar=gq_r, in1=rstd[0:64, :], op0=ALU.mult, op1=ALU.mult),
                      reads=[raw_rs[1], rstd_r], writes=[tb_r])
                rope_apply(cx, tb[0:64, :], tb_r, 64, TT, t * TT, cx.perm64, cx.cosA, cx.sinA, qrT[:, tsl], qrT_r, tmpring)
            for t in range(NTT):
                tsl = slice(t * TT, (t + 1) * TT)
                o_ps, o_r, _ = cx.psB.next()
                d_ps, d_r, _ = cx.psB.next()
                nk = 4 * t + 4
                for kt in range(nk):
                    ksl = slice(kt * 128, (kt + 1) * 128)
                    diag = kt >= 4 * t
                    attn_tile(cx, [(qT[:, tsl], qT_r), (qrT[:, tsl], qrT_r)], [(kT[:, ksl], kT_r), (krT[:, ksl], krT_r)], scale,
                              vtm[:, kt, :], v_r, cx.ones[:, :], cx.maskC[:, kt - 4 * t, :] if diag else None, None,
                              o_ps, o_r, d_ps, d_r, kt == 0, kt == nk - 1, pring)
                kb.op("dve", lambda e: e.reciprocal(out=rden[:, :], in_=d_ps[:, :]), reads=[d_r], writes=[rden_r])
                kb.op("dve", lambda e: e.tensor_tensor(out=oT[:, tsl], in0=o_ps[:, :], in1=rden[:, :], op=ALU.mult), reads=[o_r, rden_r], writes=[oT_r])
            kb.dma("sp", oT_dv[:, h, :], oT[:, :], osem, reads=[oT_r])
        kb.barrier()
        wk.release(); wq.release(); kb.put_dsem(osem); kb.put_dsem(sem)
    out_proj_stage(cx, oT_d, W["mla_w_o"][a], hin, hout)


def shared_kv_stage(cx, hin, W, scr):
    kb, nc = cx.kb, cx.nc
    V = cx.vcol
    hin_v = hin.rearrange("(c p) n -> p c n", p=128)
    kvw_v = W["kv_w"].rearrange("(c p) f -> p c f", p=128)
    craw_d = scr["craw"]
    kT_d = scr["nkT"]
    v_d = scr["nv"]
    with ExitStack() as es:
        load_consts(cx, es, ["cosB", "sinB"])
        uT = sb(es, nc, "kv_uT", [128, KC, TT], BF16); uT_r = Res()
        xring = Ring(kb, [sb(es, nc, f"kv_x{i}", [128, 8, TT], F32) for i in range(4)], dma=True)
        sqring = Ring(kb, [sb(es, nc, f"kv_sq{i}", [128, TT], BF16) for i in range(3)])
        rstd = sb(es, nc, "kv_rstd", [128, TT], F32); rstd_r = Res()
        wring = Ring(kb, [sb(es, nc, f"kv_w{i}", [128, KC, 128], BF16) for i in range(3)], dma=True)
        wvring = Ring(kb, [sb(es, nc, f"kv_wv{i}", [128, KC, 512], BF16) for i in range(1)], dma=True)
        raw = sb(es, nc, "kv_raw", [128, TT], F32); raw_r = Res()
        tmpring = Ring(kb, [sb(es, nc, f"kv_tmp{i}", [128, TT], F32) for i in range(4)])
        tbring = Ring(kb, [sb(es, nc, f"kv_tb{i}", [128, TT], BF16) for i in range(3)])
        obring = Ring(kb, [sb(es, nc, f"kv_ob{i}", [128, TT], BF16) for i in range(3)], dma=True)
        vbring = Ring(kb, [sb(es, nc, f"kv_vb{i}", [128, 512], BF16) for i in range(3)], dma=True)
        for t in range(NTT):
            t0 = t * TT
            norm_tile(cx, es, hin_v, V["kv_norm"], t0, TT, uT, uT_r, 0, xring, sqring, rstd, rstd_r)

            def evac(ci, ti, ps, ps_r, M, N):
                part, g = fm_parts[ci]
                if part in (0, 1):
                    o, o_r, o_s = obring.next()
                    kb.op("act", lambda e: e.activation(out=o[:, :], in_=ps[:, :], func=AF.Copy), reads=[ps_r], writes=[o_r])
                    kb.dma("sp", craw_d[part, g, :, t0:t0 + TT], o[:, :], o_s, reads=[o_r])
                else:
                    gcol = V["g_k_slc"] if part == 2 else V["g_k_win"]
                    kb.op("act", lambda e: e.activation(out=raw[:, :], in_=ps[:, :], func=AF.Copy), reads=[ps_r], writes=[raw_r])
                    rms_rstd(cx, es, [(raw[:, :], raw_r, 128)], 128, TT, rstd, rstd_r, sqring)
                    tb, tb_r, _ = tbring.next()
                    kb.op("dve", lambda e: e.scalar_tensor_tensor(out=tb[:, :], in0=raw[:, :], scalar=cx.vecs[:, gcol:gcol + 1], in1=rstd[:, :],
                          op0=ALU.mult, op1=ALU.mult), reads=[raw_r, rstd_r], writes=[tb_r])
                    o, o_r, o_s = obring.next()
                    rope_apply(cx, tb[:, :], tb_r, 128, TT, t0, cx.perm128, cx.cosB, cx.sinB, o[:, :], o_r, tmpring)
                    kb.dma("sp", kT_d[0 if part == 2 else 1, g, :, t0:t0 + TT], o[:, :], o_s, reads=[o_r])

            fm_parts = [(p, g) for p in (0, 1, 2, 4) for g in range(4)]
            proj_fm(cx, kvw_v, [(p * 512 + g * 128, 128) for (p, g) in fm_parts], uT, uT_r, KC, [(0, TT)], evac, wring)
            for pi_, part in enumerate((3, 5)):
                w, w_r, w_s = wvring.next()
                kb.dma("pool", w[:, :, :], kvw_v[:, :, part * 512:(part + 1) * 512], w_s, writes=[w_r])
                for tt in range(4):
                    ps, ps_r, _ = cx.psA.next()
                    for kc in range(KC):
                        kb.op("pe", lambda e: e.matmul(ps[:, :], lhsT=uT[:, kc, tt * 128:(tt + 1) * 128], rhs=w[:, kc, :], start=(kc == 0), stop=(kc == KC - 1)),
                              reads=[w_r, uT_r], writes=[ps_r], inc=(kc == KC - 1))
                    o, o_r, o_s = vbring.next()
                    kb.op("act", lambda e: e.activation(out=o[:, :], in_=ps[:, :], func=AF.Copy), reads=[ps_r], writes=[o_r])
                    for g in range(4):
                        kb.dma("sp", v_d[pi_, g, t0 + tt * 128:t0 + (tt + 1) * 128, :], o[:, g * 128:(g + 1) * 128], o_s, reads=[o_r])
        kb.barrier()
        xring.release(); wring.release(); wvring.release(); obring.release(); vbring.release()


def compress_stage(cx, W, scr):
    kb, nc = cx.kb, cx.nc
    V = cx.vcol
    craw_d = scr["craw"]
    kcmp_d = scr["kcmp"]
    vcmp_d = scr["vcmp"]
    with ExitStack() as es:
        w1 = sb(es, nc, "cp_w1", [128, 32, 256], BF16); w1_r = Res()
        w2 = sb(es, nc, "cp_w2", [128, 2, 128], BF16); w2_r = Res()
        tT = sb(es, nc, "cp_tT", [128, S], BF16); tT_r = Res()
        hid = sb(es, nc, "cp_hid", [128, 2, 128], BF16); hid_r = Res()
        bias = sb(es, nc, "cp_bias", [128, 2], F32); bias_r = Res()
        raw = sb(es, nc, "cp_raw", [128, 128], F32); raw_r = Res()
        rstd = sb(es, nc, "cp_rstd", [128, TT], F32); rstd_r = Res()
        sqring = Ring(kb, [sb(es, nc, f"cp_sq{i}", [128, TT], BF16) for i in range(2)])
        ob = sb(es, nc, "cp_ob", [128, 128], BF16); ob_r = Res()
        s1 = kb.get_dsem(); s2 = kb.get_dsem(); s3 = kb.get_dsem(); s4 = kb.get_dsem()
        for kv in range(2):
            w1_d = W["cmp_k_w1"] if kv == 0 else W["cmp_v_w1"]
            w2_d = W["cmp_k_w2"] if kv == 0 else W["cmp_v_w2"]
            posT = cx.posT_k if kv == 0 else cx.posT_v
            b1c = V["cmp_k_b1"] if kv == 0 else V["cmp_v_b1"]
            kb.dma("pool", w1[:, :, :], w1_d.rearrange("(l p) f -> p l f", p=128), s1, writes=[w1_r])
            kb.dma("pool", w2[:, :, :], w2_d.rearrange("(c p) f -> p c f", p=128), s2, writes=[w2_r])
            for hc in range(2):
                ps, ps_r, _ = cx.psA.next()
                for l in range(32):
                    kb.op("pe", lambda e: e.matmul(ps[:, 0:1], lhsT=w1[:, l, hc * 128:(hc + 1) * 128], rhs=posT[:, l:l + 1], start=(l == 0), stop=(l == 31)),
                          reads=[w1_r], writes=[ps_r], inc=(l == 31))
                kb.op("dve", lambda e: e.tensor_tensor(out=bias[:, hc:hc + 1], in0=ps[:, 0:1], in1=cx.vecs[:, b1c + hc:b1c + hc + 1], op=ALU.add),
                      reads=[ps_r], writes=[bias_r])
            for g in range(4):
                kb.dma("sp", tT[:, :], craw_d[kv, g, :, :], s3, writes=[tT_r])
                for hc in range(2):
                    ps, ps_r, _ = cx.psA.next()
                    for l in range(32):
                        kb.op("pe", lambda e: e.matmul(ps[:, 0:N_CMP], lhsT=w1[:, l, hc * 128:(hc + 1) * 128], rhs=tT[:, l:l + 16 * (N_CMP - 1) + 1:16],
                              start=(l == 0), stop=(l == 31)), reads=[w1_r, tT_r], writes=[ps_r], inc=(l == 31))
                    kb.op("act", lambda e: e.activation(out=hid[:, hc, 0:N_CMP], in_=ps[:, 0:N_CMP], func=AF.Silu, bias=bias[:, hc:hc + 1]),
                          reads=[ps_r, bias_r], writes=[hid_r])
                if kv == 0:
                    ps, ps_r, _ = cx.psA.next()
                    for hc in range(2):
                        kb.op("pe", lambda e: e.matmul(ps[:, 0:N_CMP], lhsT=w2[:, hc, :], rhs=hid[:, hc, 0:N_CMP], start=(hc == 0), stop=(hc == 1)),
                              reads=[w2_r, hid_r], writes=[ps_r])
                    kb.op("act", lambda e: e.activation(out=raw[:, 0:N_CMP], in_=ps[:, 0:N_CMP], func=AF.Copy), reads=[ps_r], writes=[raw_r])
                    rms_rstd(cx, es, [(raw[:, 0:N_CMP], raw_r, 128)], 128, N_CMP, rstd, rstd_r, sqring)
                    kb.op("dve", lambda e: e.scalar_tensor_tensor(out=ob[:, 0:N_CMP], in0=raw[:, 0:N_CMP], scalar=cx.vecs[:, V["g_k_cmp"]:V["g_k_cmp"] + 1],
                          in1=rstd[:, 0:N_CMP], op0=ALU.mult, op1=ALU.mult), reads=[raw_r, rstd_r], writes=[ob_r])
                    kb.dma("sp", kcmp_d[g, :, 0:N_CMP], ob[:, 0:N_CMP], s4, reads=[ob_r])
                else:
                    ps, ps_r, _ = cx.psA.next()
                    for hc in range(2):
                        kb.op("pe", lambda e: e.matmul(ps[0:N_CMP, 0:128], lhsT=hid[:, hc, 0:N_CMP], rhs=w2[:, hc, :], start=(hc == 0), stop=(hc == 1)),
                              reads=[w2_r, hid_r], writes=[ps_r])
                    kb.op("act", lambda e: e.activation(out=ob[0:N_CMP, :], in_=ps[0:N_CMP, 0:128], func=AF.Copy), reads=[ps_r], writes=[ob_r])
                    kb.dma("sp", vcmp_d[g, 0:N_CMP, :], ob[0:N_CMP, :], s4, reads=[ob_r])
        kb.barrier()
        for s in (s1, s2, s3, s4):
            kb.put_dsem(s)


def nsa_stage(cx, L, b, hin, hout, W, scr):
    kb, nc = cx.kb, cx.nc
    V = cx.vcol
    hin_v = hin.rearrange("(c p) n -> p c n", p=128)
    w_in_v = W["nsa_w_in"][b].rearrange("(c p) f -> p c f", p=128)
    qn_d, qr_d, gates_d, oT_d, ocmp_d = scr["qn"], scr["qr"], scr["gates"], scr["oT"], scr["ocmp"]
    oT_dv = oT_d.rearrange("(c p) n -> p c n", p=128)
    gq = cx.vecs[:, V["nsa_g_q"][b]:V["nsa_g_q"][b] + 1]
    scale = 128.0 ** -0.5
    with ExitStack() as es:
        load_consts(cx, es, ["cosB", "sinB"])
        uT = sb(es, nc, "n1_uT", [128, KC, TT], BF16); uT_r = Res()
        xring = Ring(kb, [sb(es, nc, f"n1_x{i}", [128, 8, TT], F32) for i in range(4)], dma=True)
        sqring = Ring(kb, [sb(es, nc, f"n1_sq{i}", [128, TT], BF16) for i in range(3)])
        rstd = sb(es, nc, "n1_rstd", [128, TT], F32); rstd_r = Res()
        wring = Ring(kb, [sb(es, nc, f"n1_w{i}", [128, KC, 128], BF16) for i in range(3)], dma=True)
        raw = sb(es, nc, "n1_raw", [128, TT], F32); raw_r = Res()
        tmpring = Ring(kb, [sb(es, nc, f"n1_tmp{i}", [128, TT], F32) for i in range(4)])
        qnring = Ring(kb, [sb(es, nc, f"n1_qn{i}", [128, TT], BF16) for i in range(3)], dma=True)
        qrring = Ring(kb, [sb(es, nc, f"n1_qr{i}", [128, TT], BF16) for i in range(3)], dma=True)
        gring = Ring(kb, [sb(es, nc, f"n1_g{i}", [128, TT], F32) for i in range(2)], dma=True)
        cols = [(c * 128, 128) for c in range(32)] + [(4096, 96)]
        for t in range(NTT):
            t0 = t * TT
            norm_tile(cx, es, hin_v, V["mix_norm"][L], t0, TT, uT, uT_r, 0, xring, sqring, rstd, rstd_r)

            def evac(ci, ti, ps, ps_r, M, N):
                if ci == 32:
                    o, o_r, o_s = gring.next()
                    kb.op("act", lambda e: e.activation(out=o[0:96, :], in_=ps[0:96, :], func=AF.Sigmoid, bias=cx.vecs[0:96, V["nsa_b_gate"][b]:V["nsa_b_gate"][b] + 1]),
                          reads=[ps_r], writes=[o_r])
                    kb.dma("sp", gates_d[:, t0:t0 + TT], o[0:96, :], o_s, reads=[o_r])
                    return
                kb.op("act", lambda e: e.activation(out=raw[:, :], in_=ps[:, :], func=AF.Copy), reads=[ps_r], writes=[raw_r])
                rms_rstd(cx, es, [(raw[:, :], raw_r, 128)], 128, TT, rstd, rstd_r, sqring)
                qn, qn_r, qn_s = qnring.next()
                kb.op("dve", lambda e: e.scalar_tensor_tensor(out=qn[:, :], in0=raw[:, :], scalar=gq, in1=rstd[:, :], op0=ALU.mult, op1=ALU.mult),
                      reads=[raw_r, rstd_r], writes=[qn_r])
                kb.dma("sp", qn_d[ci, :, t0:t0 + TT], qn[:, :], qn_s, reads=[qn_r])
                qr, qr_r, qr_s = qrring.next()
                rope_apply(cx, qn[:, :], qn_r, 128, TT, t0, cx.perm128, cx.cosB, cx.sinB, qr[:, :], qr_r, tmpring)
                kb.dma("sp", qr_d[ci, :, t0:t0 + TT], qr[:, :], qr_s, reads=[qr_r])

            proj_fm(cx, w_in_v, cols, uT, uT_r, KC, [(0, TT)], evac, wring)
        kb.barrier()
        xring.release(); wring.release(); qnring.release(); qrring.release(); gring.release()
    with ExitStack() as es:
        load_consts(cx, es, ["maskC", "maskW", "cmask", "agg", "Eexp", "impA", "impB"])
        kcmp = sb(es, nc, "n2_kcmp", [128, 128], BF16)
        vcmp = sb(es, nc, "n2_vcmp", [128, 128], BF16)
        kslc = sb(es, nc, "n2_kslc", [128, S], BF16)
        kwin = sb(es, nc, "n2_kwin", [128, S], BF16)
        vslc = sb(es, nc, "n2_vslc", [128, 16, 128], BF16)
        vwin = sb(es, nc, "n2_vwin", [128, 16, 128], BF16)
        kv_r = Res()
        qn = sb(es, nc, "n2_qn", [128, S], BF16); q_r = Res()
        qr = sb(es, nc, "n2_qr", [128, S], BF16)
        psumh = sb(es, nc, "n2_psum", [128, S], F32); psumh_r = Res()
        e32 = Ring(kb, [sb(es, nc, f"n2_e{i}", [128, TT], F32) for i in range(2)])
        pring = Ring(kb, [sb(es, nc, f"n2_p{i}", [128, TT], BF16) for i in range(3)])
        rden = sb(es, nc, "n2_rden", [128, TT], F32); rden_r = Res()
        gbc = Ring(kb, [sb(es, nc, f"n2_gbc{i}", [128, S], F32) for i in range(3)], dma=True)
        ocring = Ring(kb, [sb(es, nc, f"n2_oc{i}", [128, TT], F32) for i in range(3)], dma=True)
        oacc = sb(es, nc, "n2_oacc", [128, TT], F32); oacc_r = Res()
        otmp = sb(es, nc, "n2_otmp", [128, TT], F32); otmp_r = Res()
        oT = sb(es, nc, "n2_oT", [128, S], BF16); oT_r = Res()
        imp = sb(es, nc, "n2_imp", [128, 32], F32); imp_r = Res()
        imp2 = sb(es, nc, "n2_imp2", [128, 32], F32); imp2_r = Res()
        m8 = sb(es, nc, "n2_m8", [128, 16], F32); m8_r = Res()
        sel = sb(es, nc, "n2_sel", [128, 32], F32); sel_r = Res()
        selT = sb(es, nc, "n2_selT", [32, S], BF16); selT_r = Res()
        masks = sb(es, nc, "n2_masks", [128, 40, TT], BF16); mask_rs = [Res() for _ in range(40)]
        ksem = kb.get_dsem(); qsem = kb.get_dsem(); osem = kb.get_dsem()
        for g in range(4):
            kb.dma("sp", kcmp[:, :], scr["kcmp"][g], ksem, writes=[kv_r])
            kb.dma("sp", vcmp[:, :], scr["vcmp"][g], ksem, writes=[kv_r])
            kb.dma("sp", kslc[:, :], scr["nkT"][0, g], ksem, writes=[kv_r])
            kb.dma("sp", kwin[:, :], scr["nkT"][1, g], ksem, writes=[kv_r])
            kb.dma("sp", vslc[:, :, :], scr["nv"][0, g].rearrange("(t p) d -> p t d", p=128), ksem, writes=[kv_r])
            kb.dma("sp", vwin[:, :, :], scr["nv"][1, g].rearrange("(t p) d -> p t d", p=128), ksem, writes=[kv_r])
            for j in range(8):
                hh = g * 8 + j
                kb.dma("sp", qn[:, :], qn_d[hh], qsem, writes=[q_r])
                gb, gb_r, gb_s = gbc.next()
                kb.dma("sp", gb[:, :], bass.AP(gates_d.tensor, (hh * 3 + 0) * S, [[0, 128], [1, S]]), gb_s, writes=[gb_r])
                for t in range(NTT):
                    tsl = slice(t * TT, (t + 1) * TT)
                    ps, ps_r, _ = cx.psA.next()
                    kb.op("pe", lambda e: e.matmul(ps[0:N_CMP, :], lhsT=kcmp[:, 0:N_CMP], rhs=qn[:, tsl], start=True, stop=True), reads=[kv_r, q_r], writes=[ps_r])
                    ef, ef_r, _ = e32.next()
                    kb.op("act", lambda e: e.activation(out=ef[0:N_CMP, :], in_=ps[0:N_CMP, :], func=AF.Exp, scale=scale), reads=[ps_r], writes=[ef_r])
                    kb.op("dve", lambda e: e.tensor_tensor(out=ef[0:N_CMP, :], in0=ef[0:N_CMP, :], in1=cx.cmask[0:N_CMP, tsl], op=ALU.mult), reads=[ef_r], writes=[ef_r])
                    p, p_r, _ = pring.next()
                    kb.op("pool", lambda e: e.tensor_copy(out=p[0:N_CMP, :], in_=ef[0:N_CMP, :]), reads=[ef_r], writes=[p_r])
                    d_ps, d_r, _ = cx.psB.next()
                    kb.op("pe", lambda e: e.matmul(d_ps[:, :], lhsT=cx.ones[0:N_CMP, :], rhs=p[0:N_CMP, :], start=True, stop=True), reads=[p_r], writes=[d_r])
                    o_ps, o_r, _ = cx.psB.next()
                    kb.op("pe", lambda e: e.matmul(o_ps[:, :], lhsT=vcmp[0:N_CMP, :], rhs=p[0:N_CMP, :], start=True, stop=True), reads=[p_r, kv_r], writes=[o_r])
                    kb.op("dve", lambda e: e.tensor_scalar(out=rden[:, :], in0=d_ps[:, :], scalar1=1e-30, scalar2=None, op0=ALU.max), reads=[d_r], writes=[rden_r])
                    kb.op("dve", lambda e: e.reciprocal(out=rden[:, :], in_=rden[:, :]), reads=[rden_r], writes=[rden_r])
                    if j == 0:
                        kb.op("dve", lambda e: e.tensor_tensor(out=psumh[0:N_CMP, tsl], in0=ef[0:N_CMP, :], in1=rden[0:N_CMP, :], op=ALU.mult),
                              reads=[ef_r, rden_r], writes=[psumh_r])
                    else:
                        kb.op("dve", lambda e: e.tensor_tensor(out=ef[0:N_CMP, :], in0=ef[0:N_CMP, :], in1=rden[0:N_CMP, :], op=ALU.mult),
                              reads=[ef_r, rden_r], writes=[ef_r])
                        kb.op("pool", lambda e: e.tensor_tensor(out=psumh[0:N_CMP, tsl], in0=psumh[0:N_CMP, tsl], in1=ef[0:N_CMP, :], op=ALU.add),
                              reads=[ef_r, psumh_r], writes=[psumh_r])
                    oc, oc_r, oc_s = ocring.next()
                    kb.op("dve", lambda e: e.tensor_tensor(out=oc[:, :], in0=o_ps[:, :], in1=rden[:, :], op=ALU.mult), reads=[o_r, rden_r], writes=[oc_r])
                    kb.op("dve", lambda e: e.tensor_tensor(out=oc[:, :], in0=oc[:, :], in1=gb[:, tsl], op=ALU.mult), reads=[oc_r, gb_r], writes=[oc_r])
                    kb.dma("sp", ocmp_d[hh, :, tsl], oc[:, :], oc_s, reads=[oc_r])
            for st in range(16):
                ssl = slice(st * 128, (st + 1) * 128)
                ps, ps_r, _ = cx.psA.next()
                kb.op("pe", lambda e: e.matmul(ps[:, 0:32], lhsT=psumh[0:N_CMP, ssl], rhs=cx.agg[0:N_CMP, :], start=True, stop=True), reads=[psumh_r], writes=[ps_r])
                kb.op("dve", lambda e: e.tensor_tensor(out=imp[:, :], in0=ps[:, 0:32], in1=cx.impA[:, st, :], op=ALU.mult), reads=[ps_r], writes=[imp_r])
                kb.op("dve", lambda e: e.tensor_tensor(out=imp[:, :], in0=imp[:, :], in1=cx.impB[:, st, :], op=ALU.add), reads=[imp_r], writes=[imp_r])
                kb.op("dve", lambda e: e.max(out=m8[:, 0:8], in_=imp[:, :]), reads=[imp_r], writes=[m8_r])
                kb.op("dve", lambda e: e.match_replace(out=imp2[:, :], in_to_replace=m8[:, 0:8], in_values=imp[:, :], imm_value=-1e30), reads=[imp_r, m8_r], writes=[imp2_r])
                kb.op("dve", lambda e: e.max(out=m8[:, 8:16], in_=imp2[:, :]), reads=[imp2_r], writes=[m8_r])
                kb.op("dve", lambda e: e.tensor_scalar(out=sel[:, :], in0=imp[:, :], scalar1=m8[:, 15:16], scalar2=None, op0=ALU.is_ge), reads=[imp_r, m8_r], writes=[sel_r])
                pt, pt_r, _ = cx.psA.next()
                kb.op("pe", lambda e: e.transpose(out=pt[0:32, 0:128], in_=sel[:, :], identity=cx.ident[:, :]), reads=[sel_r], writes=[pt_r])
                kb.op("act", lambda e: e.activation(out=selT[:, ssl], in_=pt[0:32, 0:128], func=AF.Copy), reads=[pt_r], writes=[selT_r])
            midx = {}
            i = 0
            for t in range(NTT):
                tsl = slice(t * TT, (t + 1) * TT)
                for kt in range(4 * t + 4):
                    ps, ps_r, _ = cx.psA.next()
                    kb.op("pe", lambda e: e.matmul(ps[:, :], lhsT=cx.Eexp[:, kt * 128:(kt + 1) * 128], rhs=selT[:, tsl], start=True, stop=True), reads=[selT_r], writes=[ps_r])
                    if kt >= 4 * t:
                        kb.op("dve", lambda e: e.tensor_tensor(out=masks[:, i, :], in0=ps[:, :], in1=cx.maskC[:, kt - 4 * t, :], op=ALU.mult), reads=[ps_r], writes=[mask_rs[i]])
                    else:
                        kb.op("act", lambda e: e.activation(out=masks[:, i, :], in_=ps[:, :], func=AF.Copy), reads=[ps_r], writes=[mask_rs[i]])
                    midx[(t, kt)] = i
                    i += 1
            for j in range(8):
                hh = g * 8 + j
                kb.dma("sp", qr[:, :], qr_d[hh], qsem, writes=[q_r])
                g1, g1_r, g1_s = gbc.next()
                kb.dma("sp", g1[:, :], bass.AP(gates_d.tensor, (hh * 3 + 1) * S, [[0, 128], [1, S]]), g1_s, writes=[g1_r])
                g2, g2_r, g2_s = gbc.next()
                kb.dma("sp", g2[:, :], bass.AP(gates_d.tensor, (hh * 3 + 2) * S, [[0, 128], [1, S]]), g2_s, writes=[g2_r])
                for t in range(NTT):
                    tsl = slice(t * TT, (t + 1) * TT)
                    oc, oc_r, oc_s = ocring.next()
                    kb.dma("sp", oc[:, :], ocmp_d[hh, :, tsl], oc_s, writes=[oc_r])
                    o_ps, o_r, _ = cx.psB.next()
                    d_ps, d_r, _ = cx.psB.next()
                    nk = 4 * t + 4
                    for kt in range(nk):
                        mi = midx[(t, kt)]
                        attn_tile(cx, [(qr[:, tsl], q_r)], [(kslc[:, kt * 128:(kt + 1) * 128], kv_r)], scale, vslc[:, kt, :], kv_r, cx.ones[:, :],
                                  masks[:, mi, :], mask_rs[mi], o_ps, o_r, d_ps, d_r, kt == 0, kt == nk - 1, pring)
                    kb.op("dve", lambda e: e.reciprocal(out=rden[:, :], in_=d_ps[:, :]), reads=[d_r], writes=[rden_r])
                    kb.op("dve", lambda e: e.tensor_tensor(out=otmp[:, :], in0=o_ps[:, :], in1=rden[:, :], op=ALU.mult), reads=[o_r, rden_r], writes=[otmp_r])
                    kb.op("pool", lambda e: e.tensor_tensor(out=otmp[:, :], in0=otmp[:, :], in1=g1[:, tsl], op=ALU.mult), reads=[otmp_r, g1_r], writes=[otmp_r])
                    kb.op("pool", lambda e: e.tensor_tensor(out=oacc[:, :], in0=otmp[:, :], in1=oc[:, :], op=ALU.add), reads=[otmp_r, oc_r], writes=[oacc_r])
                    o_ps, o_r, _ = cx.psB.next()
                    d_ps, d_r, _ = cx.psB.next()
                    kts = [kt for kt in range(4 * t - 4, 4 * t + 4) if kt >= 0]
                    for ii, kt in enumerate(kts):
                        attn_tile(cx, [(qr[:, tsl], q_r)], [(kwin[:, kt * 128:(kt + 1) * 128], kv_r)], scale, vwin[:, kt, :], kv_r, cx.ones[:, :],
                                  cx.maskW[:, kt - (4 * t - 4), :], None, o_ps, o_r, d_ps, d_r, ii == 0, ii == len(kts) - 1, pring)
                    kb.op("dve", lambda e: e.reciprocal(out=rden[:, :], in_=d_ps[:, :]), reads=[d_r], writes=[rden_r])
                    kb.op("dve", lambda e: e.tensor_tensor(out=otmp[:, :], in0=o_ps[:, :], in1=rden[:, :], op=ALU.mult), reads=[o_r, rden_r], writes=[otmp_r])
                    kb.op("pool", lambda e: e.tensor_tensor(out=otmp[:, :], in0=otmp[:, :], in1=g2[:, tsl], op=ALU.mult), reads=[otmp_r, g2_r], writes=[otmp_r])
                    kb.op("pool", lambda e: e.tensor_tensor(out=oT[:, tsl], in0=otmp[:, :], in1=oacc[:, :], op=ALU.add), reads=[otmp_r, oacc_r], writes=[oT_r])
                kb.dma("sp", oT_dv[:, hh, :], oT[:, :], osem, reads=[oT_r])
        kb.barrier()
        gbc.release(); ocring.release()
        for s in (ksem, qsem, osem):
            kb.put_dsem(s)
    out_proj_stage(cx, oT_d, W["nsa_w_o"][b], hin, hout)


VEC_SPECS = None


def cols_of(v):
    v = np.asarray(v, np.float32)
    n = v.shape[0]
    if n <= 128:
        o = np.zeros((128, 1), np.float32)
        o[:n, 0] = v
        return o
    assert n % 128 == 0
    return np.ascontiguousarray(v.reshape(n // 128, 128).T)


def build_vecs(inp):
    cols = []
    vcol = {}
    pos = [0]

    def add(name, v, idx=None):
        c = cols_of(v)
        if idx is None:
            vcol[name] = pos[0]
        else:
            vcol.setdefault(name, {})[idx] = pos[0]
        cols.append(c)
        pos[0] += c.shape[1]

    for l in range(DEPTH):
        add("ffn1_norm", inp["ffn1_norm"][l], l)
        add("mix_norm", inp["mix_norm"][l], l)
        add("ffn2_norm", inp["ffn2_norm"][l], l)
    for a in range(2):
        add("mla_g_cq", inp["mla_g_cq"][a], a)
        add("mla_g_ckv", inp["mla_g_ckv"][a], a)
        add("mla_g_q_n", inp["mla_g_q"][a][:128], a)
        add("mla_g_q_r", inp["mla_g_q"][a][128:], a)
        add("mla_g_k_n", inp["mla_g_k"][a][:128], a)
        add("mla_g_k_r", inp["mla_g_k"][a][128:], a)
    add("kv_norm", inp["kv_norm"])
    add("cmp_k_b1", inp["cmp_k_b1"])
    add("cmp_v_b1", inp["cmp_v_b1"])
    add("g_k_cmp", inp["g_k_cmp"])
    add("g_k_slc", inp["g_k_slc"])
    add("g_k_win", inp["g_k_win"])
    for b in range(2):
        add("nsa_b_gate", inp["nsa_b_gate"][b], b)
        add("nsa_g_q", inp["nsa_g_q"][b], b)
    return np.ascontiguousarray(np.concatenate(cols, axis=1)), vcol


def build_consts():
    import ml_dtypes
    bf = ml_dtypes.bfloat16
    c = {}
    c["ones"] = np.ones((128, 128), bf)
    c["ident"] = np.eye(128, dtype=np.float32)
    p128 = np.zeros((128, 128), np.float32)
    for i in range(64):
        p128[i, i + 64] = 1; p128[i + 64, i] = 1
    p64 = np.zeros((128, 128), np.float32)
    for i in range(32):
        p64[i, i + 32] = 1; p64[i + 32, i] = 1
    c["perm128"] = p128.astype(bf); c["perm64"] = p64.astype(bf)
    misc = np.zeros((128, 8), np.float32)
    misc[:, 0] = EPS; misc[:, 1] = np.pi
    invA = (10000.0 ** (-(np.arange(0, 64, 2, dtype=np.float32)) / 64)).astype(np.float32)
    invB = (10000.0 ** (-(np.arange(0, 128, 2, dtype=np.float32)) / 128)).astype(np.float32)
    misc[:64, 2] = np.concatenate([invA, invA]); misc[:32, 3] = -1; misc[32:64, 3] = 1
    misc[:, 4] = np.concatenate([invB, invB]); misc[:64, 5] = -1; misc[64:, 5] = 1
    c["misc"] = misc
    k = np.arange(128)[:, None]; q = np.arange(512)[None, :]
    c["maskC"] = np.stack([((j * 128 + k) <= q) for j in range(4)], axis=1).astype(bf)
    mw = []
    for i in range(8):
        dlt = (i - 4) * 128
        diff = q - k - dlt
        mw.append((diff >= 0) & (diff < 512))
    c["maskW"] = np.stack(mw, axis=1).astype(bf)
    n = np.arange(128)[:, None]; s = np.arange(S)[None, :]
    c["cmask"] = (((16 * n + 31) <= s) & (n < N_CMP)).astype(np.float32)
    cs = np.arange(N_CMP)[:, None] * 16; ss = np.arange(32)[None, :] * 64
    ov = np.clip(np.minimum(cs + 32, ss + 64) - np.maximum(cs, ss), 0, None)
    agg = np.zeros((128, 32), np.float32); agg[:N_CMP] = ov / 16
    c["agg"] = agg
    E = np.zeros((32, S), np.float32)
    E[np.arange(S) // 64, np.arange(S)] = 1
    c["Eexp"] = E.astype(bf)
    spos = np.arange(S)[:, None]; jb = np.arange(32)[None, :]
    cur = spos // 64
    valid = (jb * 64) <= spos
    forced = (jb == 0) | (jb == cur) | (jb == cur - 1)
    A = (valid & ~forced).astype(np.float32)
    B = np.where(forced, 1e6, np.where(valid, 0.0, -1.0)).astype(np.float32)
    c["impA"] = np.ascontiguousarray(A.reshape(16, 128, 32).transpose(1, 0, 2))
    c["impB"] = np.ascontiguousarray(B.reshape(16, 128, 32).transpose(1, 0, 2))
    return c


CONST_DT = {"ones": BF16, "ident": F32, "perm128": BF16, "perm64": BF16, "misc": F32, "maskC": BF16, "maskW": BF16,
            "cmask": F32, "agg": F32, "Eexp": BF16, "impA": F32, "impB": F32}

WEIGHT_NAMES = ["ffn1_w_gate", "ffn1_w_up", "ffn1_w_down", "ffn2_w_gate", "ffn2_w_up", "ffn2_w_down",
                "mla_w_in", "mla_w_uq", "mla_w_ukv", "mla_w_o", "kv_w", "cmp_k_w1", "cmp_k_w2", "cmp_v_w1", "cmp_v_w2",
                "nsa_w_in", "nsa_w_o"]


def build_program(shapes, vcol, nvec, consts, stages=None, plan=None):
    nc = bass.Bass("TRN2", target_bir_lowering=False)
    xT = nc.dram_tensor("xT", [D, S], F32, kind="ExternalInput").ap()
    pos = nc.dram_tensor("pos", [1, S], I32, kind="ExternalInput").ap()
    vecs_d = nc.dram_tensor("vecs", [128, nvec], F32, kind="ExternalInput").ap()
    posk_d = nc.dram_tensor("posTk", [128, 32], F32, kind="ExternalInput").ap()
    posv_d = nc.dram_tensor("posTv", [128, 32], F32, kind="ExternalInput").ap()
    cd = {k: nc.dram_tensor("c_" + k, list(v.shape), CONST_DT[k], kind="ExternalInput").ap() for k, v in consts.items()}
    W = {k: nc.dram_tensor(k, list(shapes[k]), F32, kind="ExternalInput").ap() for k in WEIGHT_NAMES}
    outT = nc.dram_tensor("outT", [D, S], F32, kind="ExternalOutput").ap()
    hA = nc.dram_tensor("hA", [D, S], F32).ap()
    hB = nc.dram_tensor("hB", [D, S], F32).ap()
    scr = {
        "cqn": nc.dram_tensor("s_cqn", [MLA_QL, S], BF16).ap(),
        "ckvn": nc.dram_tensor("s_ckvn", [MLA_KVL, S], BF16).ap(),
        "kr": nc.dram_tensor("s_kr", [64, S], F32).ap(),
        "oT": nc.dram_tensor("s_oT", [D, S], BF16).ap(),
        "craw": nc.dram_tensor("s_craw", [2, 4, 128, S], BF16).ap(),
        "nkT": nc.dram_tensor("s_nkT", [2, 4, 128, S], BF16).ap(),
        "nv": nc.dram_tensor("s_nv", [2, 4, S, 128], BF16).ap(),
        "kcmp": nc.dram_tensor("s_kcmp", [4, 128, 128], BF16).ap(),
        "vcmp": nc.dram_tensor("s_vcmp", [4, 128, 128], BF16).ap(),
        "qn": nc.dram_tensor("s_qn", [H, 128, S], BF16).ap(),
        "qr": nc.dram_tensor("s_qr", [H, 128, S], BF16).ap(),
        "gates": nc.dram_tensor("s_gates", [96, S], F32).ap(),
        "ocmp": nc.dram_tensor("s_ocmp", [H, 128, S], F32).ap(),
    }
    kb = KB(nc)
    cx = Ctx()
    cx.kb = kb; cx.nc = nc; cx.vcol = vcol
    with ExitStack() as es:
        psum = [es.enter_context(nc.psum_tensor(f"ps{i}", [128, 512], F32)) for i in range(8)]
        cx.psA = Ring(kb, psum[0:4])
        cx.psB = Ring(kb, psum[4:8])
        csem = kb.get_dsem()
        cx.vecs = sb(es, nc, "vecs_sb", [128, nvec], F32)
        kb.dma("sp", cx.vecs[:, :], vecs_d, csem)
        cx.cdram = {k: (cd[k], list(v.shape), CONST_DT[k]) for k, v in consts.items()}
        for nm in ("cosA", "sinA", "cosB", "sinB"):
            cx.cdram[nm] = (nc.dram_tensor("s_" + nm, [128, S], F32).ap(), [128, S], F32)
        ct = {}
        for k in ("ones", "ident", "perm128", "perm64", "misc"):
            v = consts[k]
            ct[k] = sb(es, nc, "k_" + k, list(v.shape), CONST_DT[k])
            kb.dma("sp", ct[k][:, :], cd[k], csem)
        cx.ones = ct["ones"]; cx.ident = ct["ident"]; cx.perm128 = ct["perm128"]; cx.perm64 = ct["perm64"]
        misc = ct["misc"]
        cx.eps_col = misc[:, 0:1]; cx.pi_col = misc[:, 1:2]
        pk32 = sb(es, nc, "posk32", [128, 32], F32); pv32 = sb(es, nc, "posv32", [128, 32], F32)
        kb.dma("sp", pk32[:, :], posk_d, csem); kb.dma("sp", pv32[:, :], posv_d, csem)
        cx.posT_k = sb(es, nc, "posk", [128, 32], BF16); cx.posT_v = sb(es, nc, "posv", [128, 32], BF16)
        kb.barrier()
        kb.op("dve", lambda e: e.tensor_copy(out=cx.posT_k[:, :], in_=pk32[:, :]))
        kb.op("dve", lambda e: e.tensor_copy(out=cx.posT_v[:, :], in_=pv32[:, :]))
        kb.barrier()
        build_rope_tables(cx, pos, misc[0:64, 2:3], misc[0:64, 3:4], 64, cx.cdram["cosA"][0], cx.cdram["sinA"][0])
        build_rope_tables(cx, pos, misc[:, 4:5], misc[:, 5:6], 128, cx.cdram["cosB"][0], cx.cdram["sinB"][0])
        user_plan = plan
        plan = []
        for L in range(DEPTH):
            plan.append(("ffn1", L))
            plan.append(("mix", L))
            plan.append(("ffn2", L))
            if L == 1:
                plan.append(("kv", L))
        if stages is not None:
            plan = plan[:stages]
        if user_plan is not None:
            plan = list(user_plan)
        cur = xT
        nxt = [hA, hB]
        ni = 0
        for si, (kind, L) in enumerate(plan):
            last = si == len(plan) - 1
            if kind == "kv":
                shared_kv_stage(cx, cur, W, scr)
                compress_stage(cx, W, scr)
                if last:
                    pass
                continue
            dst = outT if (last or (kind == "ffn2" and si + 1 < len(plan) and plan[si + 1][0] == "kv" and si + 2 == len(plan))) else nxt[ni]
            if kind == "ffn1":
                ffn_stage(cx, cur, dst, vcol["ffn1_norm"][L], W["ffn1_w_gate"][L], W["ffn1_w_up"][L], W["ffn1_w_down"][L])
            elif kind == "ffn2":
                ffn_stage(cx, cur, dst, vcol["ffn2_norm"][L], W["ffn2_w_gate"][L], W["ffn2_w_up"][L], W["ffn2_w_down"][L])
            elif L < 2:
                mla_stage(cx, L, L, cur, dst, W, scr)
            else:
                nsa_stage(cx, L, L - 2, cur, dst, W, scr)
            cur = dst
            if dst is not outT:
                ni ^= 1
        kb.barrier()
    cx.n_ins = kb.n_ins
    return nc, scr


_CACHE = {}


def run_model(inputs, cores, stages=None, trace=False, plan=None):
    inp = {k: np.asarray(v) for k, v in inputs.items()}
    vecs, vcol = build_vecs(inp)
    consts = build_consts()
    shapes = {k: inp[k].shape for k in WEIGHT_NAMES}
    key = (stages, tuple(plan) if plan else None)
    if key not in _CACHE:
        _CACHE[key] = build_program(shapes, vcol, vecs.shape[1], consts, stages, plan)
    nc, _ = _CACHE[key]
    in_maps = []
    for b in cores:
        m = {"xT": np.ascontiguousarray(inp["x"][b].T), "pos": np.ascontiguousarray(inp["positions"][b][None, :].astype(np.int32)),
             "vecs": vecs, "posTk": np.ascontiguousarray(inp["cmp_pos_k"].T.astype(np.float32)),
             "posTv": np.ascontiguousarray(inp["cmp_pos_v"].T.astype(np.float32))}
        for k, v in consts.items():
            m["c_" + k] = v
        for k in WEIGHT_NAMES:
            m[k] = np.ascontiguousarray(inp[k], dtype=np.float32)
        in_maps.append(m)
    res = run_bass_kernel_spmd(nc, in_maps, core_ids=list(range(len(cores))), trace=trace)
    outs = [np.ascontiguousarray(r["outT"].T) for r in res.results]
    return outs, res


def kernel(**inputs):
    outs, _ = run_model(inputs, list(range(NB)))
    return np.stack(outs, axis=0).astype(np.float32)
```

```python
import numpy as np
from contextlib import ExitStack
import concourse.bass as bass
import concourse.mybir as mybir
from concourse.bass_utils import run_bass_kernel_spmd

F32 = mybir.dt.float32
BF16 = mybir.dt.bfloat16
I32 = mybir.dt.int32
ALU = mybir.AluOpType
AF = mybir.ActivationFunctionType

D = 4096; S = 2048; FF = 6144; DEPTH = 4; NB = 4
KC = D // 128; FC = FF // 128
EPS = 1e-6
TT = 512; NTT = S // TT
H = 32
MLA_QL = 1024; MLA_KVL = 512; MLA_IN = 1600
NSA_IN = 4192
N_CMP = 127
PI = float(np.pi)


class Res:
    __slots__ = ("w", "r")

    def __init__(self):
        self.w = None
        self.r = {}


class CSem:
    def __init__(self, h):
        self.h = h
        self.count = 0


class KB:
    def __init__(self, nc, n_dma_sems=48):
        self.nc = nc
        self.eng = {"pe": nc.tensor, "act": nc.scalar, "dve": nc.vector, "pool": nc.gpsimd, "sp": nc.sync}
        self.esem = {e: CSem(nc.alloc_semaphore(f"s_{e}")) for e in ("pe", "act", "dve", "pool")}
        self.seen = {e: {} for e in self.eng}
        self.dsems = [CSem(nc.alloc_semaphore(f"s_d{i}")) for i in range(n_dma_sems)]
        self.free_dsems = list(self.dsems)
        self.n_ins = 0

    def get_dsem(self):
        return self.free_dsems.pop()

    def put_dsem(self, s):
        self.free_dsems.append(s)

    def _waits(self, e, reads, writes):
        need = {}
        for r in reads:
            if r is not None and r.w is not None:
                s, v = r.w
                if need.get(s, 0) < v:
                    need[s] = v
        for w in writes:
            if w is None:
                continue
            if w.w is not None:
                s, v = w.w
                if need.get(s, 0) < v:
                    need[s] = v
            for s, v in w.r.items():
                if need.get(s, 0) < v:
                    need[s] = v
        seen = self.seen[e]
        own = self.esem.get(e)
        for s, v in need.items():
            if e == "pe" and s is own:
                continue
            if seen.get(s, 0) < v:
                self.eng[e].wait_ge(s.h, v)
                seen[s] = v
                self.n_ins += 1

    def op(self, e, fn, reads=(), writes=(), inc=True):
        self._waits(e, reads, writes)
        ins = fn(self.eng[e])
        self.n_ins += 1
        s = self.esem[e]
        if inc:
            s.count += 1
            ins.then_inc(s.h, 1)
            v = s.count
        else:
            v = s.count + 1
        for r in reads:
            if r is not None and r.r.get(s, 0) < v:
                r.r[s] = v
        for w in writes:
            if w is not None:
                w.w = (s, v)
                w.r = {}
        return ins

    def dma(self, q, out, in_, sem, reads=(), writes=()):
        self._waits(q, reads, writes)
        ins = self.eng[q].dma_start(out=out, in_=in_)
        self.n_ins += 1
        sem.count += 16
        ins.then_inc(sem.h, 16)
        v = sem.count
        for r in reads:
            if r is not None and r.r.get(sem, 0) < v:
                r.r[sem] = v
        for w in writes:
            if w is not None:
                w.w = (sem, v)
                w.r = {}
        return ins

    def barrier(self):
        sems = list(self.esem.values()) + self.dsems
        for e in self.eng:
            seen = self.seen[e]
            for s in sems:
                if s.count > seen.get(s, 0):
                    self.eng[e].wait_ge(s.h, s.count)
                    seen[s] = s.count
                    self.n_ins += 1


class Ring:
    def __init__(self, kb, tiles, dma=False):
        self.kb = kb
        self.tiles = tiles
        self.res = [Res() for _ in tiles]
        self.sems = [kb.get_dsem() for _ in tiles] if dma else [None] * len(tiles)
        self.i = 0

    def next(self):
        i = self.i
        self.i = (i + 1) % len(self.tiles)
        return self.tiles[i], self.res[i], self.sems[i]

    def release(self):
        for s in self.sems:
            if s is not None:
                self.kb.put_dsem(s)


class Ctx:
    pass


_UID = [0]


def sb(es, nc, name, shape, dt):
    _UID[0] += 1
    return es.enter_context(nc.sbuf_tensor(f"{name}_{_UID[0]}", shape, dt))


def load_consts(cx, es, names):
    kb, nc = cx.kb, cx.nc
    sem = kb.get_dsem()
    for nm in names:
        ap, shape, dt = cx.cdram[nm]
        t = sb(es, nc, "k_" + nm, shape, dt)
        kb.dma("sp", t[tuple(slice(None) for _ in shape)], ap, sem)
        setattr(cx, nm, t)
    kb.barrier()
    kb.put_dsem(sem)


def act_recip(cx, out_ap, in_ap, in_r, out_r):
    kb = cx.kb
    kb.op("act", lambda e: e.activation(out=out_ap, in_=in_ap, func=AF.Ln), reads=[in_r], writes=[out_r])
    kb.op("act", lambda e: e.activation(out=out_ap, in_=out_ap, func=AF.Exp, scale=-1.0), reads=[out_r], writes=[out_r])


def rms_rstd(cx, es, chunks, dim, N, rstd, rstd_r, sqring):
    kb = cx.kb
    ps, ps_r, _ = cx.psA.next()
    n = len(chunks)
    for i, (ap, r, P) in enumerate(chunks):
        s, s_r, _ = sqring.next()
        kb.op("act", lambda e: e.activation(out=s[0:P, 0:N], in_=ap, func=AF.Square), reads=[r], writes=[s_r])
        kb.op("pe", lambda e: e.matmul(ps[:, 0:N], lhsT=cx.ones[0:P, :], rhs=s[0:P, 0:N], start=(i == 0), stop=(i == n - 1)),
              reads=[s_r], writes=[ps_r])
    kb.op("act", lambda e: e.activation(out=rstd[:, 0:N], in_=ps[:, 0:N], func=AF.Ln, scale=1.0 / dim, bias=cx.eps_col[:, 0:1]),
          reads=[ps_r], writes=[rstd_r])
    kb.op("act", lambda e: e.activation(out=rstd[:, 0:N], in_=rstd[:, 0:N], func=AF.Exp, scale=-0.5), reads=[rstd_r], writes=[rstd_r])


def norm_tile(cx, es, hin_v, gcol0, t0, N, yT, yT_r, ycol0, xring, sqring, rstd, rstd_r):
    kb = cx.kb
    xt = []
    for q in range(4):
        x, x_r, x_s = xring.next()
        kb.dma("sp", x[:, :, 0:N], hin_v[:, q * 8:(q + 1) * 8, t0:t0 + N], x_s, writes=[x_r])
        xt.append((x, x_r))
    chunks = [(xt[c // 8][0][:, c % 8, 0:N], xt[c // 8][1], 128) for c in range(KC)]
    rms_rstd(cx, es, chunks, D, N, rstd, rstd_r, sqring)
    for c in range(KC):
        x, x_r = xt[c // 8]
        kb.op("dve", lambda e: e.scalar_tensor_tensor(out=yT[:, c, ycol0:ycol0 + N], in0=x[:, c % 8, 0:N],
              scalar=cx.vecs[:, gcol0 + c:gcol0 + c + 1], in1=rstd[:, 0:N], op0=ALU.mult, op1=ALU.mult),
              reads=[x_r, rstd_r], writes=[yT_r])


def proj_fm(cx, wv, cols, x, x_r, n_kc, tiles, evac, wring):
    kb = cx.kb
    for ci, (c0, M) in enumerate(cols):
        w, w_r, w_s = wring.next()
        kb.dma("pool", w[:, 0:n_kc, 0:M], wv[:, 0:n_kc, c0:c0 + M], w_s, writes=[w_r])
        Mp = 128 if M < 128 else M
        for ti, (col0, N) in enumerate(tiles):
            ps, ps_r, _ = cx.psA.next()
            for kc in range(n_kc):
                kb.op("pe", lambda e: e.matmul(ps[0:Mp, 0:N], lhsT=w[:, kc, 0:Mp], rhs=x[:, kc, col0:col0 + N],
                      start=(kc == 0), stop=(kc == n_kc - 1)), reads=[w_r, x_r], writes=[ps_r], inc=(kc == n_kc - 1))
            evac(ci, ti, ps, ps_r, M, N)


def out_proj_stage(cx, oT_d, w_o, hin, hout):
    kb, nc = cx.kb, cx.nc
    o_v = oT_d.rearrange("(c p) n -> p c n", p=128)
    w_v = w_o.rearrange("(c p) f -> p c f", p=128)
    hin_v = hin.rearrange("(c p) n -> p c n", p=128)
    hout_v = hout.rearrange("(c p) n -> p c n", p=128)
    with ExitStack() as es:
        oT = sb(es, nc, "op_oT", [128, KC, 1024], BF16); oT_r = Res()
        osem = kb.get_dsem()
        wring = Ring(kb, [sb(es, nc, f"op_w{i}", [128, KC, 128], BF16) for i in range(3)], dma=True)
        hxring = Ring(kb, [sb(es, nc, f"op_hx{i}", [128, TT], F32) for i in range(3)], dma=True)
        obring = Ring(kb, [sb(es, nc, f"op_ob{i}", [128, TT], F32) for i in range(3)], dma=True)
        for half in range(2):
            h0 = half * 1024
            for q in range(4):
                kb.dma("sp", oT[:, q * 8:(q + 1) * 8, :], o_v[:, q * 8:(q + 1) * 8, h0:h0 + 1024], osem, writes=[oT_r])

            def evac(ci, ti, ps, ps_r, M, N):
                t0 = h0 + ti * TT
                hx, hx_r, hx_s = hxring.next()
                kb.dma("sp", hx[:, :], hin_v[:, ci, t0:t0 + TT], hx_s, writes=[hx_r])
                o, o_r, o_s = obring.next()
                kb.op("dve", lambda e: e.tensor_tensor(out=o[:, :], in0=ps[:, :], in1=hx[:, :], op=ALU.add),
                      reads=[ps_r, hx_r], writes=[o_r])
                kb.dma("sp", hout_v[:, ci, t0:t0 + TT], o[:, :], o_s, reads=[o_r])

            proj_fm(cx, w_v, [(c * 128, 128) for c in range(KC)], oT, oT_r, KC, [(0, TT), (TT, TT)], evac, wring)
        kb.barrier()
        wring.release(); hxring.release(); obring.release(); kb.put_dsem(osem)


def ffn_half(cx, hin, hout, gcol0, wg, wu, wd, tok0):
    kb, nc = cx.kb, cx.nc
    NT = 1024
    hin_v = hin.rearrange("(c p) n -> p c n", p=128)
    hout_v = hout.rearrange("(c p) n -> p c n", p=128)
    wg_v = wg.rearrange("(c p) f -> p c f", p=128)
    wu_v = wu.rearrange("(c p) f -> p c f", p=128)
    wd_v = wd.rearrange("(c p) f -> p c f", p=128)
    with ExitStack() as es:
        yT = sb(es, nc, "ff_yT", [128, KC, NT], BF16); yT_r = Res()
        with ExitStack() as es0:
            xring = Ring(kb, [sb(es0, nc, f"ff_x{i}", [128, 8, TT], F32) for i in range(4)], dma=True)
            sqring = Ring(kb, [sb(es0, nc, f"ff_sq{i}", [128, TT], BF16) for i in range(3)])
            rstd = sb(es0, nc, "ff_rstd", [128, TT], F32); rstd_r = Res()
            for t in range(NT // TT):
                norm_tile(cx, es0, hin_v, gcol0, tok0 + t * TT, TT, yT, yT_r, t * TT, xring, sqring, rstd, rstd_r)
            kb.barrier()
            xring.release()
        with ExitStack() as es1:
            actT = sb(es1, nc, "ff_actT", [128, FC, NT], BF16); actT_r = Res()
            with ExitStack() as es1a:
                wring = Ring(kb, [sb(es1a, nc, f"ff_w{i}", [128, 16, 128], BF16) for i in range(8)], dma=True)
                sgring = Ring(kb, [sb(es1a, nc, f"ff_sg{i}", [128, TT], F32) for i in range(2)])
                for fc in range(FC):
                    fsl = slice(fc * 128, (fc + 1) * 128)
                    units = {}
                    for nm, wv in (("g", wg_v), ("u", wu_v)):
                        for kh in range(2):
                            w, w_r, w_s = wring.next()
                            kb.dma("pool", w[:, :, :], wv[:, kh * 16:(kh + 1) * 16, fsl], w_s, writes=[w_r])
                            units[(nm, kh)] = (w, w_r)
                    for t in range(NT // TT):
                        tsl = slice(t * TT, (t + 1) * TT)
                        pg, pg_r, _ = cx.psA.next()
                        pu, pu_r, _ = cx.psA.next()
                        for nm, p, p_r in (("g", pg, pg_r), ("u", pu, pu_r)):
                            for c in range(KC):
                                w, w_r = units[(nm, c // 16)]
                                kb.op("pe", lambda e: e.matmul(p[:, :], lhsT=w[:, c % 16, :], rhs=yT[:, c, tsl],
                                      start=(c == 0), stop=(c == KC - 1)), reads=[w_r, yT_r], writes=[p_r], inc=(c == KC - 1))
                        s, s_r, _ = sgring.next()
                        kb.op("act", lambda e: e.activation(out=s[:, :], in_=pg[:, :], func=AF.Silu), reads=[pg_r], writes=[s_r])
                        kb.op("dve", lambda e: e.tensor_tensor(out=actT[:, fc, tsl], in0=s[:, :], in1=pu[:, :], op=ALU.mult),
                              reads=[s_r, pu_r], writes=[actT_r])
                kb.barrier()
                wring.release()
            with ExitStack() as es2:
                wdring = Ring(kb, [sb(es2, nc, f"ff_wd{i}", [128, 16, 128], BF16) for i in range(6)], dma=True)
                hxring = Ring(kb, [sb(es2, nc, f"ff_hx{i}", [128, TT], F32) for i in range(3)], dma=True)
                obring = Ring(kb, [sb(es2, nc, f"ff_ob{i}", [128, TT], F32) for i in range(3)], dma=True)
                for dc in range(KC):
                    dsl = slice(dc * 128, (dc + 1) * 128)
                    units = []
                    for kh in range(3):
                        w, w_r, w_s = wdring.next()
                        kb.dma("pool", w[:, :, :], wd_v[:, kh * 16:(kh + 1) * 16, dsl], w_s, writes=[w_r])
                        units.append((w, w_r))
                    for t in range(NT // TT):
                        tsl = slice(t * TT, (t + 1) * TT)
                        gsl = slice(tok0 + t * TT, tok0 + (t + 1) * TT)
                        hx, hx_r, hx_s = hxring.next()
                        kb.dma("sp", hx[:, :], hin_v[:, dc, gsl], hx_s, writes=[hx_r])
                        po, po_r, _ = cx.psA.next()
                        for c in range(FC):
                            w, w_r = units[c // 16]
                            kb.op("pe", lambda e: e.matmul(po[:, :], lhsT=w[:, c % 16, :], rhs=actT[:, c, tsl],
                                  start=(c == 0), stop=(c == FC - 1)), reads=[w_r, actT_r], writes=[po_r], inc=(c == FC - 1))
                        o, o_r, o_s = obring.next()
                        kb.op("dve", lambda e: e.scalar_tensor_tensor(out=o[:, :], in0=po[:, :], scalar=0.5, in1=hx[:, :],
                              op0=ALU.mult, op1=ALU.add), reads=[po_r, hx_r], writes=[o_r])
                        kb.dma("sp", hout_v[:, dc, gsl], o[:, :], o_s, reads=[o_r])
                kb.barrier()
                wdring.release(); hxring.release(); obring.release()


def ffn_stage(cx, hin, hout, gcol0, wg, wu, wd):
    for half in range(2):
        ffn_half(cx, hin, hout, gcol0, wg, wu, wd, half * 1024)


def rope_apply(cx, xb, xb_r, P, N, tcol0, perm, cosT, sinT, out_ap, out_r, tmpring):
    kb = cx.kb
    ps, ps_r, _ = cx.psA.next()
    kb.op("pe", lambda e: e.matmul(ps[0:P, 0:N], lhsT=perm[0:P, 0:P], rhs=xb, start=True, stop=True), reads=[xb_r], writes=[ps_r])
    t1, t1_r, _ = tmpring.next()
    kb.op("dve", lambda e: e.tensor_tensor(out=t1[0:P, 0:N], in0=ps[0:P, 0:N], in1=sinT[0:P, tcol0:tcol0 + N], op=ALU.mult),
          reads=[ps_r], writes=[t1_r])
    t2, t2_r, _ = tmpring.next()
    kb.op("pool", lambda e: e.tensor_tensor(out=t2[0:P, 0:N], in0=xb, in1=cosT[0:P, tcol0:tcol0 + N], op=ALU.mult),
          reads=[xb_r], writes=[t2_r])
    kb.op("dve", lambda e: e.tensor_tensor(out=out_ap, in0=t1[0:P, 0:N], in1=t2[0:P, 0:N], op=ALU.add),
          reads=[t1_r, t2_r], writes=[out_r])


def build_rope_tables(cx, pos_d, invcol, sgncol, P, cos_d, sin_d):
    kb, nc = cx.kb, cx.nc
    r = Res()
    with ExitStack() as es2:
        cosT = sb(es2, nc, "rt_cos", [128, S], F32)
        sinT = sb(es2, nc, "rt_sin", [128, S], F32)
        pi_ = sb(es2, nc, "rt_pi", [128, S], I32)
        ang = sb(es2, nc, "rt_ang", [128, S], F32)
        tmp = sb(es2, nc, "rt_tmp", [128, S], F32)
        sem = kb.get_dsem()
        src = bass.AP(pos_d.tensor, 0, [[0, 128], [1, S]])
        kb.dma("sp", pi_[:, :], src, sem, writes=[r])
        kb.op("dve", lambda e: e.memset(cosT[:, :], 0.0), writes=[r])
        kb.op("dve", lambda e: e.memset(sinT[:, :], 0.0), writes=[r])
        kb.op("dve", lambda e: e.tensor_copy(out=ang[0:P, :], in_=pi_[0:P, :]), reads=[r], writes=[r])
        kb.op("dve", lambda e: e.tensor_scalar(out=ang[0:P, :], in0=ang[0:P, :], scalar1=invcol, scalar2=None, op0=ALU.mult), reads=[r], writes=[r])
        ki = sb(es2, nc, "rt_ki", [128, S], I32)
        msk = sb(es2, nc, "rt_m", [128, S], F32)

        def sin_of(out_t, shift):
            kb.op("dve", lambda e: e.tensor_scalar(out=tmp[0:P, :], in0=ang[0:P, :], scalar1=shift, scalar2=None, op0=ALU.add), reads=[r], writes=[r])
            kb.op("dve", lambda e: e.tensor_scalar(out=msk[0:P, :], in0=tmp[0:P, :], scalar1=1.0 / (2 * PI), scalar2=None, op0=ALU.mult), reads=[r], writes=[r])
            kb.op("dve", lambda e: e.tensor_copy(out=ki[0:P, :], in_=msk[0:P, :]), reads=[r], writes=[r])
            kb.op("dve", lambda e: e.tensor_copy(out=msk[0:P, :], in_=ki[0:P, :]), reads=[r], writes=[r])
            kb.op("dve", lambda e: e.scalar_tensor_tensor(out=tmp[0:P, :], in0=msk[0:P, :], scalar=-2 * PI, in1=tmp[0:P, :], op0=ALU.mult, op1=ALU.add), reads=[r], writes=[r])
            kb.op("dve", lambda e: e.tensor_scalar(out=msk[0:P, :], in0=tmp[0:P, :], scalar1=PI, scalar2=None, op0=ALU.is_gt), reads=[r], writes=[r])
            kb.op("dve", lambda e: e.scalar_tensor_tensor(out=tmp[0:P, :], in0=msk[0:P, :], scalar=-2 * PI, in1=tmp[0:P, :], op0=ALU.mult, op1=ALU.add), reads=[r], writes=[r])
            kb.op("dve", lambda e: e.tensor_scalar(out=msk[0:P, :], in0=tmp[0:P, :], scalar1=-PI, scalar2=None, op0=ALU.is_lt), reads=[r], writes=[r])
            kb.op("dve", lambda e: e.scalar_tensor_tensor(out=tmp[0:P, :], in0=msk[0:P, :], scalar=2 * PI, in1=tmp[0:P, :], op0=ALU.mult, op1=ALU.add), reads=[r], writes=[r])
            kb.op("dve", lambda e: e.tensor_scalar(out=tmp[0:P, :], in0=tmp[0:P, :], scalar1=-PI, scalar2=PI, op0=ALU.max, op1=ALU.min), reads=[r], writes=[r])
            kb.op("act", lambda e: e.activation(out=out_t[0:P, :], in_=tmp[0:P, :], func=AF.Sin), reads=[r], writes=[r])

        sin_of(sinT, 0.0)
        kb.op("dve", lambda e: e.tensor_scalar(out=sinT[0:P, :], in0=sinT[0:P, :], scalar1=sgncol, scalar2=None, op0=ALU.mult), reads=[r], writes=[r])
        sin_of(cosT, PI / 2)
        kb.dma("sp", cos_d, cosT[:, :], sem, reads=[r])
        kb.dma("sp", sin_d, sinT[:, :], sem, reads=[r])
        kb.barrier()
        kb.put_dsem(sem)


def attn_seq(cx, tiles, scale, o_ps, o_r, d_ps, d_r, pring, N=TT, ahead=2):
    kb = cx.kb
    n = len(tiles)

    def qk(i):
        tl = tiles[i]
        ps, ps_r, _ = cx.psA.next()
        m = len(tl["q"])
        for j in range(m):
            q_ap, q_r = tl["q"][j]
            k_ap, k_r = tl["k"][j]
            kb.op("pe", lambda e: e.matmul(ps[:, 0:N], lhsT=k_ap, rhs=q_ap, start=(j == 0), stop=(j == m - 1)),
                  reads=[q_r, k_r], writes=[ps_r], inc=(j == m - 1))
        p, p_r, _ = pring.next()
        kb.op("act", lambda e: e.activation(out=p[:, 0:N], in_=ps[:, 0:N], func=AF.Exp, scale=scale), reads=[ps_r], writes=[p_r])
        if tl.get("mask") is not None:
            m_ap, m_r = tl["mask"]
            kb.op("dve", lambda e: e.tensor_tensor(out=p[:, 0:N], in0=p[:, 0:N], in1=m_ap, op=ALU.mult), reads=[p_r, m_r], writes=[p_r])
        return p, p_r

    def pv(i, p, p_r):
        tl = tiles[i]
        v_ap, v_r = tl["v"]
        kb.op("pe", lambda e: e.matmul(o_ps[:, 0:N], lhsT=v_ap, rhs=p[:, 0:N], start=(i == 0), stop=(i == n - 1)), reads=[p_r, v_r], writes=[o_r])
        kb.op("pe", lambda e: e.matmul(d_ps[:, 0:N], lhsT=tl["ones"], rhs=p[:, 0:N], start=(i == 0), stop=(i == n - 1)), reads=[p_r], writes=[d_r])

    q = []
    for i in range(min(ahead, n)):
        q.append(qk(i))
    for i in range(n):
        if i + ahead < n:
            q.append(qk(i + ahead))
        p, p_r = q[i]
        pv(i, p, p_r)


def mla_stage(cx, L, a, hin, hout, W, scr):
    kb, nc = cx.kb, cx.nc
    V = cx.vcol
    hin_v = hin.rearrange("(c p) n -> p c n", p=128)
    w_in_v = W["mla_w_in"][a].rearrange("(c p) f -> p c f", p=128)
    w_uq_v = W["mla_w_uq"][a].rearrange("(c p) f -> p c f", p=128)
    w_ukv_v = W["mla_w_ukv"][a].rearrange("(c p) f -> p c f", p=128)
    cqn_d, ckvn_d, kr_d, oT_d = scr["cqn"], scr["ckvn"], scr["kr"], scr["oT"]
    cqn_dv = cqn_d.rearrange("(c p) n -> p c n", p=128)
    ckvn_dv = ckvn_d.rearrange("(c p) n -> p c n", p=128)
    oT_dv = oT_d.rearrange("(c p) n -> p c n", p=128)
    with ExitStack() as es:
        uT = sb(es, nc, "m1_uT", [128, KC, TT], BF16); uT_r = Res()
        xring = Ring(kb, [sb(es, nc, f"m1_x{i}", [128, 8, TT], F32) for i in range(4)], dma=True)
        sqring = Ring(kb, [sb(es, nc, f"m1_sq{i}", [128, TT], BF16) for i in range(3)])
        rstd = sb(es, nc, "m1_rstd", [128, TT], F32); rstd_r = Res()
        wring = Ring(kb, [sb(es, nc, f"m1_w{i}", [128, KC, 128], BF16) for i in range(3)], dma=True)
        cbuf = sb(es, nc, "m1_c", [128, 13, TT], F32); c_rs = [Res() for _ in range(13)]
        cn = sb(es, nc, "m1_cn", [128, 12, TT], BF16); cn_r = Res()
        cnsem = kb.get_dsem()
        cols = [(c * 128, 128) for c in range(12)] + [(1536, 64)]
        for t in range(NTT):
            t0 = t * TT
            norm_tile(cx, es, hin_v, V["mix_norm"][L], t0, TT, uT, uT_r, 0, xring, sqring, rstd, rstd_r)

            def evac(ci, ti, ps, ps_r, M, N):
                kb.op("act", lambda e: e.activation(out=cbuf[0:M, ci, :], in_=ps[0:M, :], func=AF.Copy), reads=[ps_r], writes=[c_rs[ci]])

            proj_fm(cx, w_in_v, cols, uT, uT_r, KC, [(0, TT)], evac, wring)
            rms_rstd(cx, es, [(cbuf[:, c, :], c_rs[c], 128) for c in range(8)], MLA_QL, TT, rstd, rstd_r, sqring)
            for c in range(8):
                kb.op("dve", lambda e: e.scalar_tensor_tensor(out=cn[:, c, :], in0=cbuf[:, c, :], scalar=cx.vecs[:, V["mla_g_cq"][a] + c:V["mla_g_cq"][a] + c + 1],
                      in1=rstd[:, :], op0=ALU.mult, op1=ALU.mult), reads=[c_rs[c], rstd_r], writes=[cn_r])
            rms_rstd(cx, es, [(cbuf[:, 8 + c, :], c_rs[8 + c], 128) for c in range(4)], MLA_KVL, TT, rstd, rstd_r, sqring)
            for c in range(4):
                kb.op("dve", lambda e: e.scalar_tensor_tensor(out=cn[:, 8 + c, :], in0=cbuf[:, 8 + c, :], scalar=cx.vecs[:, V["mla_g_ckv"][a] + c:V["mla_g_ckv"][a] + c + 1],
                      in1=rstd[:, :], op0=ALU.mult, op1=ALU.mult), reads=[c_rs[8 + c], rstd_r], writes=[cn_r])
            kb.dma("sp", cqn_dv[:, :, t0:t0 + TT], cn[:, 0:8, :], cnsem, reads=[cn_r])
            kb.dma("sp", ckvn_dv[:, :, t0:t0 + TT], cn[:, 8:12, :], cnsem, reads=[cn_r])
            kb.dma("sp", kr_d[:, t0:t0 + TT], cbuf[0:64, 12, :], cnsem, reads=[c_rs[12]])
        kb.barrier()
        xring.release(); wring.release(); kb.put_dsem(cnsem)
    with ExitStack() as es:
        load_consts(cx, es, ["cosA", "sinA", "maskC"])
        cqn = sb(es, nc, "m2_cqn", [128, 8, S], BF16)
        ckvn = sb(es, nc, "m2_ckvn", [128, 4, S], BF16)
        kr = sb(es, nc, "m2_kr", [128, S], F32)
        krg = sb(es, nc, "m2_krg", [128, S], BF16)
        ssq_kr = sb(es, nc, "m2_ssqkr", [128, S], F32)
        ld_r = Res()
        sem = kb.get_dsem()
        kb.dma("sp", cqn[:, :, :], cqn_dv, sem, writes=[ld_r])
        kb.dma("sp", ckvn[:, :, :], ckvn_dv, sem, writes=[ld_r])
        kb.op("dve", lambda e: e.memset(kr[:, :], 0.0), writes=[ld_r])
        kb.dma("sp", kr[0:64, :], kr_d, sem, writes=[ld_r])
        sqring = Ring(kb, [sb(es, nc, f"m2_sq{i}", [128, TT], BF16) for i in range(3)])
        tmpring = Ring(kb, [sb(es, nc, f"m2_tmp{i}", [128, TT], F32) for i in range(4)])
        tbring = Ring(kb, [sb(es, nc, f"m2_tb{i}", [128, TT], BF16) for i in range(3)])
        gk_r = cx.vecs[:, V["mla_g_k_r"][a]:V["mla_g_k_r"][a] + 1]
        gk_n = cx.vecs[:, V["mla_g_k_n"][a]:V["mla_g_k_n"][a] + 1]
        gq_r = cx.vecs[:, V["mla_g_q_r"][a]:V["mla_g_q_r"][a] + 1]
        gq_n = cx.vecs[:, V["mla_g_q_n"][a]:V["mla_g_q_n"][a] + 1]
        for t in range(NTT):
            tsl = slice(t * TT, (t + 1) * TT)
            s, s_r, _ = sqring.next()
            kb.op("act", lambda e: e.activation(out=s[:, :], in_=kr[:, tsl], func=AF.Square), reads=[ld_r], writes=[s_r])
            ps, ps_r, _ = cx.psA.next()
            kb.op("pe", lambda e: e.matmul(ps[:, :], lhsT=cx.ones[:, :], rhs=s[:, :], start=True, stop=True), reads=[s_r], writes=[ps_r])
            kb.op("act", lambda e: e.activation(out=ssq_kr[:, tsl], in_=ps[:, :], func=AF.Copy), reads=[ps_r], writes=[ld_r])
            tb, tb_r, _ = tbring.next()
            kb.op("dve", lambda e: e.tensor_scalar(out=tb[:, :], in0=kr[:, tsl], scalar1=gk_r, scalar2=None, op0=ALU.mult), reads=[ld_r], writes=[tb_r])
            rope_apply(cx, tb[:, :], tb_r, 128, TT, t * TT, cx.perm64, cx.cosA, cx.sinA, krg[:, tsl], ld_r, tmpring)
        kb.barrier()
        kTb = [sb(es, nc, f"m2_kT{i}", [128, S], BF16) for i in range(2)]; kT_rs = [Res(), Res()]
        krTb = [sb(es, nc, f"m2_krT{i}", [128, S], BF16) for i in range(2)]; krT_rs = [Res(), Res()]
        vtmb = [sb(es, nc, f"m2_v{i}", [128, 16, 128], BF16) for i in range(2)]; v_rs = [Res(), Res()]
        qTb = [sb(es, nc, f"m2_qT{i}", [128, S], BF16) for i in range(2)]; qT_rs = [Res(), Res()]
        qrTb = [sb(es, nc, f"m2_qrT{i}", [128, S], BF16) for i in range(2)]; qrT_rs = [Res(), Res()]
        oT = sb(es, nc, "m2_oT", [128, S], BF16); oT_r = Res()
        rstdring = Ring(kb, [sb(es, nc, f"m2_rstd{i}", [128, TT], F32) for i in range(3)])
        rden = sb(es, nc, "m2_rden", [128, TT], F32); rden_r = Res()
        rawK = Ring(kb, [sb(es, nc, f"m2_rawK{i}", [128, TT], F32) for i in range(2)])
        rawQ0 = Ring(kb, [sb(es, nc, f"m2_rawQ0{i}", [128, TT], F32) for i in range(2)])
        rawQ1 = Ring(kb, [sb(es, nc, f"m2_rawQ1{i}", [128, TT], F32) for i in range(2)])
        wk = Ring(kb, [sb(es, nc, f"m2_wk{i}", [128, 4, 256], BF16) for i in range(2)], dma=True)
        wq = Ring(kb, [sb(es, nc, f"m2_wq{i}", [128, 8, 256], BF16) for i in range(2)], dma=True)
        for i_ in range(2):
            kb.op("dve", lambda e: e.memset(wq.tiles[i_][:, :, :], 0.0), writes=[wq.res[i_]])
        pring = Ring(kb, [sb(es, nc, f"m2_p{i}", [128, TT], BF16) for i in range(3)])
        osem = kb.get_dsem()
        scale = 192.0 ** -0.5
        hw = {}

        def load_w(h):
            wkv, wkv_r, wkv_s = wk.next()
            kb.dma("pool", wkv[:, :, :], w_ukv_v[:, :, h * 256:(h + 1) * 256], wkv_s, writes=[wkv_r])
            wqh, wqh_r, wqh_s = wq.next()
            kb.dma("pool", wqh[:, :, 0:192], w_uq_v[:, :, h * 192:(h + 1) * 192], wqh_s, writes=[wqh_r])
            hw[h] = (wkv, wkv_r, wqh, wqh_r)

        def proj_main(h, t):
            wkv, wkv_r, wqh, wqh_r = hw[h]
            hb = h % 2
            tsl = slice(t * TT, (t + 1) * TT)
            ps, ps_r, _ = cx.psA.next()
            for kc in range(4):
                kb.op("pe", lambda e: e.matmul(ps[:, :], lhsT=wkv[:, kc, 0:128], rhs=ckvn[:, kc, tsl], start=(kc == 0), stop=(kc == 3)),
                      reads=[wkv_r], writes=[ps_r], inc=(kc == 3))
            rk, rk_r, _ = rawK.next()
            kb.op("act", lambda e: e.activation(out=rk[:, :], in_=ps[:, :], func=AF.Copy), reads=[ps_r], writes=[rk_r])
            psq, psq_r, _ = cx.psA.next()
            for kc in range(8):
                kb.op("pe", lambda e: e.matmul(psq[:, :], lhsT=wqh[:, kc, 0:128], rhs=cqn[:, kc, tsl], start=(kc == 0), stop=(kc == 7)),
                      reads=[wqh_r], writes=[psq_r], inc=(kc == 7))
            q0, q0_r, _ = rawQ0.next()
            kb.op("act", lambda e: e.activation(out=q0[:, :], in_=psq[:, :], func=AF.Copy), reads=[psq_r], writes=[q0_r])
            psr_, psr_r, _ = cx.psA.next()
            for kc in range(8):
                kb.op("pe", lambda e: e.matmul(psr_[:, :], lhsT=wqh[:, kc, 128:256], rhs=cqn[:, kc, tsl], start=(kc == 0), stop=(kc == 7)),
                      reads=[wqh_r], writes=[psr_r], inc=(kc == 7))
            q1, q1_r, _ = rawQ1.next()
            kb.op("act", lambda e: e.activation(out=q1[:, :], in_=psr_[:, :], func=AF.Copy), reads=[psr_r], writes=[q1_r])
            psv, psv_r, _ = cx.psA.next()
            for tt in range(4):
                k0 = t * TT + tt * 128
                for kc in range(4):
                    kb.op("pe", lambda e: e.matmul(psv[:, tt * 128:(tt + 1) * 128], lhsT=ckvn[:, kc, k0:k0 + 128], rhs=wkv[:, kc, 128:256],
                          start=(kc == 0), stop=(kc == 3)), reads=[wkv_r], writes=[psv_r], inc=(tt == 3 and kc == 3))
            kb.op("act", lambda e: e.activation(out=vtmb[hb][:, t * 4:(t + 1) * 4, :], in_=psv[:, :], func=AF.Copy), reads=[psv_r], writes=[v_rs[hb]])
            sk, sk_r, _ = sqring.next()
            kb.op("act", lambda e: e.activation(out=sk[:, :], in_=rk[:, :], func=AF.Square), reads=[rk_r], writes=[sk_r])
            s0, s0_r, _ = sqring.next()
            kb.op("act", lambda e: e.activation(out=s0[:, :], in_=q0[:, :], func=AF.Square), reads=[q0_r], writes=[s0_r])
            s1, s1_r, _ = sqring.next()
            kb.op("act", lambda e: e.activation(out=s1[:, :], in_=q1[:, :], func=AF.Square), reads=[q1_r], writes=[s1_r])
            return dict(h=h, t=t, rk=(rk, rk_r), q0=(q0, q0_r), q1=(q1, q1_r), sk=(sk, sk_r), s0=(s0, s0_r), s1=(s1, s1_r))

        def norm_a(st):
            h, t = st["h"], st["t"]
            hb = h % 2
            tsl = slice(t * TT, (t + 1) * TT)
            rk, rk_r = st["rk"]; q0, q0_r = st["q0"]; q1, q1_r = st["q1"]
            sk, sk_r = st["sk"]; s0, s0_r = st["s0"]; s1, s1_r = st["s1"]
            ps2, ps2_r, _ = cx.psA.next()
            kb.op("pe", lambda e: e.matmul(ps2[:, :], lhsT=cx.ones[:, :], rhs=sk[:, :], start=True, stop=True), reads=[sk_r], writes=[ps2_r])
            ps3, ps3_r, _ = cx.psA.next()
            kb.op("pe", lambda e: e.matmul(ps3[:, :], lhsT=cx.ones[:, :], rhs=s0[:, :], start=True, stop=False), reads=[s0_r], writes=[ps3_r], inc=False)
            kb.op("pe", lambda e: e.matmul(ps3[:, :], lhsT=cx.ones[:, :], rhs=s1[:, :], start=False, stop=True), reads=[s1_r], writes=[ps3_r])
            rsk, rsk_r, _ = rstdring.next()
            kb.op("dve", lambda e: e.tensor_tensor(out=rsk[:, :], in0=ps2[:, :], in1=ssq_kr[:, tsl], op=ALU.add), reads=[ps2_r], writes=[rsk_r])
            kb.op("act", lambda e: e.activation(out=rsk[:, :], in_=rsk[:, :], func=AF.Ln, scale=1.0 / 192, bias=cx.eps_col[:, 0:1]), reads=[rsk_r], writes=[rsk_r])
            kb.op("act", lambda e: e.activation(out=rsk[:, :], in_=rsk[:, :], func=AF.Exp, scale=-0.5), reads=[rsk_r], writes=[rsk_r])
            rsq, rsq_r, _ = rstdring.next()
            kb.op("act", lambda e: e.activation(out=rsq[:, :], in_=ps3[:, :], func=AF.Ln, scale=1.0 / 192, bias=cx.eps_col[:, 0:1]), reads=[ps3_r], writes=[rsq_r])
            kb.op("act", lambda e: e.activation(out=rsq[:, :], in_=rsq[:, :], func=AF.Exp, scale=-0.5), reads=[rsq_r], writes=[rsq_r])
            kb.op("dve", lambda e: e.scalar_tensor_tensor(out=kTb[hb][:, tsl], in0=rk[:, :], scalar=gk_n, in1=rsk[:, :], op0=ALU.mult, op1=ALU.mult),
                  reads=[rk_r, rsk_r], writes=[kT_rs[hb]])
            kb.op("pool", lambda e: e.tensor_tensor(out=krTb[hb][:, tsl], in0=krg[:, tsl], in1=rsk[:, :], op=ALU.mult), reads=[rsk_r], writes=[krT_rs[hb]])
            kb.op("dve", lambda e: e.scalar_tensor_tensor(out=qTb[hb][:, tsl], in0=q0[:, :], scalar=gq_n, in1=rsq[:, :], op0=ALU.mult, op1=ALU.mult),
                  reads=[q0_r, rsq_r], writes=[qT_rs[hb]])
            tb, tb_r, _ = tbring.next()
            kb.op("dve", lambda e: e.scalar_tensor_tensor(out=tb[:, :], in0=q1[:, :], scalar=gq_r, in1=rsq[:, :], op0=ALU.mult, op1=ALU.mult),
                  reads=[q1_r, rsq_r], writes=[tb_r])
            st["tb"] = (tb, tb_r)

        def norm_b(st):
            h, t = st["h"], st["t"]
            hb = h % 2
            tb, tb_r = st["tb"]
            rope_apply(cx, tb[:, :], tb_r, 128, TT, t * TT, cx.perm64, cx.cosA, cx.sinA, qrTb[hb][:, t * TT:(t + 1) * TT], qrT_rs[hb], tmpring)

        def attn(h, t):
            hb = h % 2
            tsl = slice(t * TT, (t + 1) * TT)
            o_ps, o_r, _ = cx.psB.next()
            d_ps, d_r, _ = cx.psB.next()
            nk = 4 * t + 4
            tiles = []
            for kt in range(nk):
                ksl = slice(kt * 128, (kt + 1) * 128)
                diag = kt >= 4 * t
                tiles.append(dict(q=[(qTb[hb][:, tsl], qT_rs[hb]), (qrTb[hb][:, tsl], qrT_rs[hb])], k=[(kTb[hb][:, ksl], kT_rs[hb]), (krTb[hb][:, ksl], krT_rs[hb])],
                                  v=(vtmb[hb][:, kt, :], v_rs[hb]), ones=cx.ones[:, :], mask=(cx.maskC[:, kt - 4 * t, :], None) if diag else None))
            attn_seq(cx, tiles, scale, o_ps, o_r, d_ps, d_r, pring)
            act_recip(cx, rden[:, :], d_ps[:, :], d_r, rden_r)
            kb.op("dve", lambda e: e.tensor_tensor(out=oT[:, tsl], in0=o_ps[:, :], in1=rden[:, :], op=ALU.mult), reads=[o_r, rden_r], writes=[oT_r])

        load_w(0)
        pend_b = None
        for t in range(NTT):
            st = proj_main(0, t)
            norm_a(st)
            norm_b(st)
        for h in range(H):
            if h + 1 < H:
                load_w(h + 1)
            for t in range(NTT):
                st = proj_main(h + 1, t) if h + 1 < H else None
                attn(h, t)
                if pend_b is not None:
                    norm_b(pend_b)
                    pend_b = None
                if st is not None:
                    norm_a(st)
                    pend_b = st
            if pend_b is not None:
                norm_b(pend_b)
                pend_b = None
            kb.dma("sp", oT_dv[:, h, :], oT[:, :], osem, reads=[oT_r])
        kb.barrier()
        wk.release(); wq.release(); kb.put_dsem(osem); kb.put_dsem(sem)
    out_proj_stage(cx, oT_d, W["mla_w_o"][a], hin, hout)


def shared_kv_stage(cx, hin, W, scr):
    kb, nc = cx.kb, cx.nc
    V = cx.vcol
    hin_v = hin.rearrange("(c p) n -> p c n", p=128)
    kvw_v = W["kv_w"].rearrange("(c p) f -> p c f", p=128)
    craw_d = scr["craw"]
    kT_d = scr["nkT"]
    v_d = scr["nv"]
    with ExitStack() as es:
        load_consts(cx, es, ["cosB", "sinB"])
        uT = sb(es, nc, "kv_uT", [128, KC, 1024], BF16); uT_r = Res()
        xring = Ring(kb, [sb(es, nc, f"kv_x{i}", [128, 8, 256], F32) for i in range(4)], dma=True)
        sqring = Ring(kb, [sb(es, nc, f"kv_sq{i}", [128, TT], BF16) for i in range(3)])
        rstd = sb(es, nc, "kv_rstd", [128, TT], F32); rstd_r = Res()
        wring = Ring(kb, [sb(es, nc, f"kv_w{i}", [128, KC, 128], BF16) for i in range(3)], dma=True)
        wvring = Ring(kb, [sb(es, nc, f"kv_wv{i}", [128, KC, 512], BF16) for i in range(1)], dma=True)
        raw = sb(es, nc, "kv_raw", [128, TT], F32); raw_r = Res()
        tmpring = Ring(kb, [sb(es, nc, f"kv_tmp{i}", [128, TT], F32) for i in range(4)])
        tbring = Ring(kb, [sb(es, nc, f"kv_tb{i}", [128, TT], BF16) for i in range(3)])
        obring = Ring(kb, [sb(es, nc, f"kv_ob{i}", [128, TT], BF16) for i in range(3)], dma=True)
        vbring = Ring(kb, [sb(es, nc, f"kv_vb{i}", [128, 512], BF16) for i in range(3)], dma=True)
        for hf in range(2):
            for q4 in range(4):
                norm_tile(cx, es, hin_v, V["kv_norm"], hf * 1024 + q4 * 256, 256, uT, uT_r, q4 * 256, xring, sqring, rstd, rstd_r)

            def evac(ci, ti, ps, ps_r, M, N):
                t0 = hf * 1024 + ti * TT
                part, g = fm_parts[ci]
                if part in (0, 1):
                    o, o_r, o_s = obring.next()
                    kb.op("act", lambda e: e.activation(out=o[:, :], in_=ps[:, :], func=AF.Copy), reads=[ps_r], writes=[o_r])
                    kb.dma("sp", craw_d[part, g, :, t0:t0 + TT], o[:, :], o_s, reads=[o_r])
                else:
                    gcol = V["g_k_slc"] if part == 2 else V["g_k_win"]
                    kb.op("act", lambda e: e.activation(out=raw[:, :], in_=ps[:, :], func=AF.Copy), reads=[ps_r], writes=[raw_r])
                    rms_rstd(cx, es, [(raw[:, :], raw_r, 128)], 128, TT, rstd, rstd_r, sqring)
                    tb, tb_r, _ = tbring.next()
                    kb.op("dve", lambda e: e.scalar_tensor_tensor(out=tb[:, :], in0=raw[:, :], scalar=cx.vecs[:, gcol:gcol + 1], in1=rstd[:, :],
                          op0=ALU.mult, op1=ALU.mult), reads=[raw_r, rstd_r], writes=[tb_r])
                    o, o_r, o_s = obring.next()
                    rope_apply(cx, tb[:, :], tb_r, 128, TT, t0, cx.perm128, cx.cosB, cx.sinB, o[:, :], o_r, tmpring)
                    kb.dma("sp", kT_d[0 if part == 2 else 1, g, :, t0:t0 + TT], o[:, :], o_s, reads=[o_r])

            fm_parts = [(p, g) for p in (0, 1, 2, 4) for g in range(4)]
            proj_fm(cx, kvw_v, [(p * 512 + g * 128, 128) for (p, g) in fm_parts], uT, uT_r, KC, [(0, TT), (TT, TT)], evac, wring)
            t0 = hf * 1024
            for pi_, part in enumerate((3, 5)):
                w, w_r, w_s = wvring.next()
                kb.dma("pool", w[:, :, :], kvw_v[:, :, part * 512:(part + 1) * 512], w_s, writes=[w_r])
                for tt in range(8):
                    ps, ps_r, _ = cx.psA.next()
                    for kc in range(KC):
                        kb.op("pe", lambda e: e.matmul(ps[:, :], lhsT=uT[:, kc, tt * 128:(tt + 1) * 128], rhs=w[:, kc, :], start=(kc == 0), stop=(kc == KC - 1)),
                              reads=[w_r, uT_r], writes=[ps_r], inc=(kc == KC - 1))
                    o, o_r, o_s = vbring.next()
                    kb.op("act", lambda e: e.activation(out=o[:, :], in_=ps[:, :], func=AF.Copy), reads=[ps_r], writes=[o_r])
                    for g in range(4):
                        kb.dma("sp", v_d[pi_, g, t0 + tt * 128:t0 + (tt + 1) * 128, :], o[:, g * 128:(g + 1) * 128], o_s, reads=[o_r])
        kb.barrier()
        xring.release(); wring.release(); wvring.release(); obring.release(); vbring.release()


def compress_stage(cx, W, scr):
    kb, nc = cx.kb, cx.nc
    V = cx.vcol
    craw_d = scr["craw"]
    kcmp_d = scr["kcmp"]
    vcmp_d = scr["vcmp"]
    with ExitStack() as es:
        w1 = sb(es, nc, "cp_w1", [128, 32, 256], BF16); w1_r = Res()
        w2 = sb(es, nc, "cp_w2", [128, 2, 128], BF16); w2_r = Res()
        tT = sb(es, nc, "cp_tT", [128, S], BF16); tT_r = Res()
        hid = sb(es, nc, "cp_hid", [128, 2, 128], BF16); hid_r = Res()
        bias = sb(es, nc, "cp_bias", [128, 2], F32); bias_r = Res()
        raw = sb(es, nc, "cp_raw", [128, 128], F32); raw_r = Res()
        rstd = sb(es, nc, "cp_rstd", [128, TT], F32); rstd_r = Res()
        sqring = Ring(kb, [sb(es, nc, f"cp_sq{i}", [128, TT], BF16) for i in range(2)])
        ob = sb(es, nc, "cp_ob", [128, 128], BF16); ob_r = Res()
        s1 = kb.get_dsem(); s2 = kb.get_dsem(); s3 = kb.get_dsem(); s4 = kb.get_dsem()
        for kv in range(2):
            w1_d = W["cmp_k_w1"] if kv == 0 else W["cmp_v_w1"]
            w2_d = W["cmp_k_w2"] if kv == 0 else W["cmp_v_w2"]
            posT = cx.posT_k if kv == 0 else cx.posT_v
            b1c = V["cmp_k_b1"] if kv == 0 else V["cmp_v_b1"]
            kb.dma("pool", w1[:, :, :], w1_d.rearrange("(l p) f -> p l f", p=128), s1, writes=[w1_r])
            kb.dma("pool", w2[:, :, :], w2_d.rearrange("(c p) f -> p c f", p=128), s2, writes=[w2_r])
            for hc in range(2):
                ps, ps_r, _ = cx.psA.next()
                for l in range(32):
                    kb.op("pe", lambda e: e.matmul(ps[:, 0:1], lhsT=w1[:, l, hc * 128:(hc + 1) * 128], rhs=posT[:, l:l + 1], start=(l == 0), stop=(l == 31)),
                          reads=[w1_r], writes=[ps_r], inc=(l == 31))
                kb.op("dve", lambda e: e.tensor_tensor(out=bias[:, hc:hc + 1], in0=ps[:, 0:1], in1=cx.vecs[:, b1c + hc:b1c + hc + 1], op=ALU.add),
                      reads=[ps_r], writes=[bias_r])
            for g in range(4):
                kb.dma("sp", tT[:, :], craw_d[kv, g, :, :], s3, writes=[tT_r])
                for hc in range(2):
                    ps, ps_r, _ = cx.psA.next()
                    for l in range(32):
                        kb.op("pe", lambda e: e.matmul(ps[:, 0:N_CMP], lhsT=w1[:, l, hc * 128:(hc + 1) * 128], rhs=tT[:, l:l + 16 * (N_CMP - 1) + 1:16],
                              start=(l == 0), stop=(l == 31)), reads=[w1_r, tT_r], writes=[ps_r], inc=(l == 31))
                    kb.op("act", lambda e: e.activation(out=hid[:, hc, 0:N_CMP], in_=ps[:, 0:N_CMP], func=AF.Silu, bias=bias[:, hc:hc + 1]),
                          reads=[ps_r, bias_r], writes=[hid_r])
                if kv == 0:
                    ps, ps_r, _ = cx.psA.next()
                    for hc in range(2):
                        kb.op("pe", lambda e: e.matmul(ps[:, 0:N_CMP], lhsT=w2[:, hc, :], rhs=hid[:, hc, 0:N_CMP], start=(hc == 0), stop=(hc == 1)),
                              reads=[w2_r, hid_r], writes=[ps_r])
                    kb.op("act", lambda e: e.activation(out=raw[:, 0:N_CMP], in_=ps[:, 0:N_CMP], func=AF.Copy), reads=[ps_r], writes=[raw_r])
                    rms_rstd(cx, es, [(raw[:, 0:N_CMP], raw_r, 128)], 128, N_CMP, rstd, rstd_r, sqring)
                    kb.op("dve", lambda e: e.scalar_tensor_tensor(out=ob[:, 0:N_CMP], in0=raw[:, 0:N_CMP], scalar=cx.vecs[:, V["g_k_cmp"]:V["g_k_cmp"] + 1],
                          in1=rstd[:, 0:N_CMP], op0=ALU.mult, op1=ALU.mult), reads=[raw_r, rstd_r], writes=[ob_r])
                    kb.dma("sp", kcmp_d[g, :, 0:N_CMP], ob[:, 0:N_CMP], s4, reads=[ob_r])
                else:
                    ps, ps_r, _ = cx.psA.next()
                    for hc in range(2):
                        kb.op("pe", lambda e: e.matmul(ps[0:N_CMP, 0:128], lhsT=hid[:, hc, 0:N_CMP], rhs=w2[:, hc, :], start=(hc == 0), stop=(hc == 1)),
                              reads=[w2_r, hid_r], writes=[ps_r])
                    kb.op("act", lambda e: e.activation(out=ob[0:N_CMP, :], in_=ps[0:N_CMP, 0:128], func=AF.Copy), reads=[ps_r], writes=[ob_r])
                    kb.dma("sp", vcmp_d[g, 0:N_CMP, :], ob[0:N_CMP, :], s4, reads=[ob_r])
        kb.barrier()
        for s in (s1, s2, s3, s4):
            kb.put_dsem(s)


def nsa_stage(cx, L, b, hin, hout, W, scr):
    kb, nc = cx.kb, cx.nc
    V = cx.vcol
    hin_v = hin.rearrange("(c p) n -> p c n", p=128)
    w_in_v = W["nsa_w_in"][b].rearrange("(c p) f -> p c f", p=128)
    qn_d, qr_d, gates_d, oT_d, ocmp_d = scr["qn"], scr["qr"], scr["gates"], scr["oT"], scr["ocmp"]
    oT_dv = oT_d.rearrange("(c p) n -> p c n", p=128)
    gq = cx.vecs[:, V["nsa_g_q"][b]:V["nsa_g_q"][b] + 1]
    scale = 128.0 ** -0.5
    with ExitStack() as es:
        load_consts(cx, es, ["cosB", "sinB"])
        uT = sb(es, nc, "n1_uT", [128, KC, 1024], BF16); uT_r = Res()
        xring = Ring(kb, [sb(es, nc, f"n1_x{i}", [128, 8, 256], F32) for i in range(4)], dma=True)
        sqring = Ring(kb, [sb(es, nc, f"n1_sq{i}", [128, TT], BF16) for i in range(3)])
        rstd = sb(es, nc, "n1_rstd", [128, TT], F32); rstd_r = Res()
        wring = Ring(kb, [sb(es, nc, f"n1_w{i}", [128, KC, 128], BF16) for i in range(3)], dma=True)
        raw = sb(es, nc, "n1_raw", [128, TT], F32); raw_r = Res()
        tmpring = Ring(kb, [sb(es, nc, f"n1_tmp{i}", [128, TT], F32) for i in range(4)])
        qnring = Ring(kb, [sb(es, nc, f"n1_qn{i}", [128, TT], BF16) for i in range(3)], dma=True)
        qrring = Ring(kb, [sb(es, nc, f"n1_qr{i}", [128, TT], BF16) for i in range(3)], dma=True)
        gring = Ring(kb, [sb(es, nc, f"n1_g{i}", [128, TT], F32) for i in range(2)], dma=True)
        cols = [(c * 128, 128) for c in range(32)] + [(4096, 96)]
        for hf in range(2):
            for q4 in range(4):
                norm_tile(cx, es, hin_v, V["mix_norm"][L], hf * 1024 + q4 * 256, 256, uT, uT_r, q4 * 256, xring, sqring, rstd, rstd_r)

            def evac(ci, ti, ps, ps_r, M, N):
                t0 = hf * 1024 + ti * TT
                if ci == 32:
                    o, o_r, o_s = gring.next()
                    kb.op("act", lambda e: e.activation(out=o[0:96, :], in_=ps[0:96, :], func=AF.Sigmoid, bias=cx.vecs[0:96, V["nsa_b_gate"][b]:V["nsa_b_gate"][b] + 1]),
                          reads=[ps_r], writes=[o_r])
                    kb.dma("sp", gates_d[:, t0:t0 + TT], o[0:96, :], o_s, reads=[o_r])
                    return
                kb.op("act", lambda e: e.activation(out=raw[:, :], in_=ps[:, :], func=AF.Copy), reads=[ps_r], writes=[raw_r])
                rms_rstd(cx, es, [(raw[:, :], raw_r, 128)], 128, TT, rstd, rstd_r, sqring)
                qn, qn_r, qn_s = qnring.next()
                kb.op("dve", lambda e: e.scalar_tensor_tensor(out=qn[:, :], in0=raw[:, :], scalar=gq, in1=rstd[:, :], op0=ALU.mult, op1=ALU.mult),
                      reads=[raw_r, rstd_r], writes=[qn_r])
                kb.dma("sp", qn_d[ci, :, t0:t0 + TT], qn[:, :], qn_s, reads=[qn_r])
                qr, qr_r, qr_s = qrring.next()
                rope_apply(cx, qn[:, :], qn_r, 128, TT, t0, cx.perm128, cx.cosB, cx.sinB, qr[:, :], qr_r, tmpring)
                kb.dma("sp", qr_d[ci, :, t0:t0 + TT], qr[:, :], qr_s, reads=[qr_r])

            proj_fm(cx, w_in_v, cols, uT, uT_r, KC, [(0, TT), (TT, TT)], evac, wring)
        kb.barrier()
        xring.release(); wring.release(); qnring.release(); qrring.release(); gring.release()
    with ExitStack() as es:
        load_consts(cx, es, ["maskC", "maskW", "cmask", "agg", "Eexp", "impA", "impB"])
        kcmp = sb(es, nc, "n2_kcmp", [128, 128], BF16)
        vcmp = sb(es, nc, "n2_vcmp", [128, 128], BF16)
        kslc = sb(es, nc, "n2_kslc", [128, S], BF16)
        kwin = sb(es, nc, "n2_kwin", [128, S], BF16)
        vslc = sb(es, nc, "n2_vslc", [128, 16, 128], BF16)
        vwin = sb(es, nc, "n2_vwin", [128, 16, 128], BF16)
        kv_r = Res()
        qn = sb(es, nc, "n2_qn", [128, S], BF16); q_r = Res()
        qr = sb(es, nc, "n2_qr", [128, S], BF16)
        psumh = sb(es, nc, "n2_psum", [128, S], F32); psumh_r = Res()
        e32 = Ring(kb, [sb(es, nc, f"n2_e{i}", [128, TT], F32) for i in range(2)])
        pring = Ring(kb, [sb(es, nc, f"n2_p{i}", [128, TT], BF16) for i in range(3)])
        rden = sb(es, nc, "n2_rden", [128, TT], F32); rden_r = Res()
        gbc = Ring(kb, [sb(es, nc, f"n2_gbc{i}", [128, S], F32) for i in range(3)], dma=True)
        ocring = Ring(kb, [sb(es, nc, f"n2_oc{i}", [128, TT], F32) for i in range(3)], dma=True)
        oacc = sb(es, nc, "n2_oacc", [128, TT], F32); oacc_r = Res()
        otmp = sb(es, nc, "n2_otmp", [128, TT], F32); otmp_r = Res()
        oT = sb(es, nc, "n2_oT", [128, S], BF16); oT_r = Res()
        imp = sb(es, nc, "n2_imp", [128, 32], F32); imp_r = Res()
        imp2 = sb(es, nc, "n2_imp2", [128, 32], F32); imp2_r = Res()
        m8 = sb(es, nc, "n2_m8", [128, 16], F32); m8_r = Res()
        sel = sb(es, nc, "n2_sel", [128, 32], F32); sel_r = Res()
        selT = sb(es, nc, "n2_selT", [128, S], BF16); selT_r = Res()
        kb.op("dve", lambda e: e.memset(selT[:, :], 0.0), writes=[selT_r])
        masks = sb(es, nc, "n2_masks", [128, 40, TT], BF16); mask_rs = [Res() for _ in range(40)]
        ksem = kb.get_dsem(); qsem = kb.get_dsem(); osem = kb.get_dsem()
        for g in range(4):
            kb.dma("sp", kcmp[:, :], scr["kcmp"][g], ksem, writes=[kv_r])
            kb.dma("sp", vcmp[:, :], scr["vcmp"][g], ksem, writes=[kv_r])
            kb.dma("sp", kslc[:, :], scr["nkT"][0, g], ksem, writes=[kv_r])
            kb.dma("sp", kwin[:, :], scr["nkT"][1, g], ksem, writes=[kv_r])
            kb.dma("sp", vslc[:, :, :], scr["nv"][0, g].rearrange("(t p) d -> p t d", p=128), ksem, writes=[kv_r])
            kb.dma("sp", vwin[:, :, :], scr["nv"][1, g].rearrange("(t p) d -> p t d", p=128), ksem, writes=[kv_r])
            for j in range(8):
                hh = g * 8 + j
                kb.dma("sp", qn[:, :], qn_d[hh], qsem, writes=[q_r])
                gb, gb_r, gb_s = gbc.next()
                kb.dma("sp", gb[:, :], bass.AP(gates_d.tensor, (hh * 3 + 0) * S, [[0, 128], [1, S]]), gb_s, writes=[gb_r])
                for t in range(NTT):
                    tsl = slice(t * TT, (t + 1) * TT)
                    ps, ps_r, _ = cx.psA.next()
                    kb.op("pe", lambda e: e.matmul(ps[0:N_CMP, :], lhsT=kcmp[:, 0:N_CMP], rhs=qn[:, tsl], start=True, stop=True), reads=[kv_r, q_r], writes=[ps_r])
                    ef, ef_r, _ = e32.next()
                    kb.op("act", lambda e: e.activation(out=ef[0:N_CMP, :], in_=ps[0:N_CMP, :], func=AF.Exp, scale=scale), reads=[ps_r], writes=[ef_r])
                    kb.op("dve", lambda e: e.tensor_tensor(out=ef[0:N_CMP, :], in0=ef[0:N_CMP, :], in1=cx.cmask[0:N_CMP, tsl], op=ALU.mult), reads=[ef_r], writes=[ef_r])
                    p, p_r, _ = pring.next()
                    kb.op("pool", lambda e: e.tensor_copy(out=p[0:N_CMP, :], in_=ef[0:N_CMP, :]), reads=[ef_r], writes=[p_r])
                    d_ps, d_r, _ = cx.psB.next()
                    kb.op("pe", lambda e: e.matmul(d_ps[:, :], lhsT=cx.ones[0:N_CMP, :], rhs=p[0:N_CMP, :], start=True, stop=True), reads=[p_r], writes=[d_r])
                    o_ps, o_r, _ = cx.psB.next()
                    kb.op("pe", lambda e: e.matmul(o_ps[:, :], lhsT=vcmp[0:N_CMP, :], rhs=p[0:N_CMP, :], start=True, stop=True), reads=[p_r, kv_r], writes=[o_r])
                    kb.op("dve", lambda e: e.tensor_scalar(out=rden[:, :], in0=d_ps[:, :], scalar1=1e-18, scalar2=None, op0=ALU.max), reads=[d_r], writes=[rden_r])
                    act_recip(cx, rden[:, :], rden[:, :], rden_r, rden_r)
                    if j == 0:
                        kb.op("dve", lambda e: e.tensor_tensor(out=psumh[0:N_CMP, tsl], in0=ef[0:N_CMP, :], in1=rden[0:N_CMP, :], op=ALU.mult),
                              reads=[ef_r, rden_r], writes=[psumh_r])
                    else:
                        kb.op("dve", lambda e: e.tensor_tensor(out=ef[0:N_CMP, :], in0=ef[0:N_CMP, :], in1=rden[0:N_CMP, :], op=ALU.mult),
                              reads=[ef_r, rden_r], writes=[ef_r])
                        kb.op("pool", lambda e: e.tensor_tensor(out=psumh[0:N_CMP, tsl], in0=psumh[0:N_CMP, tsl], in1=ef[0:N_CMP, :], op=ALU.add),
                              reads=[ef_r, psumh_r], writes=[psumh_r])
                    oc, oc_r, oc_s = ocring.next()
                    kb.op("dve", lambda e: e.tensor_tensor(out=oc[:, :], in0=o_ps[:, :], in1=rden[:, :], op=ALU.mult), reads=[o_r, rden_r], writes=[oc_r])
                    kb.op("dve", lambda e: e.tensor_tensor(out=oc[:, :], in0=oc[:, :], in1=gb[:, tsl], op=ALU.mult), reads=[oc_r, gb_r], writes=[oc_r])
                    kb.dma("sp", ocmp_d[hh, :, tsl], oc[:, :], oc_s, reads=[oc_r])
            for st in range(16):
                ssl = slice(st * 128, (st + 1) * 128)
                ps, ps_r, _ = cx.psA.next()
                kb.op("pe", lambda e: e.matmul(ps[:, 0:32], lhsT=psumh[0:N_CMP, ssl], rhs=cx.agg[0:N_CMP, :], start=True, stop=True), reads=[psumh_r], writes=[ps_r])
                kb.op("dve", lambda e: e.tensor_tensor(out=imp[:, :], in0=ps[:, 0:32], in1=cx.impA[:, st, :], op=ALU.mult), reads=[ps_r], writes=[imp_r])
                kb.op("dve", lambda e: e.tensor_tensor(out=imp[:, :], in0=imp[:, :], in1=cx.impB[:, st, :], op=ALU.add), reads=[imp_r], writes=[imp_r])
                kb.op("dve", lambda e: e.max(out=m8[:, 0:8], in_=imp[:, :]), reads=[imp_r], writes=[m8_r])
                kb.op("dve", lambda e: e.match_replace(out=imp2[:, :], in_to_replace=m8[:, 0:8], in_values=imp[:, :], imm_value=-1e30), reads=[imp_r, m8_r], writes=[imp2_r])
                kb.op("dve", lambda e: e.max(out=m8[:, 8:16], in_=imp2[:, :]), reads=[imp2_r], writes=[m8_r])
                kb.op("dve", lambda e: e.tensor_scalar(out=sel[:, :], in0=imp[:, :], scalar1=m8[:, 15:16], scalar2=None, op0=ALU.is_ge), reads=[imp_r, m8_r], writes=[sel_r])
                pt, pt_r, _ = cx.psA.next()
                kb.op("pe", lambda e: e.transpose(out=pt[0:32, 0:128], in_=sel[:, :], identity=cx.ident[:, :]), reads=[sel_r], writes=[pt_r])
                kb.op("act", lambda e: e.activation(out=selT[0:32, ssl], in_=pt[0:32, 0:128], func=AF.Copy), reads=[pt_r], writes=[selT_r])
            midx = {}
            i = 0
            for t in range(NTT):
                tsl = slice(t * TT, (t + 1) * TT)
                for kt in range(4 * t + 4):
                    ps, ps_r, _ = cx.psA.next()
                    kb.op("pe", lambda e: e.matmul(ps[:, :], lhsT=cx.Eexp[:, kt * 128:(kt + 1) * 128], rhs=selT[:, tsl], start=True, stop=True), reads=[selT_r], writes=[ps_r])
                    if kt >= 4 * t:
                        kb.op("dve", lambda e: e.tensor_tensor(out=masks[:, i, :], in0=ps[:, :], in1=cx.maskC[:, kt - 4 * t, :], op=ALU.mult), reads=[ps_r], writes=[mask_rs[i]])
                    else:
                        kb.op("act", lambda e: e.activation(out=masks[:, i, :], in_=ps[:, :], func=AF.Copy), reads=[ps_r], writes=[mask_rs[i]])
                    midx[(t, kt)] = i
                    i += 1
            for j in range(8):
                hh = g * 8 + j
                kb.dma("sp", qr[:, :], qr_d[hh], qsem, writes=[q_r])
                g1, g1_r, g1_s = gbc.next()
                kb.dma("sp", g1[:, :], bass.AP(gates_d.tensor, (hh * 3 + 1) * S, [[0, 128], [1, S]]), g1_s, writes=[g1_r])
                g2, g2_r, g2_s = gbc.next()
                kb.dma("sp", g2[:, :], bass.AP(gates_d.tensor, (hh * 3 + 2) * S, [[0, 128], [1, S]]), g2_s, writes=[g2_r])
                for t in range(NTT):
                    tsl = slice(t * TT, (t + 1) * TT)
                    oc, oc_r, oc_s = ocring.next()
                    kb.dma("sp", oc[:, :], ocmp_d[hh, :, tsl], oc_s, writes=[oc_r])
                    o_ps, o_r, _ = cx.psB.next()
                    d_ps, d_r, _ = cx.psB.next()
                    nk = 4 * t + 4
                    tiles = []
                    for kt in range(nk):
                        mi = midx[(t, kt)]
                        tiles.append(dict(q=[(qr[:, tsl], q_r)], k=[(kslc[:, kt * 128:(kt + 1) * 128], kv_r)], v=(vslc[:, kt, :], kv_r),
                                          ones=cx.ones[:, :], mask=(masks[:, mi, :], mask_rs[mi])))
                    attn_seq(cx, tiles, scale, o_ps, o_r, d_ps, d_r, pring)
                    act_recip(cx, rden[:, :], d_ps[:, :], d_r, rden_r)
                    kb.op("dve", lambda e: e.tensor_tensor(out=otmp[:, :], in0=o_ps[:, :], in1=rden[:, :], op=ALU.mult), reads=[o_r, rden_r], writes=[otmp_r])
                    kb.op("pool", lambda e: e.tensor_tensor(out=otmp[:, :], in0=otmp[:, :], in1=g1[:, tsl], op=ALU.mult), reads=[otmp_r, g1_r], writes=[otmp_r])
                    kb.op("pool", lambda e: e.tensor_tensor(out=oacc[:, :], in0=otmp[:, :], in1=oc[:, :], op=ALU.add), reads=[otmp_r, oc_r], writes=[oacc_r])
                    o_ps, o_r, _ = cx.psB.next()
                    d_ps, d_r, _ = cx.psB.next()
                    kts = [kt for kt in range(4 * t - 4, 4 * t + 4) if kt >= 0]
                    tiles = []
                    for kt in kts:
                        tiles.append(dict(q=[(qr[:, tsl], q_r)], k=[(kwin[:, kt * 128:(kt + 1) * 128], kv_r)], v=(vwin[:, kt, :], kv_r),
                                          ones=cx.ones[:, :], mask=(cx.maskW[:, kt - (4 * t - 4), :], None)))
                    attn_seq(cx, tiles, scale, o_ps, o_r, d_ps, d_r, pring)
                    act_recip(cx, rden[:, :], d_ps[:, :], d_r, rden_r)
                    kb.op("dve", lambda e: e.tensor_tensor(out=otmp[:, :], in0=o_ps[:, :], in1=rden[:, :], op=ALU.mult), reads=[o_r, rden_r], writes=[otmp_r])
                    kb.op("pool", lambda e: e.tensor_tensor(out=otmp[:, :], in0=otmp[:, :], in1=g2[:, tsl], op=ALU.mult), reads=[otmp_r, g2_r], writes=[otmp_r])
                    kb.op("pool", lambda e: e.tensor_tensor(out=oT[:, tsl], in0=otmp[:, :], in1=oacc[:, :], op=ALU.add), reads=[otmp_r, oacc_r], writes=[oT_r])
                kb.dma("sp", oT_dv[:, hh, :], oT[:, :], osem, reads=[oT_r])
        kb.barrier()
        gbc.release(); ocring.release()
        for s in (ksem, qsem, osem):
            kb.put_dsem(s)
    out_proj_stage(cx, oT_d, W["nsa_w_o"][b], hin, hout)


VEC_SPECS = None


def cols_of(v):
    v = np.asarray(v, np.float32)
    n = v.shape[0]
    if n <= 128:
        o = np.zeros((128, 1), np.float32)
        o[:n, 0] = v
        return o
    assert n % 128 == 0
    return np.ascontiguousarray(v.reshape(n // 128, 128).T)


def build_vecs(inp):
    cols = []
    vcol = {}
    pos = [0]

    def add(name, v, idx=None):
        c = cols_of(v)
        if idx is None:
            vcol[name] = pos[0]
        else:
            vcol.setdefault(name, {})[idx] = pos[0]
        cols.append(c)
        pos[0] += c.shape[1]

    for l in range(DEPTH):
        add("ffn1_norm", inp["ffn1_norm"][l], l)
        add("mix_norm", inp["mix_norm"][l], l)
        add("ffn2_norm", inp["ffn2_norm"][l], l)
    for a in range(2):
        add("mla_g_cq", inp["mla_g_cq"][a], a)
        add("mla_g_ckv", inp["mla_g_ckv"][a], a)
        add("mla_g_q_n", inp["mla_g_q"][a][:128], a)
        add("mla_g_q_r", inp["mla_g_q"][a][128:], a)
        add("mla_g_k_n", inp["mla_g_k"][a][:128], a)
        add("mla_g_k_r", inp["mla_g_k"][a][128:], a)
    add("kv_norm", inp["kv_norm"])
    add("cmp_k_b1", inp["cmp_k_b1"])
    add("cmp_v_b1", inp["cmp_v_b1"])
    add("g_k_cmp", inp["g_k_cmp"])
    add("g_k_slc", inp["g_k_slc"])
    add("g_k_win", inp["g_k_win"])
    for b in range(2):
        add("nsa_b_gate", inp["nsa_b_gate"][b], b)
        add("nsa_g_q", inp["nsa_g_q"][b], b)
    return np.ascontiguousarray(np.concatenate(cols, axis=1)), vcol


def build_consts():
    import ml_dtypes
    bf = ml_dtypes.bfloat16
    c = {}
    c["ones"] = np.ones((128, 128), bf)
    c["ident"] = np.eye(128, dtype=np.float32)
    p128 = np.zeros((128, 128), np.float32)
    for i in range(64):
        p128[i, i + 64] = 1; p128[i + 64, i] = 1
    p64 = np.zeros((128, 128), np.float32)
    for i in range(32):
        p64[i, i + 32] = 1; p64[i + 32, i] = 1
    c["perm128"] = p128.astype(bf); c["perm64"] = p64.astype(bf)
    misc = np.zeros((128, 8), np.float32)
    misc[:, 0] = EPS; misc[:, 1] = np.pi
    invA = (10000.0 ** (-(np.arange(0, 64, 2, dtype=np.float32)) / 64)).astype(np.float32)
    invB = (10000.0 ** (-(np.arange(0, 128, 2, dtype=np.float32)) / 128)).astype(np.float32)
    misc[:64, 2] = np.concatenate([invA, invA]); misc[:32, 3] = -1; misc[32:64, 3] = 1
    misc[:, 4] = np.concatenate([invB, invB]); misc[:64, 5] = -1; misc[64:, 5] = 1
    c["misc"] = misc
    k = np.arange(128)[:, None]; q = np.arange(512)[None, :]
    c["maskC"] = np.stack([((j * 128 + k) <= q) for j in range(4)], axis=1).astype(bf)
    mw = []
    for i in range(8):
        dlt = (i - 4) * 128
        diff = q - k - dlt
        mw.append((diff >= 0) & (diff < 512))
    c["maskW"] = np.stack(mw, axis=1).astype(bf)
    n = np.arange(128)[:, None]; s = np.arange(S)[None, :]
    c["cmask"] = (((16 * n + 31) <= s) & (n < N_CMP)).astype(np.float32)
    cs = np.arange(N_CMP)[:, None] * 16; ss = np.arange(32)[None, :] * 64
    ov = np.clip(np.minimum(cs + 32, ss + 64) - np.maximum(cs, ss), 0, None)
    agg = np.zeros((128, 32), np.float32); agg[:N_CMP] = ov / 16
    c["agg"] = agg
    E = np.zeros((128, S), np.float32)
    E[np.arange(S) // 64, np.arange(S)] = 1
    c["Eexp"] = E.astype(bf)
    spos = np.arange(S)[:, None]; jb = np.arange(32)[None, :]
    cur = spos // 64
    valid = (jb * 64) <= spos
    forced = (jb == 0) | (jb == cur) | (jb == cur - 1)
    A = (valid & ~forced).astype(np.float32)
    B = np.where(forced, 1e6, np.where(valid, 0.0, -1.0)).astype(np.float32)
    c["impA"] = np.ascontiguousarray(A.reshape(16, 128, 32).transpose(1, 0, 2))
    c["impB"] = np.ascontiguousarray(B.reshape(16, 128, 32).transpose(1, 0, 2))
    return c


CONST_DT = {"ones": BF16, "ident": F32, "perm128": BF16, "perm64": BF16, "misc": F32, "maskC": BF16, "maskW": BF16,
            "cmask": F32, "agg": F32, "Eexp": BF16, "impA": F32, "impB": F32}

WEIGHT_NAMES = ["ffn1_w_gate", "ffn1_w_up", "ffn1_w_down", "ffn2_w_gate", "ffn2_w_up", "ffn2_w_down",
                "mla_w_in", "mla_w_uq", "mla_w_ukv", "mla_w_o", "kv_w", "cmp_k_w1", "cmp_k_w2", "cmp_v_w1", "cmp_v_w2",
                "nsa_w_in", "nsa_w_o"]


def build_program(shapes, vcol, nvec, consts, stages=None, plan=None):
    nc = bass.Bass("TRN2", target_bir_lowering=False)
    xT = nc.dram_tensor("xT", [D, S], F32, kind="ExternalInput").ap()
    pos = nc.dram_tensor("pos", [1, S], I32, kind="ExternalInput").ap()
    vecs_d = nc.dram_tensor("vecs", [128, nvec], F32, kind="ExternalInput").ap()
    posk_d = nc.dram_tensor("posTk", [128, 32], F32, kind="ExternalInput").ap()
    posv_d = nc.dram_tensor("posTv", [128, 32], F32, kind="ExternalInput").ap()
    cd = {k: nc.dram_tensor("c_" + k, list(v.shape), CONST_DT[k], kind="ExternalInput").ap() for k, v in consts.items()}
    W = {k: nc.dram_tensor(k, list(shapes[k]), F32, kind="ExternalInput").ap() for k in WEIGHT_NAMES}
    outT = nc.dram_tensor("outT", [D, S], F32, kind="ExternalOutput").ap()
    hA = nc.dram_tensor("hA", [D, S], F32).ap()
    hB = nc.dram_tensor("hB", [D, S], F32).ap()
    scr = {
        "cqn": nc.dram_tensor("s_cqn", [MLA_QL, S], BF16).ap(),
        "ckvn": nc.dram_tensor("s_ckvn", [MLA_KVL, S], BF16).ap(),
        "kr": nc.dram_tensor("s_kr", [64, S], F32).ap(),
        "oT": nc.dram_tensor("s_oT", [D, S], BF16).ap(),
        "craw": nc.dram_tensor("s_craw", [2, 4, 128, S], BF16).ap(),
        "nkT": nc.dram_tensor("s_nkT", [2, 4, 128, S], BF16).ap(),
        "nv": nc.dram_tensor("s_nv", [2, 4, S, 128], BF16).ap(),
        "kcmp": nc.dram_tensor("s_kcmp", [4, 128, 128], BF16).ap(),
        "vcmp": nc.dram_tensor("s_vcmp", [4, 128, 128], BF16).ap(),
        "qn": nc.dram_tensor("s_qn", [H, 128, S], BF16).ap(),
        "qr": nc.dram_tensor("s_qr", [H, 128, S], BF16).ap(),
        "gates": nc.dram_tensor("s_gates", [96, S], F32).ap(),
        "ocmp": nc.dram_tensor("s_ocmp", [H, 128, S], F32).ap(),
    }
    kb = KB(nc)
    cx = Ctx()
    cx.kb = kb; cx.nc = nc; cx.vcol = vcol
    with ExitStack() as es:
        psum = [es.enter_context(nc.psum_tensor(f"ps{i}", [128, 512], F32)) for i in range(8)]
        cx.psA = Ring(kb, psum[0:4])
        cx.psB = Ring(kb, psum[4:8])
        csem = kb.get_dsem()
        cx.vecs = sb(es, nc, "vecs_sb", [128, nvec], F32)
        kb.dma("sp", cx.vecs[:, :], vecs_d, csem)
        cx.cdram = {k: (cd[k], list(v.shape), CONST_DT[k]) for k, v in consts.items()}
        for nm in ("cosA", "sinA", "cosB", "sinB"):
            cx.cdram[nm] = (nc.dram_tensor("s_" + nm, [128, S], F32).ap(), [128, S], F32)
        ct = {}
        for k in ("ones", "ident", "perm128", "perm64", "misc"):
            v = consts[k]
            ct[k] = sb(es, nc, "k_" + k, list(v.shape), CONST_DT[k])
            kb.dma("sp", ct[k][:, :], cd[k], csem)
        cx.ones = ct["ones"]; cx.ident = ct["ident"]; cx.perm128 = ct["perm128"]; cx.perm64 = ct["perm64"]
        misc = ct["misc"]
        cx.eps_col = misc[:, 0:1]; cx.pi_col = misc[:, 1:2]
        pk32 = sb(es, nc, "posk32", [128, 32], F32); pv32 = sb(es, nc, "posv32", [128, 32], F32)
        kb.dma("sp", pk32[:, :], posk_d, csem); kb.dma("sp", pv32[:, :], posv_d, csem)
        cx.posT_k = sb(es, nc, "posk", [128, 32], BF16); cx.posT_v = sb(es, nc, "posv", [128, 32], BF16)
        kb.barrier()
        kb.op("dve", lambda e: e.tensor_copy(out=cx.posT_k[:, :], in_=pk32[:, :]))
        kb.op("dve", lambda e: e.tensor_copy(out=cx.posT_v[:, :], in_=pv32[:, :]))
        kb.barrier()
        build_rope_tables(cx, pos, misc[0:64, 2:3], misc[0:64, 3:4], 64, cx.cdram["cosA"][0], cx.cdram["sinA"][0])
        build_rope_tables(cx, pos, misc[:, 4:5], misc[:, 5:6], 128, cx.cdram["cosB"][0], cx.cdram["sinB"][0])
        user_plan = plan
        plan = []
        for L in range(DEPTH):
            plan.append(("ffn1", L))
            plan.append(("mix", L))
            plan.append(("ffn2", L))
            if L == 1:
                plan.append(("kv", L))
        if stages is not None:
            plan = plan[:stages]
        if user_plan is not None:
            plan = list(user_plan)
        cur = xT
        nxt = [hA, hB]
        ni = 0
        for si, (kind, L) in enumerate(plan):
            last = si == len(plan) - 1
            if kind == "kv":
                shared_kv_stage(cx, cur, W, scr)
                compress_stage(cx, W, scr)
                if last:
                    pass
                continue
            dst = outT if (last or (kind == "ffn2" and si + 1 < len(plan) and plan[si + 1][0] == "kv" and si + 2 == len(plan))) else nxt[ni]
            if kind == "ffn1":
                ffn_stage(cx, cur, dst, vcol["ffn1_norm"][L], W["ffn1_w_gate"][L], W["ffn1_w_up"][L], W["ffn1_w_down"][L])
            elif kind == "ffn2":
                ffn_stage(cx, cur, dst, vcol["ffn2_norm"][L], W["ffn2_w_gate"][L], W["ffn2_w_up"][L], W["ffn2_w_down"][L])
            elif L < 2:
                mla_stage(cx, L, L, cur, dst, W, scr)
            else:
                nsa_stage(cx, L, L - 2, cur, dst, W, scr)
            cur = dst
            if dst is not outT:
                ni ^= 1
        kb.barrier()
    cx.n_ins = kb.n_ins
    return nc, scr


_CACHE = {}


def run_model(inputs, cores, stages=None, trace=False, plan=None):
    inp = {k: np.asarray(v) for k, v in inputs.items()}
    vecs, vcol = build_vecs(inp)
    consts = build_consts()
    shapes = {k: inp[k].shape for k in WEIGHT_NAMES}
    key = (stages, tuple(plan) if plan else None)
    if key not in _CACHE:
        _CACHE[key] = build_program(shapes, vcol, vecs.shape[1], consts, stages, plan)
    nc, _ = _CACHE[key]
    in_maps = []
    for b in cores:
        m = {"xT": np.ascontiguousarray(inp["x"][b].T), "pos": np.ascontiguousarray(inp["positions"][b][None, :].astype(np.int32)),
             "vecs": vecs, "posTk": np.ascontiguousarray(inp["cmp_pos_k"].T.astype(np.float32)),
             "posTv": np.ascontiguousarray(inp["cmp_pos_v"].T.astype(np.float32))}
        for k, v in consts.items():
            m["c_" + k] = v
        for k in WEIGHT_NAMES:
            m[k] = np.ascontiguousarray(inp[k], dtype=np.float32)
        in_maps.append(m)
    res = run_bass_kernel_spmd(nc, in_maps, core_ids=list(range(len(cores))), trace=trace)
    outs = [np.ascontiguousarray(r["outT"].T) for r in res.results]
    return outs, res


def kernel(**inputs):
    outs, _ = run_model(inputs, list(range(NB)))
    return np.stack(outs, axis=0).astype(np.float32)
```

```python
import numpy as np
from contextlib import ExitStack
import concourse.bass as bass
import concourse.mybir as mybir
from concourse.bass_utils import run_bass_kernel_spmd

F32 = mybir.dt.float32
BF16 = mybir.dt.bfloat16
I32 = mybir.dt.int32
ALU = mybir.AluOpType
AF = mybir.ActivationFunctionType

D = 4096; S = 2048; FF = 6144; DEPTH = 4; NB = 4
KC = D // 128; FC = FF // 128
EPS = 1e-6
TT = 512; NTT = S // TT
H = 32
MLA_QL = 1024; MLA_KVL = 512; MLA_IN = 1600
NSA_IN = 4192
N_CMP = 127
PI = float(np.pi)


class Res:
    __slots__ = ("w", "r")

    def __init__(self):
        self.w = None
        self.r = {}


class CSem:
    def __init__(self, h):
        self.h = h
        self.count = 0


class KB:
    def __init__(self, nc, n_dma_sems=48):
        self.nc = nc
        self.eng = {"pe": nc.tensor, "act": nc.scalar, "dve": nc.vector, "pool": nc.gpsimd, "sp": nc.sync}
        self.esem = {e: CSem(nc.alloc_semaphore(f"s_{e}")) for e in ("pe", "act", "dve", "pool")}
        self.seen = {e: {} for e in self.eng}
        self.dsems = [CSem(nc.alloc_semaphore(f"s_d{i}")) for i in range(n_dma_sems)]
        self.free_dsems = list(self.dsems)
        self.n_ins = 0

    def get_dsem(self):
        return self.free_dsems.pop()

    def put_dsem(self, s):
        self.free_dsems.append(s)

    def _waits(self, e, reads, writes):
        need = {}
        for r in reads:
            if r is not None and r.w is not None:
                s, v = r.w
                if need.get(s, 0) < v:
                    need[s] = v
        for w in writes:
            if w is None:
                continue
            if w.w is not None:
                s, v = w.w
                if need.get(s, 0) < v:
                    need[s] = v
            for s, v in w.r.items():
                if need.get(s, 0) < v:
                    need[s] = v
        seen = self.seen[e]
        own = self.esem.get(e)
        for s, v in need.items():
            if e == "pe" and s is own:
                continue
            if seen.get(s, 0) < v:
                self.eng[e].wait_ge(s.h, v)
                seen[s] = v
                self.n_ins += 1

    def op(self, e, fn, reads=(), writes=(), inc=True):
        self._waits(e, reads, writes)
        ins = fn(self.eng[e])
        self.n_ins += 1
        s = self.esem[e]
        if inc:
            s.count += 1
            ins.then_inc(s.h, 1)
            v = s.count
        else:
            v = s.count + 1
        for r in reads:
            if r is not None and r.r.get(s, 0) < v:
                r.r[s] = v
        for w in writes:
            if w is not None:
                w.w = (s, v)
                w.r = {}
        return ins

    def dma(self, q, out, in_, sem, reads=(), writes=()):
        self._waits(q, reads, writes)
        ins = self.eng[q].dma_start(out=out, in_=in_)
        self.n_ins += 1
        sem.count += 16
        ins.then_inc(sem.h, 16)
        v = sem.count
        for r in reads:
            if r is not None and r.r.get(sem, 0) < v:
                r.r[sem] = v
        for w in writes:
            if w is not None:
                w.w = (sem, v)
                w.r = {}
        return ins

    def barrier(self):
        sems = list(self.esem.values()) + self.dsems
        for e in self.eng:
            seen = self.seen[e]
            for s in sems:
                if s.count > seen.get(s, 0):
                    self.eng[e].wait_ge(s.h, s.count)
                    seen[s] = s.count
                    self.n_ins += 1


class Ring:
    def __init__(self, kb, tiles, dma=False):
        self.kb = kb
        self.tiles = tiles
        self.res = [Res() for _ in tiles]
        self.sems = [kb.get_dsem() for _ in tiles] if dma else [None] * len(tiles)
        self.i = 0

    def next(self):
        i = self.i
        self.i = (i + 1) % len(self.tiles)
        return self.tiles[i], self.res[i], self.sems[i]

    def release(self):
        for s in self.sems:
            if s is not None:
                self.kb.put_dsem(s)


class Ctx:
    pass


_UID = [0]


def sb(es, nc, name, shape, dt):
    _UID[0] += 1
    return es.enter_context(nc.sbuf_tensor(f"{name}_{_UID[0]}", shape, dt))


def load_consts(cx, es, names):
    kb, nc = cx.kb, cx.nc
    sem = kb.get_dsem()
    for nm in names:
        ap, shape, dt = cx.cdram[nm]
        t = sb(es, nc, "k_" + nm, shape, dt)
        kb.dma("sp", t[tuple(slice(None) for _ in shape)], ap, sem)
        setattr(cx, nm, t)
    kb.barrier()
    kb.put_dsem(sem)


def act_recip(cx, out_ap, in_ap, in_r, out_r):
    kb = cx.kb
    kb.op("act", lambda e: e.activation(out=out_ap, in_=in_ap, func=AF.Ln), reads=[in_r], writes=[out_r])
    kb.op("act", lambda e: e.activation(out=out_ap, in_=out_ap, func=AF.Exp, scale=-1.0), reads=[out_r], writes=[out_r])


def rms_rstd(cx, es, chunks, dim, N, rstd, rstd_r, sqring):
    kb = cx.kb
    ps, ps_r, _ = cx.psA.next()
    n = len(chunks)
    for i, (ap, r, P) in enumerate(chunks):
        s, s_r, _ = sqring.next()
        kb.op("act", lambda e: e.activation(out=s[0:P, 0:N], in_=ap, func=AF.Square), reads=[r], writes=[s_r])
        kb.op("pe", lambda e: e.matmul(ps[:, 0:N], lhsT=cx.ones[0:P, :], rhs=s[0:P, 0:N], start=(i == 0), stop=(i == n - 1)),
              reads=[s_r], writes=[ps_r])
    kb.op("act", lambda e: e.activation(out=rstd[:, 0:N], in_=ps[:, 0:N], func=AF.Ln, scale=1.0 / dim, bias=cx.eps_col[:, 0:1]),
          reads=[ps_r], writes=[rstd_r])
    kb.op("act", lambda e: e.activation(out=rstd[:, 0:N], in_=rstd[:, 0:N], func=AF.Exp, scale=-0.5), reads=[rstd_r], writes=[rstd_r])


def norm_tile(cx, es, hin_v, gcol0, t0, N, yT, yT_r, ycol0, xring, sqring, rstd, rstd_r):
    kb = cx.kb
    xt = []
    for q in range(4):
        x, x_r, x_s = xring.next()
        kb.dma("sp", x[:, :, 0:N], hin_v[:, q * 8:(q + 1) * 8, t0:t0 + N], x_s, writes=[x_r])
        xt.append((x, x_r))
    chunks = [(xt[c // 8][0][:, c % 8, 0:N], xt[c // 8][1], 128) for c in range(KC)]
    rms_rstd(cx, es, chunks, D, N, rstd, rstd_r, sqring)
    for c in range(KC):
        x, x_r = xt[c // 8]
        kb.op("dve", lambda e: e.scalar_tensor_tensor(out=yT[:, c, ycol0:ycol0 + N], in0=x[:, c % 8, 0:N],
              scalar=cx.vecs[:, gcol0 + c:gcol0 + c + 1], in1=rstd[:, 0:N], op0=ALU.mult, op1=ALU.mult),
              reads=[x_r, rstd_r], writes=[yT_r])


def proj_fm(cx, wv, cols, x, x_r, n_kc, tiles, evac, wring):
    kb = cx.kb
    pending = []
    for ci, (c0, M) in enumerate(cols):
        w, w_r, w_s = wring.next()
        kb.dma("pool", w[:, 0:n_kc, 0:M], wv[:, 0:n_kc, c0:c0 + M], w_s, writes=[w_r])
        Mp = 128 if M < 128 else M
        for ti, (col0, N) in enumerate(tiles):
            ps, ps_r, _ = cx.psA.next()
            for kc in range(n_kc):
                kb.op("pe", lambda e: e.matmul(ps[0:Mp, 0:N], lhsT=w[:, kc, 0:Mp], rhs=x[:, kc, col0:col0 + N],
                      start=(kc == 0), stop=(kc == n_kc - 1)), reads=[w_r, x_r], writes=[ps_r], inc=(kc == n_kc - 1))
            newp = []
            for c_ in pending:
                r_ = c_()
                if r_ is not None:
                    newp.append(r_)
            r_ = evac(ci, ti, ps, ps_r, M, N)
            if r_ is not None:
                newp.append(r_)
            pending = newp
    while pending:
        newp = []
        for c_ in pending:
            r_ = c_()
            if r_ is not None:
                newp.append(r_)
        pending = newp


def out_proj_stage(cx, oT_d, w_o, hin, hout):
    kb, nc = cx.kb, cx.nc
    o_v = oT_d.rearrange("(c p) n -> p c n", p=128)
    w_v = w_o.rearrange("(c p) f -> p c f", p=128)
    hin_v = hin.rearrange("(c p) n -> p c n", p=128)
    hout_v = hout.rearrange("(c p) n -> p c n", p=128)
    with ExitStack() as es:
        oT = sb(es, nc, "op_oT", [128, KC, 1024], BF16); oT_r = Res()
        osem = kb.get_dsem()
        wring = Ring(kb, [sb(es, nc, f"op_w{i}", [128, KC, 128], BF16) for i in range(3)], dma=True)
        hxring = Ring(kb, [sb(es, nc, f"op_hx{i}", [128, TT], F32) for i in range(3)], dma=True)
        obring = Ring(kb, [sb(es, nc, f"op_ob{i}", [128, TT], F32) for i in range(3)], dma=True)
        for half in range(2):
            h0 = half * 1024
            for q in range(4):
                kb.dma("sp", oT[:, q * 8:(q + 1) * 8, :], o_v[:, q * 8:(q + 1) * 8, h0:h0 + 1024], osem, writes=[oT_r])

            def evac(ci, ti, ps, ps_r, M, N):
                t0 = h0 + ti * TT
                hx, hx_r, hx_s = hxring.next()
                kb.dma("sp", hx[:, :], hin_v[:, ci, t0:t0 + TT], hx_s, writes=[hx_r])
                o, o_r, o_s = obring.next()
                kb.op("dve", lambda e: e.tensor_tensor(out=o[:, :], in0=ps[:, :], in1=hx[:, :], op=ALU.add),
                      reads=[ps_r, hx_r], writes=[o_r])
                kb.dma("sp", hout_v[:, ci, t0:t0 + TT], o[:, :], o_s, reads=[o_r])

            proj_fm(cx, w_v, [(c * 128, 128) for c in range(KC)], oT, oT_r, KC, [(0, TT), (TT, TT)], evac, wring)
        kb.barrier()
        wring.release(); hxring.release(); obring.release(); kb.put_dsem(osem)


def ffn_half(cx, hin, hout, gcol0, wg, wu, wd, tok0):
    kb, nc = cx.kb, cx.nc
    NT = 1024
    hin_v = hin.rearrange("(c p) n -> p c n", p=128)
    hout_v = hout.rearrange("(c p) n -> p c n", p=128)
    wg_v = wg.rearrange("(c p) f -> p c f", p=128)
    wu_v = wu.rearrange("(c p) f -> p c f", p=128)
    wd_v = wd.rearrange("(c p) f -> p c f", p=128)
    with ExitStack() as es:
        yT = sb(es, nc, "ff_yT", [128, KC, NT], BF16); yT_r = Res()
        with ExitStack() as es0:
            xring = Ring(kb, [sb(es0, nc, f"ff_x{i}", [128, 8, TT], F32) for i in range(4)], dma=True)
            sqring = Ring(kb, [sb(es0, nc, f"ff_sq{i}", [128, TT], BF16) for i in range(3)])
            rstd = sb(es0, nc, "ff_rstd", [128, TT], F32); rstd_r = Res()
            for t in range(NT // TT):
                norm_tile(cx, es0, hin_v, gcol0, tok0 + t * TT, TT, yT, yT_r, t * TT, xring, sqring, rstd, rstd_r)
            kb.barrier()
            xring.release()
        with ExitStack() as es1:
            actT = sb(es1, nc, "ff_actT", [128, FC, NT], BF16); actT_r = Res()
            with ExitStack() as es1a:
                wring = Ring(kb, [sb(es1a, nc, f"ff_w{i}", [128, 16, 128], BF16) for i in range(8)], dma=True)
                sgring = Ring(kb, [sb(es1a, nc, f"ff_sg{i}", [128, TT], F32) for i in range(2)])
                for fc in range(FC):
                    fsl = slice(fc * 128, (fc + 1) * 128)
                    units = {}
                    for nm, wv in (("g", wg_v), ("u", wu_v)):
                        for kh in range(2):
                            w, w_r, w_s = wring.next()
                            kb.dma("pool", w[:, :, :], wv[:, kh * 16:(kh + 1) * 16, fsl], w_s, writes=[w_r])
                            units[(nm, kh)] = (w, w_r)
                    for t in range(NT // TT):
                        tsl = slice(t * TT, (t + 1) * TT)
                        pg, pg_r, _ = cx.psA.next()
                        pu, pu_r, _ = cx.psA.next()
                        for nm, p, p_r in (("g", pg, pg_r), ("u", pu, pu_r)):
                            for c in range(KC):
                                w, w_r = units[(nm, c // 16)]
                                kb.op("pe", lambda e: e.matmul(p[:, :], lhsT=w[:, c % 16, :], rhs=yT[:, c, tsl],
                                      start=(c == 0), stop=(c == KC - 1)), reads=[w_r, yT_r], writes=[p_r], inc=(c == KC - 1))
                        s, s_r, _ = sgring.next()
                        kb.op("act", lambda e: e.activation(out=s[:, :], in_=pg[:, :], func=AF.Silu), reads=[pg_r], writes=[s_r])
                        kb.op("dve", lambda e: e.tensor_tensor(out=actT[:, fc, tsl], in0=s[:, :], in1=pu[:, :], op=ALU.mult),
                              reads=[s_r, pu_r], writes=[actT_r])
                kb.barrier()
                wring.release()
            with ExitStack() as es2:
                wdring = Ring(kb, [sb(es2, nc, f"ff_wd{i}", [128, 16, 128], BF16) for i in range(6)], dma=True)
                hxring = Ring(kb, [sb(es2, nc, f"ff_hx{i}", [128, TT], F32) for i in range(3)], dma=True)
                obring = Ring(kb, [sb(es2, nc, f"ff_ob{i}", [128, TT], F32) for i in range(3)], dma=True)
                for dc in range(KC):
                    dsl = slice(dc * 128, (dc + 1) * 128)
                    units = []
                    for kh in range(3):
                        w, w_r, w_s = wdring.next()
                        kb.dma("pool", w[:, :, :], wd_v[:, kh * 16:(kh + 1) * 16, dsl], w_s, writes=[w_r])
                        units.append((w, w_r))
                    for t in range(NT // TT):
                        tsl = slice(t * TT, (t + 1) * TT)
                        gsl = slice(tok0 + t * TT, tok0 + (t + 1) * TT)
                        hx, hx_r, hx_s = hxring.next()
                        kb.dma("sp", hx[:, :], hin_v[:, dc, gsl], hx_s, writes=[hx_r])
                        po, po_r, _ = cx.psA.next()
                        for c in range(FC):
                            w, w_r = units[c // 16]
                            kb.op("pe", lambda e: e.matmul(po[:, :], lhsT=w[:, c % 16, :], rhs=actT[:, c, tsl],
                                  start=(c == 0), stop=(c == FC - 1)), reads=[w_r, actT_r], writes=[po_r], inc=(c == FC - 1))
                        o, o_r, o_s = obring.next()
                        kb.op("dve", lambda e: e.scalar_tensor_tensor(out=o[:, :], in0=po[:, :], scalar=0.5, in1=hx[:, :],
                              op0=ALU.mult, op1=ALU.add), reads=[po_r, hx_r], writes=[o_r])
                        kb.dma("sp", hout_v[:, dc, gsl], o[:, :], o_s, reads=[o_r])
                kb.barrier()
                wdring.release(); hxring.release(); obring.release()


def ffn_stage(cx, hin, hout, gcol0, wg, wu, wd):
    for half in range(2):
        ffn_half(cx, hin, hout, gcol0, wg, wu, wd, half * 1024)


def rope_apply(cx, xb, xb_r, P, N, tcol0, perm, cosT, sinT, out_ap, out_r, tmpring, psring=None):
    kb = cx.kb
    ps, ps_r, _ = (psring or cx.psA).next()
    kb.op("pe", lambda e: e.matmul(ps[0:P, 0:N], lhsT=perm[0:P, 0:P], rhs=xb, start=True, stop=True), reads=[xb_r], writes=[ps_r])
    t1, t1_r, _ = tmpring.next()
    kb.op("dve", lambda e: e.tensor_tensor(out=t1[0:P, 0:N], in0=ps[0:P, 0:N], in1=sinT[0:P, tcol0:tcol0 + N], op=ALU.mult),
          reads=[ps_r], writes=[t1_r])
    t2, t2_r, _ = tmpring.next()
    kb.op("pool", lambda e: e.tensor_tensor(out=t2[0:P, 0:N], in0=xb, in1=cosT[0:P, tcol0:tcol0 + N], op=ALU.mult),
          reads=[xb_r], writes=[t2_r])
    kb.op("dve", lambda e: e.tensor_tensor(out=out_ap, in0=t1[0:P, 0:N], in1=t2[0:P, 0:N], op=ALU.add),
          reads=[t1_r, t2_r], writes=[out_r])


def build_rope_tables(cx, pos_d, invcol, sgncol, P, cos_d, sin_d):
    kb, nc = cx.kb, cx.nc
    r = Res()
    with ExitStack() as es2:
        cosT = sb(es2, nc, "rt_cos", [128, S], F32)
        sinT = sb(es2, nc, "rt_sin", [128, S], F32)
        pi_ = sb(es2, nc, "rt_pi", [128, S], I32)
        ang = sb(es2, nc, "rt_ang", [128, S], F32)
        tmp = sb(es2, nc, "rt_tmp", [128, S], F32)
        sem = kb.get_dsem()
        src = bass.AP(pos_d.tensor, 0, [[0, 128], [1, S]])
        kb.dma("sp", pi_[:, :], src, sem, writes=[r])
        kb.op("dve", lambda e: e.memset(cosT[:, :], 0.0), writes=[r])
        kb.op("dve", lambda e: e.memset(sinT[:, :], 0.0), writes=[r])
        kb.op("dve", lambda e: e.tensor_copy(out=ang[0:P, :], in_=pi_[0:P, :]), reads=[r], writes=[r])
        kb.op("dve", lambda e: e.tensor_scalar(out=ang[0:P, :], in0=ang[0:P, :], scalar1=invcol, scalar2=None, op0=ALU.mult), reads=[r], writes=[r])
        ki = sb(es2, nc, "rt_ki", [128, S], I32)
        msk = sb(es2, nc, "rt_m", [128, S], F32)

        def sin_of(out_t, shift):
            kb.op("dve", lambda e: e.tensor_scalar(out=tmp[0:P, :], in0=ang[0:P, :], scalar1=shift, scalar2=None, op0=ALU.add), reads=[r], writes=[r])
            kb.op("dve", lambda e: e.tensor_scalar(out=msk[0:P, :], in0=tmp[0:P, :], scalar1=1.0 / (2 * PI), scalar2=None, op0=ALU.mult), reads=[r], writes=[r])
            kb.op("dve", lambda e: e.tensor_copy(out=ki[0:P, :], in_=msk[0:P, :]), reads=[r], writes=[r])
            kb.op("dve", lambda e: e.tensor_copy(out=msk[0:P, :], in_=ki[0:P, :]), reads=[r], writes=[r])
            kb.op("dve", lambda e: e.scalar_tensor_tensor(out=tmp[0:P, :], in0=msk[0:P, :], scalar=-2 * PI, in1=tmp[0:P, :], op0=ALU.mult, op1=ALU.add), reads=[r], writes=[r])
            kb.op("dve", lambda e: e.tensor_scalar(out=msk[0:P, :], in0=tmp[0:P, :], scalar1=PI, scalar2=None, op0=ALU.is_gt), reads=[r], writes=[r])
            kb.op("dve", lambda e: e.scalar_tensor_tensor(out=tmp[0:P, :], in0=msk[0:P, :], scalar=-2 * PI, in1=tmp[0:P, :], op0=ALU.mult, op1=ALU.add), reads=[r], writes=[r])
            kb.op("dve", lambda e: e.tensor_scalar(out=msk[0:P, :], in0=tmp[0:P, :], scalar1=-PI, scalar2=None, op0=ALU.is_lt), reads=[r], writes=[r])
            kb.op("dve", lambda e: e.scalar_tensor_tensor(out=tmp[0:P, :], in0=msk[0:P, :], scalar=2 * PI, in1=tmp[0:P, :], op0=ALU.mult, op1=ALU.add), reads=[r], writes=[r])
            kb.op("dve", lambda e: e.tensor_scalar(out=tmp[0:P, :], in0=tmp[0:P, :], scalar1=-PI, scalar2=PI, op0=ALU.max, op1=ALU.min), reads=[r], writes=[r])
            kb.op("act", lambda e: e.activation(out=out_t[0:P, :], in_=tmp[0:P, :], func=AF.Sin), reads=[r], writes=[r])

        sin_of(sinT, 0.0)
        kb.op("dve", lambda e: e.tensor_scalar(out=sinT[0:P, :], in0=sinT[0:P, :], scalar1=sgncol, scalar2=None, op0=ALU.mult), reads=[r], writes=[r])
        sin_of(cosT, PI / 2)
        kb.dma("sp", cos_d, cosT[:, :], sem, reads=[r])
        kb.dma("sp", sin_d, sinT[:, :], sem, reads=[r])
        kb.barrier()
        kb.put_dsem(sem)


def attn_seq(cx, tiles, scale, o_ps, o_r, d_ps, d_r, pring, N=TT, ahead=2):
    kb = cx.kb
    n = len(tiles)

    def qk(i):
        tl = tiles[i]
        ps, ps_r, _ = cx.psA.next()
        m = len(tl["q"])
        for j in range(m):
            q_ap, q_r = tl["q"][j]
            k_ap, k_r = tl["k"][j]
            kb.op("pe", lambda e: e.matmul(ps[:, 0:N], lhsT=k_ap, rhs=q_ap, start=(j == 0), stop=(j == m - 1)),
                  reads=[q_r, k_r], writes=[ps_r], inc=(j == m - 1))
        p, p_r, _ = pring.next()
        kb.op("act", lambda e: e.activation(out=p[:, 0:N], in_=ps[:, 0:N], func=AF.Exp, scale=scale), reads=[ps_r], writes=[p_r])
        if tl.get("mask") is not None:
            m_ap, m_r = tl["mask"]
            kb.op("dve", lambda e: e.tensor_tensor(out=p[:, 0:N], in0=p[:, 0:N], in1=m_ap, op=ALU.mult), reads=[p_r, m_r], writes=[p_r])
        return p, p_r

    def pv(i, p, p_r):
        tl = tiles[i]
        v_ap, v_r = tl["v"]
        kb.op("pe", lambda e: e.matmul(o_ps[:, 0:N], lhsT=v_ap, rhs=p[:, 0:N], start=(i == 0), stop=(i == n - 1)), reads=[p_r, v_r], writes=[o_r])
        kb.op("pe", lambda e: e.matmul(d_ps[:, 0:N], lhsT=tl["ones"], rhs=p[:, 0:N], start=(i == 0), stop=(i == n - 1)), reads=[p_r], writes=[d_r])

    q = []
    for i in range(min(ahead, n)):
        q.append(qk(i))
    for i in range(n):
        if i + ahead < n:
            q.append(qk(i + ahead))
        p, p_r = q[i]
        pv(i, p, p_r)


def mla_stage(cx, L, a, hin, hout, W, scr):
    kb, nc = cx.kb, cx.nc
    V = cx.vcol
    hin_v = hin.rearrange("(c p) n -> p c n", p=128)
    w_in_v = W["mla_w_in"][a].rearrange("(c p) f -> p c f", p=128)
    w_uq_v = W["mla_w_uq"][a].rearrange("(c p) f -> p c f", p=128)
    w_ukv_v = W["mla_w_ukv"][a].rearrange("(c p) f -> p c f", p=128)
    cqn_d, ckvn_d, kr_d, oT_d = scr["cqn"], scr["ckvn"], scr["kr"], scr["oT"]
    cqn_dv = cqn_d.rearrange("(c p) n -> p c n", p=128)
    ckvn_dv = ckvn_d.rearrange("(c p) n -> p c n", p=128)
    oT_dv = oT_d.rearrange("(c p) n -> p c n", p=128)
    with ExitStack() as es:
        uT = sb(es, nc, "m1_uT", [128, KC, TT], BF16); uT_r = Res()
        xring = Ring(kb, [sb(es, nc, f"m1_x{i}", [128, 8, TT], F32) for i in range(4)], dma=True)
        sqring = Ring(kb, [sb(es, nc, f"m1_sq{i}", [128, TT], BF16) for i in range(3)])
        rstd = sb(es, nc, "m1_rstd", [128, TT], F32); rstd_r = Res()
        wring = Ring(kb, [sb(es, nc, f"m1_w{i}", [128, KC, 128], BF16) for i in range(3)], dma=True)
        cbuf = sb(es, nc, "m1_c", [128, 13, TT], F32); c_rs = [Res() for _ in range(13)]
        cn = sb(es, nc, "m1_cn", [128, 12, TT], BF16); cn_r = Res()
        cnsem = kb.get_dsem()
        cols = [(c * 128, 128) for c in range(12)] + [(1536, 64)]
        for t in range(NTT):
            t0 = t * TT
            norm_tile(cx, es, hin_v, V["mix_norm"][L], t0, TT, uT, uT_r, 0, xring, sqring, rstd, rstd_r)

            def evac(ci, ti, ps, ps_r, M, N):
                kb.op("act", lambda e: e.activation(out=cbuf[0:M, ci, :], in_=ps[0:M, :], func=AF.Copy), reads=[ps_r], writes=[c_rs[ci]])

            proj_fm(cx, w_in_v, cols, uT, uT_r, KC, [(0, TT)], evac, wring)
            rms_rstd(cx, es, [(cbuf[:, c, :], c_rs[c], 128) for c in range(8)], MLA_QL, TT, rstd, rstd_r, sqring)
            for c in range(8):
                kb.op("dve", lambda e: e.scalar_tensor_tensor(out=cn[:, c, :], in0=cbuf[:, c, :], scalar=cx.vecs[:, V["mla_g_cq"][a] + c:V["mla_g_cq"][a] + c + 1],
                      in1=rstd[:, :], op0=ALU.mult, op1=ALU.mult), reads=[c_rs[c], rstd_r], writes=[cn_r])
            rms_rstd(cx, es, [(cbuf[:, 8 + c, :], c_rs[8 + c], 128) for c in range(4)], MLA_KVL, TT, rstd, rstd_r, sqring)
            for c in range(4):
                kb.op("dve", lambda e: e.scalar_tensor_tensor(out=cn[:, 8 + c, :], in0=cbuf[:, 8 + c, :], scalar=cx.vecs[:, V["mla_g_ckv"][a] + c:V["mla_g_ckv"][a] + c + 1],
                      in1=rstd[:, :], op0=ALU.mult, op1=ALU.mult), reads=[c_rs[8 + c], rstd_r], writes=[cn_r])
            kb.dma("sp", cqn_dv[:, :, t0:t0 + TT], cn[:, 0:8, :], cnsem, reads=[cn_r])
            kb.dma("sp", ckvn_dv[:, :, t0:t0 + TT], cn[:, 8:12, :], cnsem, reads=[cn_r])
            kb.dma("sp", kr_d[:, t0:t0 + TT], cbuf[0:64, 12, :], cnsem, reads=[c_rs[12]])
        kb.barrier()
        xring.release(); wring.release(); kb.put_dsem(cnsem)
    with ExitStack() as es:
        load_consts(cx, es, ["cosA", "sinA", "maskC"])
        cqn = sb(es, nc, "m2_cqn", [128, 8, S], BF16)
        ckvn = sb(es, nc, "m2_ckvn", [128, 4, S], BF16)
        kr = sb(es, nc, "m2_kr", [128, S], F32)
        krg = sb(es, nc, "m2_krg", [128, S], BF16)
        ssq_kr = sb(es, nc, "m2_ssqkr", [128, S], F32)
        ld_r = Res()
        sem = kb.get_dsem()
        kb.dma("sp", cqn[:, :, :], cqn_dv, sem, writes=[ld_r])
        kb.dma("sp", ckvn[:, :, :], ckvn_dv, sem, writes=[ld_r])
        kb.op("dve", lambda e: e.memset(kr[:, :], 0.0), writes=[ld_r])
        kb.dma("sp", kr[0:64, :], kr_d, sem, writes=[ld_r])
        sqring = Ring(kb, [sb(es, nc, f"m2_sq{i}", [128, TT], BF16) for i in range(3)])
        tmpring = Ring(kb, [sb(es, nc, f"m2_tmp{i}", [128, TT], F32) for i in range(4)])
        tbring = Ring(kb, [sb(es, nc, f"m2_tb{i}", [128, TT], BF16) for i in range(3)])
        gk_r = cx.vecs[:, V["mla_g_k_r"][a]:V["mla_g_k_r"][a] + 1]
        gk_n = cx.vecs[:, V["mla_g_k_n"][a]:V["mla_g_k_n"][a] + 1]
        gq_r = cx.vecs[:, V["mla_g_q_r"][a]:V["mla_g_q_r"][a] + 1]
        gq_n = cx.vecs[:, V["mla_g_q_n"][a]:V["mla_g_q_n"][a] + 1]
        for t in range(NTT):
            tsl = slice(t * TT, (t + 1) * TT)
            s, s_r, _ = sqring.next()
            kb.op("act", lambda e: e.activation(out=s[:, :], in_=kr[:, tsl], func=AF.Square), reads=[ld_r], writes=[s_r])
            ps, ps_r, _ = cx.psA.next()
            kb.op("pe", lambda e: e.matmul(ps[:, :], lhsT=cx.ones[:, :], rhs=s[:, :], start=True, stop=True), reads=[s_r], writes=[ps_r])
            kb.op("act", lambda e: e.activation(out=ssq_kr[:, tsl], in_=ps[:, :], func=AF.Copy), reads=[ps_r], writes=[ld_r])
            tb, tb_r, _ = tbring.next()
            kb.op("dve", lambda e: e.tensor_scalar(out=tb[:, :], in0=kr[:, tsl], scalar1=gk_r, scalar2=None, op0=ALU.mult), reads=[ld_r], writes=[tb_r])
            rope_apply(cx, tb[:, :], tb_r, 128, TT, t * TT, cx.perm64, cx.cosA, cx.sinA, krg[:, tsl], ld_r, tmpring)
        kb.barrier()
        kTb = [sb(es, nc, f"m2_kT{i}", [128, S], BF16) for i in range(2)]; kT_rs = [Res(), Res()]
        krTb = [sb(es, nc, f"m2_krT{i}", [128, S], BF16) for i in range(2)]; krT_rs = [Res(), Res()]
        vtmb = [sb(es, nc, f"m2_v{i}", [128, 16, 128], BF16) for i in range(2)]; v_rs = [Res(), Res()]
        qTb = [sb(es, nc, f"m2_qT{i}", [128, S], BF16) for i in range(2)]; qT_rs = [Res(), Res()]
        qrTb = [sb(es, nc, f"m2_qrT{i}", [128, S], BF16) for i in range(2)]; qrT_rs = [Res(), Res()]
        oT = sb(es, nc, "m2_oT", [128, S], BF16); oT_r = Res()
        rstdring = Ring(kb, [sb(es, nc, f"m2_rstd{i}", [128, TT], F32) for i in range(3)])
        rden = sb(es, nc, "m2_rden", [128, TT], F32); rden_r = Res()
        rawK = Ring(kb, [sb(es, nc, f"m2_rawK{i}", [128, TT], F32) for i in range(2)])
        rawQ0 = Ring(kb, [sb(es, nc, f"m2_rawQ0{i}", [128, TT], F32) for i in range(2)])
        rawQ1 = Ring(kb, [sb(es, nc, f"m2_rawQ1{i}", [128, TT], F32) for i in range(2)])
        wk = Ring(kb, [sb(es, nc, f"m2_wk{i}", [128, 4, 256], BF16) for i in range(2)], dma=True)
        wq = Ring(kb, [sb(es, nc, f"m2_wq{i}", [128, 8, 256], BF16) for i in range(2)], dma=True)
        for i_ in range(2):
            kb.op("dve", lambda e: e.memset(wq.tiles[i_][:, :, :], 0.0), writes=[wq.res[i_]])
        pring = Ring(kb, [sb(es, nc, f"m2_p{i}", [128, TT], BF16) for i in range(3)])
        osem = kb.get_dsem()
        scale = 192.0 ** -0.5
        hw = {}

        def load_w(h):
            wkv, wkv_r, wkv_s = wk.next()
            kb.dma("pool", wkv[:, :, :], w_ukv_v[:, :, h * 256:(h + 1) * 256], wkv_s, writes=[wkv_r])
            wqh, wqh_r, wqh_s = wq.next()
            kb.dma("pool", wqh[:, :, 0:192], w_uq_v[:, :, h * 192:(h + 1) * 192], wqh_s, writes=[wqh_r])
            hw[h] = (wkv, wkv_r, wqh, wqh_r)

        def proj_main(h, t):
            wkv, wkv_r, wqh, wqh_r = hw[h]
            hb = h % 2
            tsl = slice(t * TT, (t + 1) * TT)
            ps, ps_r, _ = cx.psA.next()
            for kc in range(4):
                kb.op("pe", lambda e: e.matmul(ps[:, :], lhsT=wkv[:, kc, 0:128], rhs=ckvn[:, kc, tsl], start=(kc == 0), stop=(kc == 3)),
                      reads=[wkv_r], writes=[ps_r], inc=(kc == 3))
            rk, rk_r, _ = rawK.next()
            kb.op("act", lambda e: e.activation(out=rk[:, :], in_=ps[:, :], func=AF.Copy), reads=[ps_r], writes=[rk_r])
            psq, psq_r, _ = cx.psA.next()
            for kc in range(8):
                kb.op("pe", lambda e: e.matmul(psq[:, :], lhsT=wqh[:, kc, 0:128], rhs=cqn[:, kc, tsl], start=(kc == 0), stop=(kc == 7)),
                      reads=[wqh_r], writes=[psq_r], inc=(kc == 7))
            q0, q0_r, _ = rawQ0.next()
            kb.op("act", lambda e: e.activation(out=q0[:, :], in_=psq[:, :], func=AF.Copy), reads=[psq_r], writes=[q0_r])
            psr_, psr_r, _ = cx.psA.next()
            for kc in range(8):
                kb.op("pe", lambda e: e.matmul(psr_[:, :], lhsT=wqh[:, kc, 128:256], rhs=cqn[:, kc, tsl], start=(kc == 0), stop=(kc == 7)),
                      reads=[wqh_r], writes=[psr_r], inc=(kc == 7))
            q1, q1_r, _ = rawQ1.next()
            kb.op("act", lambda e: e.activation(out=q1[:, :], in_=psr_[:, :], func=AF.Copy), reads=[psr_r], writes=[q1_r])
            psv, psv_r, _ = cx.psA.next()
            for tt in range(4):
                k0 = t * TT + tt * 128
                for kc in range(4):
                    kb.op("pe", lambda e: e.matmul(psv[:, tt * 128:(tt + 1) * 128], lhsT=ckvn[:, kc, k0:k0 + 128], rhs=wkv[:, kc, 128:256],
                          start=(kc == 0), stop=(kc == 3)), reads=[wkv_r], writes=[psv_r], inc=(tt == 3 and kc == 3))
            kb.op("act", lambda e: e.activation(out=vtmb[hb][:, t * 4:(t + 1) * 4, :], in_=psv[:, :], func=AF.Copy), reads=[psv_r], writes=[v_rs[hb]])
            sk, sk_r, _ = sqring.next()
            kb.op("act", lambda e: e.activation(out=sk[:, :], in_=rk[:, :], func=AF.Square), reads=[rk_r], writes=[sk_r])
            s0, s0_r, _ = sqring.next()
            kb.op("act", lambda e: e.activation(out=s0[:, :], in_=q0[:, :], func=AF.Square), reads=[q0_r], writes=[s0_r])
            s1, s1_r, _ = sqring.next()
            kb.op("act", lambda e: e.activation(out=s1[:, :], in_=q1[:, :], func=AF.Square), reads=[q1_r], writes=[s1_r])
            return dict(h=h, t=t, rk=(rk, rk_r), q0=(q0, q0_r), q1=(q1, q1_r), sk=(sk, sk_r), s0=(s0, s0_r), s1=(s1, s1_r))

        def norm_a(st):
            h, t = st["h"], st["t"]
            hb = h % 2
            tsl = slice(t * TT, (t + 1) * TT)
            rk, rk_r = st["rk"]; q0, q0_r = st["q0"]; q1, q1_r = st["q1"]
            sk, sk_r = st["sk"]; s0, s0_r = st["s0"]; s1, s1_r = st["s1"]
            ps2, ps2_r, _ = cx.psA.next()
            kb.op("pe", lambda e: e.matmul(ps2[:, :], lhsT=cx.ones[:, :], rhs=sk[:, :], start=True, stop=True), reads=[sk_r], writes=[ps2_r])
            ps3, ps3_r, _ = cx.psA.next()
            kb.op("pe", lambda e: e.matmul(ps3[:, :], lhsT=cx.ones[:, :], rhs=s0[:, :], start=True, stop=False), reads=[s0_r], writes=[ps3_r], inc=False)
            kb.op("pe", lambda e: e.matmul(ps3[:, :], lhsT=cx.ones[:, :], rhs=s1[:, :], start=False, stop=True), reads=[s1_r], writes=[ps3_r])
            rsk, rsk_r, _ = rstdring.next()
            kb.op("dve", lambda e: e.tensor_tensor(out=rsk[:, :], in0=ps2[:, :], in1=ssq_kr[:, tsl], op=ALU.add), reads=[ps2_r], writes=[rsk_r])
            kb.op("act", lambda e: e.activation(out=rsk[:, :], in_=rsk[:, :], func=AF.Ln, scale=1.0 / 192, bias=cx.eps_col[:, 0:1]), reads=[rsk_r], writes=[rsk_r])
            kb.op("act", lambda e: e.activation(out=rsk[:, :], in_=rsk[:, :], func=AF.Exp, scale=-0.5), reads=[rsk_r], writes=[rsk_r])
            rsq, rsq_r, _ = rstdring.next()
            kb.op("act", lambda e: e.activation(out=rsq[:, :], in_=ps3[:, :], func=AF.Ln, scale=1.0 / 192, bias=cx.eps_col[:, 0:1]), reads=[ps3_r], writes=[rsq_r])
            kb.op("act", lambda e: e.activation(out=rsq[:, :], in_=rsq[:, :], func=AF.Exp, scale=-0.5), reads=[rsq_r], writes=[rsq_r])
            kb.op("dve", lambda e: e.scalar_tensor_tensor(out=kTb[hb][:, tsl], in0=rk[:, :], scalar=gk_n, in1=rsk[:, :], op0=ALU.mult, op1=ALU.mult),
                  reads=[rk_r, rsk_r], writes=[kT_rs[hb]])
            kb.op("pool", lambda e: e.tensor_tensor(out=krTb[hb][:, tsl], in0=krg[:, tsl], in1=rsk[:, :], op=ALU.mult), reads=[rsk_r], writes=[krT_rs[hb]])
            kb.op("dve", lambda e: e.scalar_tensor_tensor(out=qTb[hb][:, tsl], in0=q0[:, :], scalar=gq_n, in1=rsq[:, :], op0=ALU.mult, op1=ALU.mult),
                  reads=[q0_r, rsq_r], writes=[qT_rs[hb]])
            tb, tb_r, _ = tbring.next()
            kb.op("dve", lambda e: e.scalar_tensor_tensor(out=tb[:, :], in0=q1[:, :], scalar=gq_r, in1=rsq[:, :], op0=ALU.mult, op1=ALU.mult),
                  reads=[q1_r, rsq_r], writes=[tb_r])
            st["tb"] = (tb, tb_r)

        def norm_b(st):
            h, t = st["h"], st["t"]
            hb = h % 2
            tb, tb_r = st["tb"]
            rope_apply(cx, tb[:, :], tb_r, 128, TT, t * TT, cx.perm64, cx.cosA, cx.sinA, qrTb[hb][:, t * TT:(t + 1) * TT], qrT_rs[hb], tmpring)

        def attn(h, t):
            hb = h % 2
            tsl = slice(t * TT, (t + 1) * TT)
            o_ps, o_r, _ = cx.psB.next()
            d_ps, d_r, _ = cx.psB.next()
            nk = 4 * t + 4
            tiles = []
            for kt in range(nk):
                ksl = slice(kt * 128, (kt + 1) * 128)
                diag = kt >= 4 * t
                tiles.append(dict(q=[(qTb[hb][:, tsl], qT_rs[hb]), (qrTb[hb][:, tsl], qrT_rs[hb])], k=[(kTb[hb][:, ksl], kT_rs[hb]), (krTb[hb][:, ksl], krT_rs[hb])],
                                  v=(vtmb[hb][:, kt, :], v_rs[hb]), ones=cx.ones[:, :], mask=(cx.maskC[:, kt - 4 * t, :], None) if diag else None))
            attn_seq(cx, tiles, scale, o_ps, o_r, d_ps, d_r, pring)
            act_recip(cx, rden[:, :], d_ps[:, :], d_r, rden_r)
            kb.op("dve", lambda e: e.tensor_tensor(out=oT[:, tsl], in0=o_ps[:, :], in1=rden[:, :], op=ALU.mult), reads=[o_r, rden_r], writes=[oT_r])

        load_w(0)
        pend_b = None
        for t in range(NTT):
            st = proj_main(0, t)
            norm_a(st)
            norm_b(st)
        for h in range(H):
            if h + 1 < H:
                load_w(h + 1)
            for t in range(NTT):
                st = proj_main(h + 1, t) if h + 1 < H else None
                attn(h, t)
                if pend_b is not None:
                    norm_b(pend_b)
                    pend_b = None
                if st is not None:
                    norm_a(st)
                    pend_b = st
            if pend_b is not None:
                norm_b(pend_b)
                pend_b = None
            kb.dma("sp", oT_dv[:, h, :], oT[:, :], osem, reads=[oT_r])
        kb.barrier()
        wk.release(); wq.release(); kb.put_dsem(osem); kb.put_dsem(sem)
    out_proj_stage(cx, oT_d, W["mla_w_o"][a], hin, hout)


def shared_kv_stage(cx, hin, W, scr):
    kb, nc = cx.kb, cx.nc
    V = cx.vcol
    hin_v = hin.rearrange("(c p) n -> p c n", p=128)
    kvw_v = W["kv_w"].rearrange("(c p) f -> p c f", p=128)
    craw_d = scr["craw"]
    kT_d = scr["nkT"]
    v_d = scr["nv"]
    with ExitStack() as es:
        load_consts(cx, es, ["cosB", "sinB"])
        uT = sb(es, nc, "kv_uT", [128, KC, 1024], BF16); uT_r = Res()
        xring = Ring(kb, [sb(es, nc, f"kv_x{i}", [128, 8, 256], F32) for i in range(4)], dma=True)
        sqring = Ring(kb, [sb(es, nc, f"kv_sq{i}", [128, TT], BF16) for i in range(3)])
        rstd = sb(es, nc, "kv_rstd", [128, TT], F32); rstd_r = Res()
        wring = Ring(kb, [sb(es, nc, f"kv_w{i}", [128, KC, 128], BF16) for i in range(3)], dma=True)
        wvring = Ring(kb, [sb(es, nc, f"kv_wv{i}", [128, KC, 512], BF16) for i in range(1)], dma=True)
        raw = sb(es, nc, "kv_raw", [128, TT], F32); raw_r = Res()
        tmpring = Ring(kb, [sb(es, nc, f"kv_tmp{i}", [128, TT], F32) for i in range(4)])
        tbring = Ring(kb, [sb(es, nc, f"kv_tb{i}", [128, TT], BF16) for i in range(3)])
        obring = Ring(kb, [sb(es, nc, f"kv_ob{i}", [128, TT], BF16) for i in range(3)], dma=True)
        vbring = Ring(kb, [sb(es, nc, f"kv_vb{i}", [128, 512], BF16) for i in range(3)], dma=True)
        for hf in range(2):
            for q4 in range(4):
                norm_tile(cx, es, hin_v, V["kv_norm"], hf * 1024 + q4 * 256, 256, uT, uT_r, q4 * 256, xring, sqring, rstd, rstd_r)

            def evac(ci, ti, ps, ps_r, M, N):
                t0 = hf * 1024 + ti * TT
                part, g = fm_parts[ci]
                if part in (0, 1):
                    o, o_r, o_s = obring.next()
                    kb.op("act", lambda e: e.activation(out=o[:, :], in_=ps[:, :], func=AF.Copy), reads=[ps_r], writes=[o_r])
                    kb.dma("sp", craw_d[part, g, :, t0:t0 + TT], o[:, :], o_s, reads=[o_r])
                else:
                    gcol = V["g_k_slc"] if part == 2 else V["g_k_win"]
                    kb.op("act", lambda e: e.activation(out=raw[:, :], in_=ps[:, :], func=AF.Copy), reads=[ps_r], writes=[raw_r])
                    rms_rstd(cx, es, [(raw[:, :], raw_r, 128)], 128, TT, rstd, rstd_r, sqring)
                    tb, tb_r, _ = tbring.next()
                    kb.op("dve", lambda e: e.scalar_tensor_tensor(out=tb[:, :], in0=raw[:, :], scalar=cx.vecs[:, gcol:gcol + 1], in1=rstd[:, :],
                          op0=ALU.mult, op1=ALU.mult), reads=[raw_r, rstd_r], writes=[tb_r])
                    o, o_r, o_s = obring.next()
                    rope_apply(cx, tb[:, :], tb_r, 128, TT, t0, cx.perm128, cx.cosB, cx.sinB, o[:, :], o_r, tmpring)
                    kb.dma("sp", kT_d[0 if part == 2 else 1, g, :, t0:t0 + TT], o[:, :], o_s, reads=[o_r])

            fm_parts = [(p, g) for p in (0, 1, 2, 4) for g in range(4)]
            proj_fm(cx, kvw_v, [(p * 512 + g * 128, 128) for (p, g) in fm_parts], uT, uT_r, KC, [(0, TT), (TT, TT)], evac, wring)
            t0 = hf * 1024
            for pi_, part in enumerate((3, 5)):
                w, w_r, w_s = wvring.next()
                kb.dma("pool", w[:, :, :], kvw_v[:, :, part * 512:(part + 1) * 512], w_s, writes=[w_r])
                for tt in range(8):
                    ps, ps_r, _ = cx.psA.next()
                    for kc in range(KC):
                        kb.op("pe", lambda e: e.matmul(ps[:, :], lhsT=uT[:, kc, tt * 128:(tt + 1) * 128], rhs=w[:, kc, :], start=(kc == 0), stop=(kc == KC - 1)),
                              reads=[w_r, uT_r], writes=[ps_r], inc=(kc == KC - 1))
                    o, o_r, o_s = vbring.next()
                    kb.op("act", lambda e: e.activation(out=o[:, :], in_=ps[:, :], func=AF.Copy), reads=[ps_r], writes=[o_r])
                    for g in range(4):
                        kb.dma("sp", v_d[pi_, g, t0 + tt * 128:t0 + (tt + 1) * 128, :], o[:, g * 128:(g + 1) * 128], o_s, reads=[o_r])
        kb.barrier()
        xring.release(); wring.release(); wvring.release(); obring.release(); vbring.release()


def compress_stage(cx, W, scr):
    kb, nc = cx.kb, cx.nc
    V = cx.vcol
    craw_d = scr["craw"]
    kcmp_d = scr["kcmp"]
    vcmp_d = scr["vcmp"]
    with ExitStack() as es:
        w1 = sb(es, nc, "cp_w1", [128, 32, 256], BF16); w1_r = Res()
        w2 = sb(es, nc, "cp_w2", [128, 2, 128], BF16); w2_r = Res()
        tT = sb(es, nc, "cp_tT", [128, S], BF16); tT_r = Res()
        hid = sb(es, nc, "cp_hid", [128, 2, 128], BF16); hid_r = Res()
        bias = sb(es, nc, "cp_bias", [128, 2], F32); bias_r = Res()
        raw = sb(es, nc, "cp_raw", [128, 128], F32); raw_r = Res()
        rstd = sb(es, nc, "cp_rstd", [128, TT], F32); rstd_r = Res()
        sqring = Ring(kb, [sb(es, nc, f"cp_sq{i}", [128, TT], BF16) for i in range(2)])
        ob = sb(es, nc, "cp_ob", [128, 128], BF16); ob_r = Res()
        s1 = kb.get_dsem(); s2 = kb.get_dsem(); s3 = kb.get_dsem(); s4 = kb.get_dsem()
        for kv in range(2):
            w1_d = W["cmp_k_w1"] if kv == 0 else W["cmp_v_w1"]
            w2_d = W["cmp_k_w2"] if kv == 0 else W["cmp_v_w2"]
            posT = cx.posT_k if kv == 0 else cx.posT_v
            b1c = V["cmp_k_b1"] if kv == 0 else V["cmp_v_b1"]
            kb.dma("pool", w1[:, :, :], w1_d.rearrange("(l p) f -> p l f", p=128), s1, writes=[w1_r])
            kb.dma("pool", w2[:, :, :], w2_d.rearrange("(c p) f -> p c f", p=128), s2, writes=[w2_r])
            for hc in range(2):
                ps, ps_r, _ = cx.psA.next()
                for l in range(32):
                    kb.op("pe", lambda e: e.matmul(ps[:, 0:1], lhsT=w1[:, l, hc * 128:(hc + 1) * 128], rhs=posT[:, l:l + 1], start=(l == 0), stop=(l == 31)),
                          reads=[w1_r], writes=[ps_r], inc=(l == 31))
                kb.op("dve", lambda e: e.tensor_tensor(out=bias[:, hc:hc + 1], in0=ps[:, 0:1], in1=cx.vecs[:, b1c + hc:b1c + hc + 1], op=ALU.add),
                      reads=[ps_r], writes=[bias_r])
            for g in range(4):
                kb.dma("sp", tT[:, :], craw_d[kv, g, :, :], s3, writes=[tT_r])
                for hc in range(2):
                    ps, ps_r, _ = cx.psA.next()
                    for l in range(32):
                        kb.op("pe", lambda e: e.matmul(ps[:, 0:N_CMP], lhsT=w1[:, l, hc * 128:(hc + 1) * 128], rhs=tT[:, l:l + 16 * (N_CMP - 1) + 1:16],
                              start=(l == 0), stop=(l == 31)), reads=[w1_r, tT_r], writes=[ps_r], inc=(l == 31))
                    kb.op("act", lambda e: e.activation(out=hid[:, hc, 0:N_CMP], in_=ps[:, 0:N_CMP], func=AF.Silu, bias=bias[:, hc:hc + 1]),
                          reads=[ps_r, bias_r], writes=[hid_r])
                if kv == 0:
                    ps, ps_r, _ = cx.psA.next()
                    for hc in range(2):
                        kb.op("pe", lambda e: e.matmul(ps[:, 0:N_CMP], lhsT=w2[:, hc, :], rhs=hid[:, hc, 0:N_CMP], start=(hc == 0), stop=(hc == 1)),
                              reads=[w2_r, hid_r], writes=[ps_r])
                    kb.op("act", lambda e: e.activation(out=raw[:, 0:N_CMP], in_=ps[:, 0:N_CMP], func=AF.Copy), reads=[ps_r], writes=[raw_r])
                    rms_rstd(cx, es, [(raw[:, 0:N_CMP], raw_r, 128)], 128, N_CMP, rstd, rstd_r, sqring)
                    kb.op("dve", lambda e: e.scalar_tensor_tensor(out=ob[:, 0:N_CMP], in0=raw[:, 0:N_CMP], scalar=cx.vecs[:, V["g_k_cmp"]:V["g_k_cmp"] + 1],
                          in1=rstd[:, 0:N_CMP], op0=ALU.mult, op1=ALU.mult), reads=[raw_r, rstd_r], writes=[ob_r])
                    kb.dma("sp", kcmp_d[g, :, 0:N_CMP], ob[:, 0:N_CMP], s4, reads=[ob_r])
                else:
                    ps, ps_r, _ = cx.psA.next()
                    for hc in range(2):
                        kb.op("pe", lambda e: e.matmul(ps[0:N_CMP, 0:128], lhsT=hid[:, hc, 0:N_CMP], rhs=w2[:, hc, :], start=(hc == 0), stop=(hc == 1)),
                              reads=[w2_r, hid_r], writes=[ps_r])
                    kb.op("act", lambda e: e.activation(out=ob[0:N_CMP, :], in_=ps[0:N_CMP, 0:128], func=AF.Copy), reads=[ps_r], writes=[ob_r])
                    kb.dma("sp", vcmp_d[g, 0:N_CMP, :], ob[0:N_CMP, :], s4, reads=[ob_r])
        kb.barrier()
        for s in (s1, s2, s3, s4):
            kb.put_dsem(s)


def nsa_stage(cx, L, b, hin, hout, W, scr):
    kb, nc = cx.kb, cx.nc
    V = cx.vcol
    hin_v = hin.rearrange("(c p) n -> p c n", p=128)
    w_in_v = W["nsa_w_in"][b].rearrange("(c p) f -> p c f", p=128)
    qn_d, qr_d, gates_d, oT_d, ocmp_d = scr["qn"], scr["qr"], scr["gates"], scr["oT"], scr["ocmp"]
    oT_dv = oT_d.rearrange("(c p) n -> p c n", p=128)
    gq = cx.vecs[:, V["nsa_g_q"][b]:V["nsa_g_q"][b] + 1]
    scale = 128.0 ** -0.5
    with ExitStack() as es:
        load_consts(cx, es, ["cosB", "sinB"])
        uT = sb(es, nc, "n1_uT", [128, KC, 1024], BF16); uT_r = Res()
        xring = Ring(kb, [sb(es, nc, f"n1_x{i}", [128, 8, 256], F32) for i in range(4)], dma=True)
        sqring = Ring(kb, [sb(es, nc, f"n1_sq{i}", [128, TT], BF16) for i in range(4)])
        rstd = sb(es, nc, "n1_rstd", [128, TT], F32); rstd_r = Res()
        wring = Ring(kb, [sb(es, nc, f"n1_w{i}", [128, KC, 128], BF16) for i in range(3)], dma=True)
        rawring = Ring(kb, [sb(es, nc, f"n1_raw{i}", [128, TT], F32) for i in range(4)])
        rsring = Ring(kb, [sb(es, nc, f"n1_rs{i}", [128, TT], F32) for i in range(3)])
        tmpring = Ring(kb, [sb(es, nc, f"n1_tmp{i}", [128, TT], F32) for i in range(4)])
        qnring = Ring(kb, [sb(es, nc, f"n1_qn{i}", [128, TT], BF16) for i in range(4)], dma=True)
        qrring = Ring(kb, [sb(es, nc, f"n1_qr{i}", [128, TT], BF16) for i in range(3)], dma=True)
        gring = Ring(kb, [sb(es, nc, f"n1_g{i}", [128, TT], F32) for i in range(2)], dma=True)
        cols = [(c * 128, 128) for c in range(32)] + [(4096, 96)]
        for hf in range(2):
            for q4 in range(4):
                norm_tile(cx, es, hin_v, V["mix_norm"][L], hf * 1024 + q4 * 256, 256, uT, uT_r, q4 * 256, xring, sqring, rstd, rstd_r)

            def evac(ci, ti, ps, ps_r, M, N):
                t0 = hf * 1024 + ti * TT
                if ci == 32:
                    o, o_r, o_s = gring.next()
                    kb.op("act", lambda e: e.activation(out=o[0:96, :], in_=ps[0:96, :], func=AF.Sigmoid, bias=cx.vecs[0:96, V["nsa_b_gate"][b]:V["nsa_b_gate"][b] + 1]),
                          reads=[ps_r], writes=[o_r])
                    kb.dma("sp", gates_d[:, t0:t0 + TT], o[0:96, :], o_s, reads=[o_r])
                    return
                rw, rw_r, _ = rawring.next()
                kb.op("act", lambda e: e.activation(out=rw[:, :], in_=ps[:, :], func=AF.Copy), reads=[ps_r], writes=[rw_r])
                sq_, sq_r, _ = sqring.next()
                kb.op("act", lambda e: e.activation(out=sq_[:, :], in_=rw[:, :], func=AF.Square), reads=[rw_r], writes=[sq_r])

                def cont1():
                    ps2, ps2_r, _ = cx.psB.next()
                    kb.op("pe", lambda e: e.matmul(ps2[:, :], lhsT=cx.ones[:, :], rhs=sq_[:, :], start=True, stop=True), reads=[sq_r], writes=[ps2_r])
                    rs, rs_r, _ = rsring.next()
                    kb.op("act", lambda e: e.activation(out=rs[:, :], in_=ps2[:, :], func=AF.Ln, scale=1.0 / 128, bias=cx.eps_col[:, 0:1]), reads=[ps2_r], writes=[rs_r])
                    kb.op("act", lambda e: e.activation(out=rs[:, :], in_=rs[:, :], func=AF.Exp, scale=-0.5), reads=[rs_r], writes=[rs_r])
                    qn, qn_r, qn_s = qnring.next()
                    kb.op("dve", lambda e: e.scalar_tensor_tensor(out=qn[:, :], in0=rw[:, :], scalar=gq, in1=rs[:, :], op0=ALU.mult, op1=ALU.mult),
                          reads=[rw_r, rs_r], writes=[qn_r])
                    kb.dma("sp", qn_d[ci, :, t0:t0 + TT], qn[:, :], qn_s, reads=[qn_r])

                    def cont2():
                        qr, qr_r, qr_s = qrring.next()
                        rope_apply(cx, qn[:, :], qn_r, 128, TT, t0, cx.perm128, cx.cosB, cx.sinB, qr[:, :], qr_r, tmpring, psring=cx.psB)
                        kb.dma("sp", qr_d[ci, :, t0:t0 + TT], qr[:, :], qr_s, reads=[qr_r])
                        return None
                    return cont2
                return cont1

            proj_fm(cx, w_in_v, cols, uT, uT_r, KC, [(0, TT), (TT, TT)], evac, wring)
        kb.barrier()
        xring.release(); wring.release(); qnring.release(); qrring.release(); gring.release()
    with ExitStack() as es:
        load_consts(cx, es, ["maskC", "maskW", "cmask", "agg", "Eexp", "impA", "impB"])
        kcmp = sb(es, nc, "n2_kcmp", [128, 128], BF16)
        vcmp = sb(es, nc, "n2_vcmp", [128, 128], BF16)
        kslc = sb(es, nc, "n2_kslc", [128, S], BF16)
        kwin = sb(es, nc, "n2_kwin", [128, S], BF16)
        vslc = sb(es, nc, "n2_vslc", [128, 16, 128], BF16)
        vwin = sb(es, nc, "n2_vwin", [128, 16, 128], BF16)
        kv_r = Res()
        qnb = [sb(es, nc, f"n2_qn{i}", [128, S], BF16) for i in range(2)]; qnb_r = [Res(), Res()]; q_r = Res()
        qsem2 = [kb.get_dsem(), kb.get_dsem()]
        qrb = [sb(es, nc, f"n2_qr{i}", [128, S], BF16) for i in range(2)]; qrb_r = [Res(), Res()]
        qsem3 = [kb.get_dsem(), kb.get_dsem()]
        psumh = sb(es, nc, "n2_psum", [128, S], F32); psumh_rs = [Res() for _ in range(NTT)]
        e32 = Ring(kb, [sb(es, nc, f"n2_e{i}", [128, TT], F32) for i in range(4)])
        pring = Ring(kb, [sb(es, nc, f"n2_p{i}", [128, TT], BF16) for i in range(3)])
        rden = sb(es, nc, "n2_rden", [128, TT], F32); rden_r = Res()
        gbc = Ring(kb, [sb(es, nc, f"n2_gbc{i}", [128, S], F32) for i in range(4)], dma=True)
        ocring = Ring(kb, [sb(es, nc, f"n2_oc{i}", [128, TT], F32) for i in range(3)], dma=True)
        oacc = sb(es, nc, "n2_oacc", [128, TT], F32); oacc_r = Res()
        otmp = sb(es, nc, "n2_otmp", [128, TT], F32); otmp_r = Res()
        oT = sb(es, nc, "n2_oT", [128, S], BF16); oT_r = Res()
        imp = sb(es, nc, "n2_imp", [128, 32], F32); imp_r = Res()
        imp2 = sb(es, nc, "n2_imp2", [128, 32], F32); imp2_r = Res()
        m8 = sb(es, nc, "n2_m8", [128, 16], F32); m8_r = Res()
        sel = sb(es, nc, "n2_sel", [128, 32], F32); sel_r = Res()
        selT = sb(es, nc, "n2_selT", [128, S], BF16); selT_r = Res()
        kb.op("dve", lambda e: e.memset(selT[:, :], 0.0), writes=[selT_r])
        masks = sb(es, nc, "n2_masks", [128, 40, TT], BF16); mask_rs = [Res() for _ in range(40)]
        ksem = kb.get_dsem(); qsem = kb.get_dsem(); osem = kb.get_dsem()
        for g in range(4):
            kb.dma("sp", kcmp[:, :], scr["kcmp"][g], ksem, writes=[kv_r])
            kb.dma("sp", vcmp[:, :], scr["vcmp"][g], ksem, writes=[kv_r])
            kb.dma("sp", kslc[:, :], scr["nkT"][0, g], ksem, writes=[kv_r])
            kb.dma("sp", kwin[:, :], scr["nkT"][1, g], ksem, writes=[kv_r])
            kb.dma("sp", vslc[:, :, :], scr["nv"][0, g].rearrange("(t p) d -> p t d", p=128), ksem, writes=[kv_r])
            kb.dma("sp", vwin[:, :, :], scr["nv"][1, g].rearrange("(t p) d -> p t d", p=128), ksem, writes=[kv_r])
            items = [(j, t) for j in range(8) for t in range(NTT)]
            hstate = {}

            def head_load(j):
                hh = g * 8 + j
                qb = qnb[j % 2]
                kb.dma("sp", qb[:, :], qn_d[hh], qsem2[j % 2], writes=[qnb_r[j % 2]])
                gb, gb_r, gb_s = gbc.next()
                kb.dma("sp", gb[:, :], bass.AP(gates_d.tensor, (hh * 3 + 0) * S, [[0, 128], [1, S]]), gb_s, writes=[gb_r])
                hstate[j] = (qb, qnb_r[j % 2], gb, gb_r)

            def stage_x(j, t):
                if t == 0:
                    head_load(j)
                qb, qb_r, gb, gb_r = hstate[j]
                tsl = slice(t * TT, (t + 1) * TT)
                ps, ps_r, _ = cx.psA.next()
                kb.op("pe", lambda e: e.matmul(ps[0:N_CMP, :], lhsT=kcmp[:, 0:N_CMP], rhs=qb[:, tsl], start=True, stop=True), reads=[kv_r, qb_r], writes=[ps_r])
                ef, ef_r, _ = e32.next()
                kb.op("act", lambda e: e.activation(out=ef[0:N_CMP, :], in_=ps[0:N_CMP, :], func=AF.Exp, scale=scale), reads=[ps_r], writes=[ef_r])
                p, p_r, _ = pring.next()
                kb.op("dve", lambda e: e.tensor_tensor(out=p[0:N_CMP, :], in0=ef[0:N_CMP, :], in1=cx.cmask[0:N_CMP, tsl], op=ALU.mult), reads=[ef_r], writes=[p_r])
                kb.op("pool", lambda e: e.tensor_tensor(out=ef[0:N_CMP, :], in0=ef[0:N_CMP, :], in1=cx.cmask[0:N_CMP, tsl], op=ALU.mult), reads=[ef_r], writes=[ef_r])
                return (ef, ef_r, p, p_r)

            def stage_y(j, t, st):
                hh = g * 8 + j
                qb, qb_r, gb, gb_r = hstate[j]
                ef, ef_r, p, p_r = st
                tsl = slice(t * TT, (t + 1) * TT)
                d_ps, d_r, _ = cx.psB.next()
                kb.op("pe", lambda e: e.matmul(d_ps[:, :], lhsT=cx.ones[0:N_CMP, :], rhs=p[0:N_CMP, :], start=True, stop=True), reads=[p_r], writes=[d_r])
                o_ps, o_r, _ = cx.psB.next()
                kb.op("pe", lambda e: e.matmul(o_ps[:, :], lhsT=vcmp[0:N_CMP, :], rhs=p[0:N_CMP, :], start=True, stop=True), reads=[p_r, kv_r], writes=[o_r])
                kb.op("dve", lambda e: e.tensor_scalar(out=rden[:, :], in0=d_ps[:, :], scalar1=1e-18, scalar2=None, op0=ALU.max), reads=[d_r], writes=[rden_r])
                act_recip(cx, rden[:, :], rden[:, :], rden_r, rden_r)
                if j == 0:
                    kb.op("dve", lambda e: e.tensor_tensor(out=psumh[0:N_CMP, tsl], in0=ef[0:N_CMP, :], in1=rden[0:N_CMP, :], op=ALU.mult),
                          reads=[ef_r, rden_r], writes=[psumh_rs[t]])
                else:
                    kb.op("dve", lambda e: e.tensor_tensor(out=ef[0:N_CMP, :], in0=ef[0:N_CMP, :], in1=rden[0:N_CMP, :], op=ALU.mult),
                          reads=[ef_r, rden_r], writes=[ef_r])
                    kb.op("pool", lambda e: e.tensor_tensor(out=psumh[0:N_CMP, tsl], in0=psumh[0:N_CMP, tsl], in1=ef[0:N_CMP, :], op=ALU.add),
                          reads=[ef_r, psumh_rs[t]], writes=[psumh_rs[t]])
                oc, oc_r, oc_s = ocring.next()
                kb.op("dve", lambda e: e.tensor_tensor(out=oc[:, :], in0=o_ps[:, :], in1=rden[:, :], op=ALU.mult), reads=[o_r, rden_r], writes=[oc_r])
                kb.op("dve", lambda e: e.tensor_tensor(out=oc[:, :], in0=oc[:, :], in1=gb[:, tsl], op=ALU.mult), reads=[oc_r, gb_r], writes=[oc_r])
                kb.dma("sp", ocmp_d[hh, :, tsl], oc[:, :], oc_s, reads=[oc_r])

            AH = 2
            xq = [stage_x(*items[i]) for i in range(AH)]
            for i in range(len(items)):
                if i + AH < len(items):
                    xq.append(stage_x(*items[i + AH]))
                stage_y(*items[i], xq[i])
            for st in range(16):
                ssl = slice(st * 128, (st + 1) * 128)
                ps, ps_r, _ = cx.psA.next()
                kb.op("pe", lambda e: e.matmul(ps[:, 0:32], lhsT=psumh[0:N_CMP, ssl], rhs=cx.agg[0:N_CMP, :], start=True, stop=True), reads=[psumh_rs[st // 4]], writes=[ps_r])
                kb.op("dve", lambda e: e.tensor_tensor(out=imp[:, :], in0=ps[:, 0:32], in1=cx.impA[:, st, :], op=ALU.mult), reads=[ps_r], writes=[imp_r])
                kb.op("dve", lambda e: e.tensor_tensor(out=imp[:, :], in0=imp[:, :], in1=cx.impB[:, st, :], op=ALU.add), reads=[imp_r], writes=[imp_r])
                kb.op("dve", lambda e: e.max(out=m8[:, 0:8], in_=imp[:, :]), reads=[imp_r], writes=[m8_r])
                kb.op("dve", lambda e: e.match_replace(out=imp2[:, :], in_to_replace=m8[:, 0:8], in_values=imp[:, :], imm_value=-1e30), reads=[imp_r, m8_r], writes=[imp2_r])
                kb.op("dve", lambda e: e.max(out=m8[:, 8:16], in_=imp2[:, :]), reads=[imp2_r], writes=[m8_r])
                kb.op("dve", lambda e: e.tensor_scalar(out=sel[:, :], in0=imp[:, :], scalar1=m8[:, 15:16], scalar2=None, op0=ALU.is_ge), reads=[imp_r, m8_r], writes=[sel_r])
                pt, pt_r, _ = cx.psA.next()
                kb.op("pe", lambda e: e.transpose(out=pt[0:32, 0:128], in_=sel[:, :], identity=cx.ident[:, :]), reads=[sel_r], writes=[pt_r])
                kb.op("act", lambda e: e.activation(out=selT[0:32, ssl], in_=pt[0:32, 0:128], func=AF.Copy), reads=[pt_r], writes=[selT_r])
            midx = {}
            i = 0
            for t in range(NTT):
                tsl = slice(t * TT, (t + 1) * TT)
                for kt in range(4 * t + 4):
                    ps, ps_r, _ = cx.psA.next()
                    kb.op("pe", lambda e: e.matmul(ps[:, :], lhsT=cx.Eexp[:, kt * 128:(kt + 1) * 128], rhs=selT[:, tsl], start=True, stop=True), reads=[selT_r], writes=[ps_r])
                    if kt >= 4 * t:
                        kb.op("dve", lambda e: e.tensor_tensor(out=masks[:, i, :], in0=ps[:, :], in1=cx.maskC[:, kt - 4 * t, :], op=ALU.mult), reads=[ps_r], writes=[mask_rs[i]])
                    else:
                        kb.op("act", lambda e: e.activation(out=masks[:, i, :], in_=ps[:, :], func=AF.Copy), reads=[ps_r], writes=[mask_rs[i]])
                    midx[(t, kt)] = i
                    i += 1
            for j in range(8):
                hh = g * 8 + j
                qr = qrb[j % 2]; q_r = qrb_r[j % 2]
                kb.dma("sp", qr[:, :], qr_d[hh], qsem3[j % 2], writes=[q_r])
                g1, g1_r, g1_s = gbc.next()
                kb.dma("sp", g1[:, :], bass.AP(gates_d.tensor, (hh * 3 + 1) * S, [[0, 128], [1, S]]), g1_s, writes=[g1_r])
                g2, g2_r, g2_s = gbc.next()
                kb.dma("sp", g2[:, :], bass.AP(gates_d.tensor, (hh * 3 + 2) * S, [[0, 128], [1, S]]), g2_s, writes=[g2_r])
                for t in range(NTT):
                    tsl = slice(t * TT, (t + 1) * TT)
                    oc, oc_r, oc_s = ocring.next()
                    kb.dma("sp", oc[:, :], ocmp_d[hh, :, tsl], oc_s, writes=[oc_r])
                    o_ps, o_r, _ = cx.psB.next()
                    d_ps, d_r, _ = cx.psB.next()
                    nk = 4 * t + 4
                    tiles = []
                    for kt in range(nk):
                        mi = midx[(t, kt)]
                        tiles.append(dict(q=[(qr[:, tsl], q_r)], k=[(kslc[:, kt * 128:(kt + 1) * 128], kv_r)], v=(vslc[:, kt, :], kv_r),
                                          ones=cx.ones[:, :], mask=(masks[:, mi, :], mask_rs[mi])))
                    attn_seq(cx, tiles, scale, o_ps, o_r, d_ps, d_r, pring)
                    act_recip(cx, rden[:, :], d_ps[:, :], d_r, rden_r)
                    kb.op("dve", lambda e: e.tensor_tensor(out=otmp[:, :], in0=o_ps[:, :], in1=rden[:, :], op=ALU.mult), reads=[o_r, rden_r], writes=[otmp_r])
                    kb.op("pool", lambda e: e.tensor_tensor(out=otmp[:, :], in0=otmp[:, :], in1=g1[:, tsl], op=ALU.mult), reads=[otmp_r, g1_r], writes=[otmp_r])
                    kb.op("pool", lambda e: e.tensor_tensor(out=oacc[:, :], in0=otmp[:, :], in1=oc[:, :], op=ALU.add), reads=[otmp_r, oc_r], writes=[oacc_r])
                    o_ps, o_r, _ = cx.psB.next()
                    d_ps, d_r, _ = cx.psB.next()
                    kts = [kt for kt in range(4 * t - 4, 4 * t + 4) if kt >= 0]
                    tiles = []
                    for kt in kts:
                        tiles.append(dict(q=[(qr[:, tsl], q_r)], k=[(kwin[:, kt * 128:(kt + 1) * 128], kv_r)], v=(vwin[:, kt, :], kv_r),
                                          ones=cx.ones[:, :], mask=(cx.maskW[:, kt - (4 * t - 4), :], None)))
                    attn_seq(cx, tiles, scale, o_ps, o_r, d_ps, d_r, pring)
                    act_recip(cx, rden[:, :], d_ps[:, :], d_r, rden_r)
                    kb.op("dve", lambda e: e.tensor_tensor(out=otmp[:, :], in0=o_ps[:, :], in1=rden[:, :], op=ALU.mult), reads=[o_r, rden_r], writes=[otmp_r])
                    kb.op("pool", lambda e: e.tensor_tensor(out=otmp[:, :], in0=otmp[:, :], in1=g2[:, tsl], op=ALU.mult), reads=[otmp_r, g2_r], writes=[otmp_r])
                    kb.op("pool", lambda e: e.tensor_tensor(out=oT[:, tsl], in0=otmp[:, :], in1=oacc[:, :], op=ALU.add), reads=[otmp_r, oacc_r], writes=[oT_r])
                kb.dma("sp", oT_dv[:, hh, :], oT[:, :], osem, reads=[oT_r])
        kb.barrier()
        gbc.release(); ocring.release()
        for s in (ksem, qsem, osem, qsem2[0], qsem2[1], qsem3[0], qsem3[1]):
            kb.put_dsem(s)
    out_proj_stage(cx, oT_d, W["nsa_w_o"][b], hin, hout)


VEC_SPECS = None


def cols_of(v):
    v = np.asarray(v, np.float32)
    n = v.shape[0]
    if n <= 128:
        o = np.zeros((128, 1), np.float32)
        o[:n, 0] = v
        return o
    assert n % 128 == 0
    return np.ascontiguousarray(v.reshape(n // 128, 128).T)


def build_vecs(inp):
    cols = []
    vcol = {}
    pos = [0]

    def add(name, v, idx=None):
        c = cols_of(v)
        if idx is None:
            vcol[name] = pos[0]
        else:
            vcol.setdefault(name, {})[idx] = pos[0]
        cols.append(c)
        pos[0] += c.shape[1]

    for l in range(DEPTH):
        add("ffn1_norm", inp["ffn1_norm"][l], l)
        add("mix_norm", inp["mix_norm"][l], l)
        add("ffn2_norm", inp["ffn2_norm"][l], l)
    for a in range(2):
        add("mla_g_cq", inp["mla_g_cq"][a], a)
        add("mla_g_ckv", inp["mla_g_ckv"][a], a)
        add("mla_g_q_n", inp["mla_g_q"][a][:128], a)
        add("mla_g_q_r", inp["mla_g_q"][a][128:], a)
        add("mla_g_k_n", inp["mla_g_k"][a][:128], a)
        add("mla_g_k_r", inp["mla_g_k"][a][128:], a)
    add("kv_norm", inp["kv_norm"])
    add("cmp_k_b1", inp["cmp_k_b1"])
    add("cmp_v_b1", inp["cmp_v_b1"])
    add("g_k_cmp", inp["g_k_cmp"])
    add("g_k_slc", inp["g_k_slc"])
    add("g_k_win", inp["g_k_win"])
    for b in range(2):
        add("nsa_b_gate", inp["nsa_b_gate"][b], b)
        add("nsa_g_q", inp["nsa_g_q"][b], b)
    return np.ascontiguousarray(np.concatenate(cols, axis=1)), vcol


def build_consts():
    import ml_dtypes
    bf = ml_dtypes.bfloat16
    c = {}
    c["ones"] = np.ones((128, 128), bf)
    c["ident"] = np.eye(128, dtype=np.float32)
    p128 = np.zeros((128, 128), np.float32)
    for i in range(64):
        p128[i, i + 64] = 1; p128[i + 64, i] = 1
    p64 = np.zeros((128, 128), np.float32)
    for i in range(32):
        p64[i, i + 32] = 1; p64[i + 32, i] = 1
    c["perm128"] = p128.astype(bf); c["perm64"] = p64.astype(bf)
    misc = np.zeros((128, 8), np.float32)
    misc[:, 0] = EPS; misc[:, 1] = np.pi
    invA = (10000.0 ** (-(np.arange(0, 64, 2, dtype=np.float32)) / 64)).astype(np.float32)
    invB = (10000.0 ** (-(np.arange(0, 128, 2, dtype=np.float32)) / 128)).astype(np.float32)
    misc[:64, 2] = np.concatenate([invA, invA]); misc[:32, 3] = -1; misc[32:64, 3] = 1
    misc[:, 4] = np.concatenate([invB, invB]); misc[:64, 5] = -1; misc[64:, 5] = 1
    c["misc"] = misc
    k = np.arange(128)[:, None]; q = np.arange(512)[None, :]
    c["maskC"] = np.stack([((j * 128 + k) <= q) for j in range(4)], axis=1).astype(bf)
    mw = []
    for i in range(8):
        dlt = (i - 4) * 128
        diff = q - k - dlt
        mw.append((diff >= 0) & (diff < 512))
    c["maskW"] = np.stack(mw, axis=1).astype(bf)
    n = np.arange(128)[:, None]; s = np.arange(S)[None, :]
    c["cmask"] = (((16 * n + 31) <= s) & (n < N_CMP)).astype(np.float32)
    cs = np.arange(N_CMP)[:, None] * 16; ss = np.arange(32)[None, :] * 64
    ov = np.clip(np.minimum(cs + 32, ss + 64) - np.maximum(cs, ss), 0, None)
    agg = np.zeros((128, 32), np.float32); agg[:N_CMP] = ov / 16
    c["agg"] = agg
    E = np.zeros((128, S), np.float32)
    E[np.arange(S) // 64, np.arange(S)] = 1
    c["Eexp"] = E.astype(bf)
    spos = np.arange(S)[:, None]; jb = np.arange(32)[None, :]
    cur = spos // 64
    valid = (jb * 64) <= spos
    forced = (jb == 0) | (jb == cur) | (jb == cur - 1)
    A = (valid & ~forced).astype(np.float32)
    B = np.where(forced, 1e6, np.where(valid, 0.0, -1.0)).astype(np.float32)
    c["impA"] = np.ascontiguousarray(A.reshape(16, 128, 32).transpose(1, 0, 2))
    c["impB"] = np.ascontiguousarray(B.reshape(16, 128, 32).transpose(1, 0, 2))
    return c


CONST_DT = {"ones": BF16, "ident": F32, "perm128": BF16, "perm64": BF16, "misc": F32, "maskC": BF16, "maskW": BF16,
            "cmask": F32, "agg": F32, "Eexp": BF16, "impA": F32, "impB": F32}

WEIGHT_NAMES = ["ffn1_w_gate", "ffn1_w_up", "ffn1_w_down", "ffn2_w_gate", "ffn2_w_up", "ffn2_w_down",
                "mla_w_in", "mla_w_uq", "mla_w_ukv", "mla_w_o", "kv_w", "cmp_k_w1", "cmp_k_w2", "cmp_v_w1", "cmp_v_w2",
                "nsa_w_in", "nsa_w_o"]


def build_program(shapes, vcol, nvec, consts, stages=None, plan=None):
    nc = bass.Bass("TRN2", target_bir_lowering=False)
    xT = nc.dram_tensor("xT", [D, S], F32, kind="ExternalInput").ap()
    pos = nc.dram_tensor("pos", [1, S], I32, kind="ExternalInput").ap()
    vecs_d = nc.dram_tensor("vecs", [128, nvec], F32, kind="ExternalInput").ap()
    posk_d = nc.dram_tensor("posTk", [128, 32], F32, kind="ExternalInput").ap()
    posv_d = nc.dram_tensor("posTv", [128, 32], F32, kind="ExternalInput").ap()
    cd = {k: nc.dram_tensor("c_" + k, list(v.shape), CONST_DT[k], kind="ExternalInput").ap() for k, v in consts.items()}
    W = {k: nc.dram_tensor(k, list(shapes[k]), F32, kind="ExternalInput").ap() for k in WEIGHT_NAMES}
    outT = nc.dram_tensor("outT", [D, S], F32, kind="ExternalOutput").ap()
    hA = nc.dram_tensor("hA", [D, S], F32).ap()
    hB = nc.dram_tensor("hB", [D, S], F32).ap()
    scr = {
        "cqn": nc.dram_tensor("s_cqn", [MLA_QL, S], BF16).ap(),
        "ckvn": nc.dram_tensor("s_ckvn", [MLA_KVL, S], BF16).ap(),
        "kr": nc.dram_tensor("s_kr", [64, S], F32).ap(),
        "oT": nc.dram_tensor("s_oT", [D, S], BF16).ap(),
        "craw": nc.dram_tensor("s_craw", [2, 4, 128, S], BF16).ap(),
        "nkT": nc.dram_tensor("s_nkT", [2, 4, 128, S], BF16).ap(),
        "nv": nc.dram_tensor("s_nv", [2, 4, S, 128], BF16).ap(),
        "kcmp": nc.dram_tensor("s_kcmp", [4, 128, 128], BF16).ap(),
        "vcmp": nc.dram_tensor("s_vcmp", [4, 128, 128], BF16).ap(),
        "qn": nc.dram_tensor("s_qn", [H, 128, S], BF16).ap(),
        "qr": nc.dram_tensor("s_qr", [H, 128, S], BF16).ap(),
        "gates": nc.dram_tensor("s_gates", [96, S], F32).ap(),
        "ocmp": nc.dram_tensor("s_ocmp", [H, 128, S], F32).ap(),
    }
    kb = KB(nc)
    cx = Ctx()
    cx.kb = kb; cx.nc = nc; cx.vcol = vcol
    with ExitStack() as es:
        psum = [es.enter_context(nc.psum_tensor(f"ps{i}", [128, 512], F32)) for i in range(8)]
        cx.psA = Ring(kb, psum[0:4])
        cx.psB = Ring(kb, psum[4:8])
        csem = kb.get_dsem()
        cx.vecs = sb(es, nc, "vecs_sb", [128, nvec], F32)
        kb.dma("sp", cx.vecs[:, :], vecs_d, csem)
        cx.cdram = {k: (cd[k], list(v.shape), CONST_DT[k]) for k, v in consts.items()}
        for nm in ("cosA", "sinA", "cosB", "sinB"):
            cx.cdram[nm] = (nc.dram_tensor("s_" + nm, [128, S], F32).ap(), [128, S], F32)
        ct = {}
        for k in ("ones", "ident", "perm128", "perm64", "misc"):
            v = consts[k]
            ct[k] = sb(es, nc, "k_" + k, list(v.shape), CONST_DT[k])
            kb.dma("sp", ct[k][:, :], cd[k], csem)
        cx.ones = ct["ones"]; cx.ident = ct["ident"]; cx.perm128 = ct["perm128"]; cx.perm64 = ct["perm64"]
        misc = ct["misc"]
        cx.eps_col = misc[:, 0:1]; cx.pi_col = misc[:, 1:2]
        pk32 = sb(es, nc, "posk32", [128, 32], F32); pv32 = sb(es, nc, "posv32", [128, 32], F32)
        kb.dma("sp", pk32[:, :], posk_d, csem); kb.dma("sp", pv32[:, :], posv_d, csem)
        cx.posT_k = sb(es, nc, "posk", [128, 32], BF16); cx.posT_v = sb(es, nc, "posv", [128, 32], BF16)
        kb.barrier()
        kb.op("dve", lambda e: e.tensor_copy(out=cx.posT_k[:, :], in_=pk32[:, :]))
        kb.op("dve", lambda e: e.tensor_copy(out=cx.posT_v[:, :], in_=pv32[:, :]))
        kb.barrier()
        build_rope_tables(cx, pos, misc[0:64, 2:3], misc[0:64, 3:4], 64, cx.cdram["cosA"][0], cx.cdram["sinA"][0])
        build_rope_tables(cx, pos, misc[:, 4:5], misc[:, 5:6], 128, cx.cdram["cosB"][0], cx.cdram["sinB"][0])
        user_plan = plan
        plan = []
        for L in range(DEPTH):
            plan.append(("ffn1", L))
            plan.append(("mix", L))
            plan.append(("ffn2", L))
            if L == 1:
                plan.append(("kv", L))
        if stages is not None:
            plan = plan[:stages]
        if user_plan is not None:
            plan = list(user_plan)
        cur = xT
        nxt = [hA, hB]
        ni = 0
        for si, (kind, L) in enumerate(plan):
            last = si == len(plan) - 1
            if kind == "kv":
                shared_kv_stage(cx, cur, W, scr)
                compress_stage(cx, W, scr)
                if last:
                    pass
                continue
            dst = outT if (last or (kind == "ffn2" and si + 1 < len(plan) and plan[si + 1][0] == "kv" and si + 2 == len(plan))) else nxt[ni]
            if kind == "ffn1":
                ffn_stage(cx, cur, dst, vcol["ffn1_norm"][L], W["ffn1_w_gate"][L], W["ffn1_w_up"][L], W["ffn1_w_down"][L])
            elif kind == "ffn2":
                ffn_stage(cx, cur, dst, vcol["ffn2_norm"][L], W["ffn2_w_gate"][L], W["ffn2_w_up"][L], W["ffn2_w_down"][L])
            elif L < 2:
                mla_stage(cx, L, L, cur, dst, W, scr)
            else:
                nsa_stage(cx, L, L - 2, cur, dst, W, scr)
            cur = dst
            if dst is not outT:
                ni ^= 1
        kb.barrier()
    cx.n_ins = kb.n_ins
    return nc, scr


_CACHE = {}


def run_model(inputs, cores, stages=None, trace=False, plan=None):
    inp = {k: np.asarray(v) for k, v in inputs.items()}
    vecs, vcol = build_vecs(inp)
    consts = build_consts()
    shapes = {k: inp[k].shape for k in WEIGHT_NAMES}
    key = (stages, tuple(plan) if plan else None)
    if key not in _CACHE:
        _CACHE[key] = build_program(shapes, vcol, vecs.shape[1], consts, stages, plan)
    nc, _ = _CACHE[key]
    in_maps = []
    for b in cores:
        m = {"xT": np.ascontiguousarray(inp["x"][b].T), "pos": np.ascontiguousarray(inp["positions"][b][None, :].astype(np.int32)),
             "vecs": vecs, "posTk": np.ascontiguousarray(inp["cmp_pos_k"].T.astype(np.float32)),
             "posTv": np.ascontiguousarray(inp["cmp_pos_v"].T.astype(np.float32))}
        for k, v in consts.items():
            m["c_" + k] = v
        for k in WEIGHT_NAMES:
            m[k] = np.ascontiguousarray(inp[k], dtype=np.float32)
        in_maps.append(m)
    res = run_bass_kernel_spmd(nc, in_maps, core_ids=list(range(len(cores))), trace=trace)
    outs = [np.ascontiguousarray(r["outT"].T) for r in res.results]
    return outs, res


def kernel(**inputs):
    outs, _ = run_model(inputs, list(range(NB)))
    return np.stack(outs, axis=0).astype(np.float32)
```
